# Optimizing a Trainium2 kernel written in Bass

```python
import math
import jax
import jax.numpy as jnp
from jax import lax
import numpy as np


D_MODEL = 1024
BATCH = 4
SEQ = 8192
DEPTH = 2

GLA_HEADS = 4
GLA_DK = 64
GLA_DV = 128
GLA_GATE_RANK = 16
GLA_TAU = 16.0
GLA_CHUNK = 64
MLA_HEADS = 8
MLA_Q_RANK = 256
MLA_KV_RANK = 128
MLA_NOPE = 64
MLA_ROPE = 32
MLA_V = 64
ROPE_THETA = 10000.0
DSA_HEADS = 8
DSA_HEAD_DIM = 64
IDX_HEADS = 8
IDX_DIM = 64
TOPK_MAX = 256
S5_GROUPS = 32
S5_GROUP_CH = 16
S5_STATE = 64
S5_DT_MIN = 0.001
S5_DT_MAX = 0.1
D_FF = 4 * D_MODEL
Q_BLOCK = 128
LN_EPS = 1e-5
DEEPNORM_ALPHA = (2 * DEPTH) ** 0.25
DEEPNORM_BETA = (8 * DEPTH) ** -0.25

L0_SPLITS = (GLA_HEADS * GLA_DK, GLA_HEADS * GLA_DK, GLA_HEADS * GLA_DV, GLA_GATE_RANK, GLA_HEADS * GLA_DV, MLA_Q_RANK, MLA_KV_RANK, MLA_ROPE)
L1_SPLITS = (DSA_HEADS * DSA_HEAD_DIM, DSA_HEADS * DSA_HEAD_DIM, DSA_HEADS * DSA_HEAD_DIM, IDX_HEADS * IDX_DIM, IDX_DIM, IDX_HEADS, S5_GROUPS * S5_GROUP_CH)
L0_IN = sum(L0_SPLITS)
L1_IN = sum(L1_SPLITS)
MIX0 = GLA_HEADS * GLA_DV + MLA_HEADS * MLA_V
MIX1 = DSA_HEADS * DSA_HEAD_DIM + S5_GROUPS * S5_GROUP_CH

kernel_name = 'hybrid_gla_mla_dsa_s5_block'


def _split(h, sizes):
    out = []
    off = 0
    for s in sizes:
        out.append(h[..., off:off + s])
        off += s
    return out


def _layernorm(x, g, b):
    xf = x.astype(jnp.float32)
    mu = jnp.mean(xf, -1, keepdims=True)
    var = jnp.mean(jnp.square(xf - mu), -1, keepdims=True)
    return ((xf - mu) * lax.rsqrt(var + LN_EPS) * g + b).astype(x.dtype)


def _rmsnorm(x, g):
    xf = x.astype(jnp.float32)
    return (xf * lax.rsqrt(jnp.mean(jnp.square(xf), -1, keepdims=True) + LN_EPS) * g).astype(x.dtype)


def _rope(t, pos):
    half = t.shape[-1] // 2
    inv_freq = ROPE_THETA ** (-jnp.arange(half, dtype=jnp.float32) / half)
    ang = pos.astype(jnp.float32)[:, :, None, None] * inv_freq
    cos, sin = jnp.cos(ang), jnp.sin(ang)
    tf = t.astype(jnp.float32)
    t1, t2 = tf[..., :half], tf[..., half:]
    return jnp.concatenate([t1 * cos - t2 * sin, t2 * cos + t1 * sin], -1).astype(t.dtype)


def _gla(q, k, v, log_a):
    b_, l_, h_, dk = q.shape
    dv = v.shape[-1]
    n_chunks = l_ // GLA_CHUNK

    def to_chunks(t):
        return t.astype(jnp.float32).reshape(b_, n_chunks, GLA_CHUNK, h_, t.shape[-1]).transpose(0, 3, 1, 2, 4)

    qc, kc, vc, gc = (to_chunks(t) for t in (q, k, v, log_a))
    cum = jnp.cumsum(gc, axis=3)
    cum_last = cum[:, :, :, -1:, :]
    q_dec = qc * jnp.exp(cum)
    k_inv = kc * jnp.exp(-cum)
    causal = jnp.tril(jnp.ones((GLA_CHUNK, GLA_CHUNK), dtype=bool))
    att = jnp.where(causal, jnp.einsum('bhnid,bhnjd->bhnij', q_dec, k_inv), 0.0)
    o_intra = jnp.einsum('bhnij,bhnje->bhnie', att, vc)
    k_end = kc * jnp.exp(cum_last - cum)
    chunk_kv = jnp.einsum('bhncd,bhnce->nbhde', k_end, vc)
    chunk_decay = jnp.exp(cum_last[:, :, :, 0, :]).transpose(2, 0, 1, 3)

    def step(state, inp):
        dec, kv = inp
        return dec[..., None] * state + kv, state

    s0 = jnp.zeros((b_, h_, dk, dv), jnp.float32)
    _, s_prev = lax.scan(step, s0, (chunk_decay, chunk_kv))
    o_inter = jnp.einsum('bhncd,nbhde->bhnce', q_dec, s_prev)
    return (o_intra + o_inter).transpose(0, 2, 3, 1, 4).reshape(b_, l_, h_, dv)


def _causal_block_attention(q, k, v, scale):
    b_, l_, h_, dq = q.shape
    dv = v.shape[-1]
    nb = l_ // Q_BLOCK
    qb = q.reshape(b_, nb, Q_BLOCK, h_, dq).transpose(1, 0, 2, 3, 4)
    key_pos = jnp.arange(l_)

    def one_block(args):
        qi, bi = args
        s = jnp.einsum('bqhd,bshd->bhqs', qi, k).astype(jnp.float32) * scale
        qpos = bi * Q_BLOCK + jnp.arange(Q_BLOCK)
        s = jnp.where((key_pos[None, :] <= qpos[:, None])[None, None], s, -jnp.inf)
        p = jax.nn.softmax(s, axis=-1)
        return jnp.einsum('bhqs,bshe->bqhe', p.astype(v.dtype), v)

    out = lax.map(one_block, (qb, jnp.arange(nb)))
    return out.transpose(1, 0, 2, 3, 4).reshape(b_, l_, h_, dv)


def _dsa(q, k, v, q_idx, k_idx, w_idx):
    b_, l_, h_, dh = q.shape
    topk = min(TOPK_MAX, l_ // 4)
    nb = l_ // Q_BLOCK

    def blocks(t):
        return t.reshape((b_, nb, Q_BLOCK) + t.shape[2:]).swapaxes(0, 1)

    key_pos = jnp.arange(l_)
    gather = jax.vmap(lambda kb, ib: kb[ib])

    def one_block(args):
        qi, qii, wi, bi = args
        qpos = bi * Q_BLOCK + jnp.arange(Q_BLOCK)
        logits = jnp.einsum('bqhd,bsd->bqhs', qii, k_idx).astype(jnp.float32) * (IDX_DIM ** -0.5)
        score = jnp.einsum('bqhs,bqh->bqs', jax.nn.relu(logits), wi.astype(jnp.float32))
        score = jnp.where((key_pos[None, :] <= qpos[:, None])[None], score, -jnp.inf)
        _, sel = lax.top_k(score, topk)
        sel_ok = sel <= qpos[None, :, None]
        k_sel = gather(k, sel)
        v_sel = gather(v, sel)
        s = jnp.einsum('bqhd,bqkhd->bhqk', qi, k_sel).astype(jnp.float32) * (dh ** -0.5)
        s = jnp.where(sel_ok[:, None], s, -jnp.inf)
        p = jax.nn.softmax(s, axis=-1)
        return jnp.einsum('bhqk,bqkhd->bqhd', p.astype(v.dtype), v_sel)

    out = lax.map(one_block, (blocks(q), blocks(q_idx), blocks(w_idx), jnp.arange(nb)))
    return out.swapaxes(0, 1).reshape(b_, l_, h_, dh)


def _s5(u, a_re, a_im, b_re, b_im, c_re, c_im, d_skip, log_step):
    b_, l_, _ = u.shape
    uf = u.astype(jnp.float32).reshape(b_, l_, S5_GROUPS, S5_GROUP_CH)
    lam_re = jnp.minimum(a_re.astype(jnp.float32), -1e-4)
    lam_im = a_im.astype(jnp.float32)
    dt = jnp.exp(log_step.astype(jnp.float32))[:, None]
    mag = jnp.exp(lam_re * dt)
    abar_re = mag * jnp.cos(lam_im * dt)
    abar_im = mag * jnp.sin(lam_im * dt)
    den = jnp.square(lam_re) + jnp.square(lam_im)
    nr = abar_re - 1.0
    ni = abar_im
    coef_re = (nr * lam_re + ni * lam_im) / den
    coef_im = (ni * lam_re - nr * lam_im) / den
    bf_re, bf_im = b_re.astype(jnp.float32), b_im.astype(jnp.float32)
    bbar_re = coef_re[..., None] * bf_re - coef_im[..., None] * bf_im
    bbar_im = coef_re[..., None] * bf_im + coef_im[..., None] * bf_re
    bu_re = jnp.einsum('blgc,gpc->blgp', uf, bbar_re)
    bu_im = jnp.einsum('blgc,gpc->blgp', uf, bbar_im)
    at_re = jnp.broadcast_to(abar_re, bu_re.shape)
    at_im = jnp.broadcast_to(abar_im, bu_im.shape)

    def combine(e1, e2):
        a1r, a1i, b1r, b1i = e1
        a2r, a2i, b2r, b2i = e2
        return (a2r * a1r - a2i * a1i,
                a2r * a1i + a2i * a1r,
                a2r * b1r - a2i * b1i + b2r,
                a2r * b1i + a2i * b1r + b2i)

    _, _, x_re, x_im = lax.associative_scan(combine, (at_re, at_im, bu_re, bu_im), axis=1)
    y = (jnp.einsum('blgp,gcp->blgc', x_re, c_re.astype(jnp.float32))
         - jnp.einsum('blgp,gcp->blgc', x_im, c_im.astype(jnp.float32)))
    y = y.reshape(b_, l_, S5_GROUPS * S5_GROUP_CH) + d_skip.astype(jnp.float32) * u.astype(jnp.float32)
    return y.astype(u.dtype)


def _mixer_gla_mla(x, positions, w_in, gla_wg2, gla_bg, gla_norm, mla_q_norm, mla_w_uq, mla_kv_norm, mla_w_ukv, w_out):
    b_, l_, _ = x.shape
    h = x @ w_in
    q, k, v, g_lr, r, c_q, c_kv, k_rope = _split(h, L0_SPLITS)
    gq = q.reshape(b_, l_, GLA_HEADS, GLA_DK) * (GLA_DK ** -0.5)
    gk = k.reshape(b_, l_, GLA_HEADS, GLA_DK)
    gv = v.reshape(b_, l_, GLA_HEADS, GLA_DV)
    log_a = (jax.nn.log_sigmoid((g_lr @ gla_wg2 + gla_bg).astype(jnp.float32)) / GLA_TAU).reshape(b_, l_, GLA_HEADS, GLA_DK)
    o = _rmsnorm(_gla(gq, gk, gv, log_a), gla_norm.reshape(GLA_HEADS, GLA_DV))
    o_gla = (o.reshape(b_, l_, GLA_HEADS * GLA_DV) * jax.nn.silu(r.astype(jnp.float32))).astype(x.dtype)
    qm = (_rmsnorm(c_q, mla_q_norm) @ mla_w_uq).reshape(b_, l_, MLA_HEADS, MLA_NOPE + MLA_ROPE)
    kvm = (_rmsnorm(c_kv, mla_kv_norm) @ mla_w_ukv).reshape(b_, l_, MLA_HEADS, MLA_NOPE + MLA_V)
    q_nope, q_rot = qm[..., :MLA_NOPE], _rope(qm[..., MLA_NOPE:], positions)
    k_nope, vm = kvm[..., :MLA_NOPE], kvm[..., MLA_NOPE:]
    k_rot = _rope(k_rope[:, :, None, :], positions)
    qf = jnp.concatenate([q_nope, q_rot], -1)
    kf = jnp.concatenate([k_nope, jnp.broadcast_to(k_rot, (b_, l_, MLA_HEADS, MLA_ROPE))], -1)
    o_mla = _causal_block_attention(qf, kf, vm, (MLA_NOPE + MLA_ROPE) ** -0.5).reshape(b_, l_, MLA_HEADS * MLA_V)
    return jnp.concatenate([o_gla, o_mla], -1) @ w_out


def _mixer_dsa_s5(x, w_in, s5_a_re, s5_a_im, s5_b_re, s5_b_im, s5_c_re, s5_c_im, s5_d, s5_log_step, glu_w, glu_b, w_out):
    b_, l_, _ = x.shape
    h = x @ w_in
    q, k, v, qi, ki, wi, u = _split(h, L1_SPLITS)
    o_dsa = _dsa(q.reshape(b_, l_, DSA_HEADS, DSA_HEAD_DIM),
                 k.reshape(b_, l_, DSA_HEADS, DSA_HEAD_DIM),
                 v.reshape(b_, l_, DSA_HEADS, DSA_HEAD_DIM),
                 qi.reshape(b_, l_, IDX_HEADS, IDX_DIM),
                 ki,
                 wi * (IDX_HEADS ** -0.5)).reshape(b_, l_, DSA_HEADS * DSA_HEAD_DIM)
    y = jax.nn.gelu(_s5(u, s5_a_re, s5_a_im, s5_b_re, s5_b_im, s5_c_re, s5_c_im, s5_d, s5_log_step))
    o_s5 = y * jax.nn.sigmoid(y @ glu_w + glu_b)
    return jnp.concatenate([o_dsa, o_s5], -1) @ w_out


def _sq_relu_mlp(x, w1, w2):
    return jnp.square(jax.nn.relu(x @ w1)) @ w2


def setup_inputs(seed: int = 0) -> dict:
    key = jax.random.key(seed)
    ks = list(jax.random.split(key, 64))

    def nrm(shape, scale):
        return jax.random.normal(ks.pop(), shape, jnp.float32) * scale

    def gain(n):
        return 1.0 + nrm((n,), 0.01)

    x = nrm((BATCH, SEQ, D_MODEL), 1.0)
    offs = jax.random.randint(ks.pop(), (BATCH, 1), 0, 1024, dtype=jnp.int32)
    positions = offs + jnp.arange(SEQ, dtype=jnp.int32)[None, :]
    s5w = S5_GROUPS * S5_GROUP_CH
    n_idx = jnp.arange(S5_STATE, dtype=jnp.float32)[None, :]
    return {
        'x': x,
        'positions': positions,
        'l0_w_in': nrm((D_MODEL, L0_IN), D_MODEL ** -0.5),
        'l0_gla_wg2': nrm((GLA_GATE_RANK, GLA_HEADS * GLA_DK), GLA_GATE_RANK ** -0.5),
        'l0_gla_bg': nrm((GLA_HEADS * GLA_DK,), 0.01),
        'l0_gla_norm': gain(GLA_HEADS * GLA_DV),
        'l0_mla_q_norm': gain(MLA_Q_RANK),
        'l0_mla_w_uq': nrm((MLA_Q_RANK, MLA_HEADS * (MLA_NOPE + MLA_ROPE)), MLA_Q_RANK ** -0.5),
        'l0_mla_kv_norm': gain(MLA_KV_RANK),
        'l0_mla_w_ukv': nrm((MLA_KV_RANK, MLA_HEADS * (MLA_NOPE + MLA_V)), MLA_KV_RANK ** -0.5),
        'l0_w_out': nrm((MIX0, D_MODEL), DEEPNORM_BETA * MIX0 ** -0.5),
        'l0_ln1_g': gain(D_MODEL),
        'l0_ln1_b': nrm((D_MODEL,), 0.01),
        'l0_mlp_w1': nrm((D_MODEL, D_FF), D_MODEL ** -0.5),
        'l0_mlp_w2': nrm((D_FF, D_MODEL), DEEPNORM_BETA * D_FF ** -0.5),
        'l0_ln2_g': gain(D_MODEL),
        'l0_ln2_b': nrm((D_MODEL,), 0.01),
        'l1_w_in': nrm((D_MODEL, L1_IN), D_MODEL ** -0.5),
        'l1_s5_a_re': -0.5 + nrm((S5_GROUPS, S5_STATE), 0.01),
        'l1_s5_a_im': math.pi * n_idx + nrm((S5_GROUPS, S5_STATE), 0.01),
        'l1_s5_b_re': nrm((S5_GROUPS, S5_STATE, S5_GROUP_CH), (2 * S5_GROUP_CH) ** -0.5),
        'l1_s5_b_im': nrm((S5_GROUPS, S5_STATE, S5_GROUP_CH), (2 * S5_GROUP_CH) ** -0.5),
        'l1_s5_c_re': nrm((S5_GROUPS, S5_GROUP_CH, S5_STATE), (2 * S5_STATE) ** -0.5),
        'l1_s5_c_im': nrm((S5_GROUPS, S5_GROUP_CH, S5_STATE), (2 * S5_STATE) ** -0.5),
        'l1_s5_d': nrm((s5w,), 1.0),
        'l1_s5_log_step': jax.random.uniform(ks.pop(), (S5_GROUPS,), jnp.float32, math.log(S5_DT_MIN), math.log(S5_DT_MAX)),
        'l1_glu_w': nrm((s5w, s5w), s5w ** -0.5),
        'l1_glu_b': nrm((s5w,), 0.01),
        'l1_w_out': nrm((MIX1, D_MODEL), DEEPNORM_BETA * MIX1 ** -0.5),
        'l1_ln1_g': gain(D_MODEL),
        'l1_ln1_b': nrm((D_MODEL,), 0.01),
        'l1_mlp_w1': nrm((D_MODEL, D_FF), D_MODEL ** -0.5),
        'l1_mlp_w2': nrm((D_FF, D_MODEL), DEEPNORM_BETA * D_FF ** -0.5),
        'l1_ln2_g': gain(D_MODEL),
        'l1_ln2_b': nrm((D_MODEL,), 0.01),
    }


def reference(x, positions,
              l0_w_in, l0_gla_wg2, l0_gla_bg, l0_gla_norm, l0_mla_q_norm, l0_mla_w_uq, l0_mla_kv_norm, l0_mla_w_ukv, l0_w_out,
              l0_ln1_g, l0_ln1_b, l0_mlp_w1, l0_mlp_w2, l0_ln2_g, l0_ln2_b,
              l1_w_in, l1_s5_a_re, l1_s5_a_im, l1_s5_b_re, l1_s5_b_im, l1_s5_c_re, l1_s5_c_im, l1_s5_d, l1_s5_log_step,
              l1_glu_w, l1_glu_b, l1_w_out,
              l1_ln1_g, l1_ln1_b, l1_mlp_w1, l1_mlp_w2, l1_ln2_g, l1_ln2_b):
    mixer_params = (
        (l0_w_in, l0_gla_wg2, l0_gla_bg, l0_gla_norm, l0_mla_q_norm, l0_mla_w_uq, l0_mla_kv_norm, l0_mla_w_ukv, l0_w_out),
        (l1_w_in, l1_s5_a_re, l1_s5_a_im, l1_s5_b_re, l1_s5_b_im, l1_s5_c_re, l1_s5_c_im, l1_s5_d, l1_s5_log_step, l1_glu_w, l1_glu_b, l1_w_out),
    )
    ffn_params = (
        (l0_ln1_g, l0_ln1_b, l0_mlp_w1, l0_mlp_w2, l0_ln2_g, l0_ln2_b),
        (l1_ln1_g, l1_ln1_b, l1_mlp_w1, l1_mlp_w2, l1_ln2_g, l1_ln2_b),
    )
    for i in range(DEPTH):
        ln1_g, ln1_b, w1, w2, ln2_g, ln2_b = ffn_params[i]
        if i % 2 == 0:
            mix = _mixer_gla_mla(x, positions, *mixer_params[i])
        else:
            mix = _mixer_dsa_s5(x, *mixer_params[i])
        x = _layernorm(DEEPNORM_ALPHA * x + mix, ln1_g, ln1_b)
        x = _layernorm(DEEPNORM_ALPHA * x + _sq_relu_mlp(x, w1, w2), ln2_g, ln2_b)
    return x
```

```python
import numpy as np
import ml_dtypes
import concourse.bass as bass
import concourse.mybir as mybir
from concourse.bass_utils import run_bass_kernel_spmd

F32 = mybir.dt.float32
BF16 = mybir.dt.bfloat16
I32 = mybir.dt.int32
AF = mybir.ActivationFunctionType
ALU = mybir.AluOpType
AX = mybir.AxisListType
NPBF = ml_dtypes.bfloat16

NCORES = 8
D = 1024
SEQ = 8192
TOK = 4096
LN_EPS = 1e-5
ALPHA = 4 ** 0.25


class Buf:
    __slots__ = ("t", "w", "wb", "r", "pr", "name", "psum")

    def __init__(self, t, name="", psum=False):
        self.psum = psum
        self.t = t
        self.w = {}
        self.wb = {}
        self.r = {}
        self.pr = {}
        self.name = name

    def __getitem__(self, idx):
        return self.t[idx]


class Prog:
    NDMA = 14

    def __init__(self, nc):
        self.nc = nc
        self.E = {"pe": nc.tensor, "act": nc.scalar, "dve": nc.vector,
                  "pool": nc.gpsimd, "sp": nc.sync}
        self.sem = {}
        self.cnt = {}
        for k in self.E:
            self.sem[k] = nc.alloc_semaphore("s_" + k)
            self.cnt[k] = 0
        for i in range(self.NDMA):
            k = "d%d" % i
            self.sem[k] = nc.alloc_semaphore("s_" + k)
            self.cnt[k] = 0
        self.waited = {}
        self.dma_rr = 0
        self.n_inst = 0
        self.dq = 0

    def sb(self, name, shape, dt):
        return Buf(self.nc.alloc_sbuf_tensor(name, list(shape), dt), name)

    def ps(self, name, shape, dt=F32):
        return Buf(self.nc.alloc_psum_tensor(name, list(shape), dt), name, psum=True)

    def _need(self, eng, deps):
        q = self.E[eng]
        for (k, v) in deps.items():
            if k == "pe" and eng == "pe":
                continue
            if self.waited.get((eng, k), 0) < v:
                q.wait_ge(self.sem[k], v)
                self.waited[(eng, k)] = v
                self.n_inst += 1

    @staticmethod
    def _mx(m, k, v):
        if m.get(k, 0) < v:
            m[k] = v

    def _deps(self, reads, writes, add):
        m = {}
        for b in reads:
            for k, v in b.w.items():
                self._mx(m, k, v)
            if b.psum:
                for k, v in b.r.items():
                    self._mx(m, k, v)
        for b in writes:
            for k, v in (b.wb if add else b.w).items():
                self._mx(m, k, v)
            for k, v in b.r.items():
                self._mx(m, k, v)
            if add:
                for k, v in b.pr.items():
                    self._mx(m, k, v)
        return m

    def _commit(self, key, val, reads, writes, add):
        for b in reads:
            b.r[key] = val
        for b in writes:
            if add:
                b.w[key] = val
            else:
                b.w = {key: val}
                b.wb = {key: val}
                b.pr = b.r
                b.r = {}

    def op(self, eng, fn, reads=(), writes=(), add=False):
        self.nops = getattr(self, "nops", 0) + 1
        if self.nops > getattr(self, "limit", 10 ** 9):
            return None
        self._need(eng, self._deps(reads, writes, add))
        ins = fn(self.E[eng])
        self.cnt[eng] += 1
        ins.then_inc(self.sem[eng], 1)
        self._commit(eng, self.cnt[eng], reads, writes, add)
        self.n_inst += 1
        return ins

    def dma(self, out, in_, reads=(), writes=(), q=None, add=False, **kw):
        if q is None:
            q = ("sp", "sp")[self.dq % 2]
            self.dq += 1
        i = self.dma_rr
        self.dma_rr = (self.dma_rr + 1) % self.NDMA
        k = "d%d" % i
        deps = self._deps(reads, writes, add)
        if self.cnt[k] > 0:
            self._mx(deps, k, self.cnt[k])
        self._need(q, deps)
        ins = self.E[q].dma_start(out=out, in_=in_, **kw)
        self.cnt[k] += 16
        ins.then_inc(self.sem[k], 16)
        self._commit(k, self.cnt[k], reads, writes, add)
        self.n_inst += 1
        return ins

    def finish(self):
        deps = {"d%d" % i: self.cnt["d%d" % i] for i in range(self.NDMA)
                if self.cnt["d%d" % i] > 0}
        self._need("sp", deps)


def _newnc():
    return bass.Bass("TRN2", target_bir_lowering=False)


def _din(nc, name, shape, dt=F32):
    return nc.dram_tensor(name, list(shape), dt, kind="ExternalInput").ap()


def _dout(nc, name, shape, dt=F32):
    return nc.dram_tensor(name, list(shape), dt, kind="ExternalOutput").ap()


def _run(nc, in_maps):
    res = run_bass_kernel_spmd(nc, in_maps, core_ids=list(range(NCORES)))
    return res.results


def load_cast(p, dst_bf, dst_ap, src_ap, stage, stage_ap, eng="pool"):
    p.dma(stage_ap, src_ap, writes=[stage])
    p.op(eng, lambda e: e.tensor_copy(dst_ap, stage_ap), [stage], [dst_bf])


L0_IN = 1968


def build_ka0():
    nc = _newnc()
    xT = _din(nc, "xT", [D, TOK])
    posl = _din(nc, "posl", [128, 32], I32)
    w_in = _din(nc, "w_in", [D, L0_IN])
    wg2 = _din(nc, "wg2", [16, 256])
    bg = _din(nc, "bg", [1, 256])
    qn = _din(nc, "qn", [128, 2])
    w_uq = _din(nc, "w_uq", [256, 768])
    kvn = _din(nc, "kvn", [128, 1])
    w_ukv = _din(nc, "w_ukv", [128, 1024])
    invf = _din(nc, "invf", [1, 128])
    o_gq = _dout(nc, "o_gq", [TOK, 256], BF16)
    o_gk = _dout(nc, "o_gk", [TOK, 256], BF16)
    o_gv = _dout(nc, "o_gv", [TOK, 512], BF16)
    o_ga = _dout(nc, "o_ga", [TOK, 256], F32)
    o_sr = _dout(nc, "o_sr", [TOK, 512], F32)
    o_qf = _dout(nc, "o_qf", [TOK, 768], BF16)
    o_kf = _dout(nc, "o_kf", [TOK, 768], BF16)
    o_vm = _dout(nc, "o_vm", [TOK, 512], BF16)
    p = Prog(nc)
    import os
    p.limit = int(os.environ.get('KA0_LIM', '1000000000'))

    winb = p.sb("winb", [128, 8, L0_IN], BF16)
    wst = [p.sb("wst%d" % i, [128, L0_IN], F32) for i in range(2)]
    w_in_v = w_in.rearrange("(c p) n -> p c n", p=128)
    for c in range(8):
        st = wst[c % 2]
        p.dma(st[:], w_in_v[:, c, :], writes=[st])
        p.op(("pool", "dve")[c % 2], lambda e: e.tensor_copy(winb[:, c, :], st[:]), [st], [winb], add=True)
    qn_s = p.sb("qn_s", [128, 2], F32)
    kvn_s = p.sb("kvn_s", [128, 1], F32)
    p.dma(qn_s[:], qn[:, :], writes=[qn_s])
    p.dma(kvn_s[:], kvn[:, :], writes=[kvn_s])
    wuqb = p.sb("wuqb", [128, 2, 768], BF16)
    wukvb = p.sb("wukvb", [128, 1024], BF16)
    st = wst[0]
    p.dma(st[:, 0:1536].rearrange("p (c n) -> p c n", c=2), w_uq.rearrange("(c p) n -> p c n", p=128), writes=[st])
    for c in range(2):
        p.op("dve", lambda e: e.tensor_scalar(wuqb[:, c, :], st[:, c * 768:(c + 1) * 768], qn_s[:, c:c + 1],
                                               96 ** -0.5, ALU.mult, ALU.mult), [st, qn_s], [wuqb])
    st = wst[1]
    p.dma(st[:, 0:1024], w_ukv[:, :], writes=[st])
    p.op("dve", lambda e: e.tensor_scalar(wukvb[:], st[:, 0:1024], kvn_s[:, 0:1], None, ALU.mult), [st, kvn_s], [wukvb])
    wg2s = p.sb("wg2s", [16, 256], F32)
    wg2b = p.sb("wg2b", [16, 256], BF16)
    bgs = p.sb("bgs", [1, 256], F32)
    bgb = p.sb("bgb", [1, 256], BF16)
    onesb = p.sb("onesb", [1, 128], BF16)
    p.dma(wg2s[:], wg2[:, :], writes=[wg2s])
    p.dma(bgs[:], bg[:, :], writes=[bgs])
    p.op("dve", lambda e: e.tensor_copy(wg2b[:], wg2s[:]), [wg2s], [wg2b])
    p.op("dve", lambda e: e.tensor_copy(bgb[:], bgs[:]), [bgs], [bgb])
    p.op("dve", lambda e: e.memset(onesb[:], 1.0), [], [onesb])
    invf8 = p.sb("invf8", [128, 128], F32)
    p.dma(invf8[:], invf[0:1, :].to_broadcast([128, 128]), writes=[invf8])
    p.op("dve", lambda e: e.tensor_scalar(invf8[:], invf8[:], 1.0 / (2 * np.pi), None, ALU.mult), [invf8], [invf8])
    posi = p.sb("posi", [128, 32], I32)
    posf = p.sb("posf", [128, 32], F32)
    p.dma(posi[:], posl[:, :], writes=[posi])
    p.op("dve", lambda e: e.tensor_copy(posf[:], posi[:]), [posi], [posf])

    xs = [p.sb("xs%d" % i, [128, 8, 512], F32) for i in range(2)]
    xb = [p.sb("xb%d" % i, [128, 8, 512], BF16) for i in range(2)]
    fmb = [p.sb("fmb%d" % i, [128, 4, 512], BF16) for i in range(2)]
    hps = [p.ps("hps%d" % i, [128, 512]) for i in range(4)]
    fps = [p.ps("fps%d" % i, [128, 512]) for i in range(2)]
    sps = [p.ps("sps%d" % i, [128, 512]) for i in range(2)]
    NB = 2
    gq_t = [p.sb("gq_t%d" % i, [128, 256], BF16) for i in range(NB)]
    gk_t = [p.sb("gk_t%d" % i, [128, 256], BF16) for i in range(NB)]
    gv_t = [p.sb("gv_t%d" % i, [128, 512], BF16) for i in range(NB)]
    ga_t = [p.sb("ga_t%d" % i, [128, 256], F32) for i in range(NB)]
    sr_t = [p.sb("sr_t%d" % i, [128, 512], F32) for i in range(NB)]
    qf_t = [p.sb("qf_t%d" % i, [128, 8, 96], BF16) for i in range(NB)]
    kf_t = [p.sb("kf_t%d" % i, [128, 8, 96], BF16) for i in range(NB)]
    vm_t = [p.sb("vm_t%d" % i, [128, 8, 64], BF16) for i in range(NB)]
    junk = p.sb("junk", [128, 256], F32)
    ss = p.sb("ss", [128, 2], F32)
    rstd = p.sb("rstd", [128, 2], F32)
    t2 = p.sb("t2", [128, 2, 128], F32)
    ti = p.sb("ti", [128, 2, 128], I32)
    tf = p.sb("tf", [128, 2, 128], F32)
    sc = p.sb("sc", [128, 2, 128], F32)
    Qs = p.sb("Qs", [128, 8, 96], F32)
    ra = p.sb("ra", [128, 8, 16], F32)
    rb = p.sb("rb", [128, 8, 16], F32)
    kr = p.sb("kr", [128, 32], F32)
    kro = p.sb("kro", [128, 32], F32)
    ez = p.sb("ez", [128, 256], F32)

    xT_v = xT.rearrange("(c p) t -> p c t", p=128)
    fm_cols = [(1552, 128), (1680, 128), (1808, 128), (1024, 16)]
    import os
    for g in range(int(os.environ.get('KA0_G', '8'))):
        xsg, xbg, fm = xs[g % 2], xb[g % 2], fmb[g % 2]
        for c in range(8):
            p.dma(xsg[:, c, :], xT_v[:, c, g * 512:(g + 1) * 512], writes=[xsg], add=(c > 0))
        for c in range(8):
            p.op(("pool", "dve")[c % 2], lambda e: e.tensor_copy(xbg[:, c, :], xsg[:, c, :]), [xsg], [xbg], add=(c > 0))
        for mi, (c0, m) in enumerate(fm_cols):
            fp_ = fps[mi % 2]
            for c in range(8):
                p.op("pe", lambda e: e.matmul(fp_[0:m, :], winb[:, c, c0:c0 + m], xbg[:, c, :],
                                              start=(c == 0), stop=(c == 7)), [winb, xbg], [fp_])
            p.op("act", lambda e: e.copy(fm[0:m, mi, :], fp_[0:m, :]), [fp_], [fm])
        for s in range(int(os.environ.get('KA0_S', '4'))):
            sub = g * 4 + s
            b = sub % NB
            tsl = slice(s * 128, (s + 1) * 128)
            for bi in range(4):
                n0 = bi * 512
                n = min(512, L0_IN - n0)
                for c in range(8):
                    p.op("pe", lambda e: e.matmul(hps[bi][:, 0:n], xbg[:, c, tsl], winb[:, c, n0:n0 + n],
                                                  start=(c == 0), stop=(c == 7)), [xbg, winb], [hps[bi]])
            if os.environ.get('KA0_NOROPE', '0') == '1':
                p.op('act', lambda e: e.activation(sc[:], posf[:, 0:1].to_broadcast([128, 256]).rearrange('p (a b) -> p a b', a=2), AF.Copy), [posf], [sc])
            else:
                p.op("dve", lambda e: e.tensor_scalar(t2[:, 0, :], invf8[:], posf[:, sub:sub + 1], None, ALU.mult),
                     [invf8, posf], [t2])
                p.op("dve", lambda e: e.tensor_scalar(t2[:, 1, :], t2[:, 0, :], 0.25, None, ALU.add), [t2], [t2])
                p.op("dve", lambda e: e.tensor_copy(ti[:], t2[:]), [t2], [ti])
                p.op("dve", lambda e: e.tensor_copy(tf[:], ti[:]), [ti], [tf])
                p.op("dve", lambda e: e.tensor_sub(t2[:], t2[:], tf[:]), [t2, tf], [t2])
                p.op("act", lambda e: e.activation(sc[:], t2[:], AF.Sin, scale=6.28318), [t2], [sc])
            p.op("act", lambda e: e.mul(gq_t[b][:], hps[0][:, 0:256], 0.125), [hps[0]], [gq_t[b]])
            if os.environ.get('KA0_V', '0') == '1':
                p.op("act", lambda e: e.copy(gk_t[b][:], hps[0][:, 256:512]), [hps[0]], [gk_t[b]])
            elif os.environ.get('KA0_V', '0') == '4':
                p.op("dve", lambda e: e.tensor_copy(gk_t[b][:], hps[0][:, 256:512]), [hps[0], gq_t[b]], [gk_t[b]])
            elif os.environ.get('KA0_V', '0') == '2':
                p.op("pool", lambda e: e.tensor_copy(kr[:], kr[:]), [kr], [kr])
            else:
                p.op("dve", lambda e: e.tensor_copy(gk_t[b][:], hps[0][:, 256:512]), [hps[0]], [gk_t[b]])
            p.op("act", lambda e: e.copy(gv_t[b][:], hps[1][:, :]), [hps[1]], [gv_t[b]])
            p.op("act", lambda e: e.activation(sr_t[b][:, 0:496], hps[2][:, 16:512], AF.Silu), [hps[2]], [sr_t[b]])
            p.op("act", lambda e: e.activation(sr_t[b][:, 496:512], hps[3][:, 0:16], AF.Silu), [hps[3]], [sr_t[b]])
            p.op("act", lambda e: e.activation(junk[:, 0:256], hps[3][:, 16:272], AF.Square, accum_out=ss[:, 0:1]),
                 [hps[3]], [junk, ss])
            p.op("act", lambda e: e.activation(junk[:, 0:128], hps[3][:, 272:400], AF.Square, accum_out=ss[:, 1:2]),
                 [hps[3]], [junk, ss])
            p.op("dve", lambda e: e.tensor_scalar(rstd[:, 0:1], ss[:, 0:1], 1.0 / 256, LN_EPS, ALU.mult, ALU.add), [ss], [rstd])
            p.op("dve", lambda e: e.tensor_scalar(rstd[:, 1:2], ss[:, 1:2], 1.0 / 128, LN_EPS, ALU.mult, ALU.add), [ss], [rstd])
            p.op("act", lambda e: e.sqrt(rstd[:], rstd[:]), [rstd], [rstd])
            p.op("dve", lambda e: e.reciprocal(rstd[:], rstd[:]), [rstd], [rstd])
            p.op("dve", lambda e: e.tensor_copy(kr[:], hps[3][:, 400:432]), [hps[3]], [kr])
            for bi, (n0, n) in enumerate(((0, 512), (512, 256))):
                for c in range(2):
                    p.op("pe", lambda e: e.matmul(sps[bi][:, 0:n], fm[:, c, tsl], wuqb[:, c, n0:n0 + n],
                                                  start=(c == 0), stop=(c == 1)), [fm, wuqb], [sps[bi]])
            Qf = Qs[:].rearrange("p h d -> p (h d)")
            p.op("act", lambda e: e.activation(Qf[:, 0:512], sps[0][:, :], AF.Copy, scale=rstd[:, 0:1]), [sps[0], rstd], [Qs])
            p.op("act", lambda e: e.activation(Qf[:, 512:768], sps[1][:, 0:256], AF.Copy, scale=rstd[:, 0:1]), [sps[1], rstd], [Qs])
            qf = qf_t[b]
            p.op("pool", lambda e: e.tensor_copy(qf[:, :, 0:64], Qs[:, :, 0:64]), [Qs], [qf])
            sin8 = sc[:, 0, :].rearrange("p (h j) -> p h j", h=8)
            cos8 = sc[:, 1, :].rearrange("p (h j) -> p h j", h=8)
            p.op("dve", lambda e: e.tensor_mul(ra[:], Qs[:, :, 64:80], cos8), [Qs, sc], [ra])
            p.op("pool", lambda e: e.tensor_mul(rb[:], Qs[:, :, 80:96], sin8), [Qs, sc], [rb])
            p.op("dve", lambda e: e.tensor_sub(qf[:, :, 64:80], ra[:], rb[:]), [ra, rb], [qf])
            p.op("dve", lambda e: e.tensor_mul(ra[:], Qs[:, :, 80:96], cos8), [Qs, sc], [ra])
            p.op("pool", lambda e: e.tensor_mul(rb[:], Qs[:, :, 64:80], sin8), [Qs, sc], [rb])
            p.op("dve", lambda e: e.tensor_add(qf[:, :, 80:96], ra[:], rb[:]), [ra, rb], [qf])
            for bi in range(2):
                p.op("pe", lambda e: e.matmul(sps[bi][:, :], fm[:, 2, tsl], wukvb[:, bi * 512:(bi + 1) * 512],
                                              start=True, stop=True), [fm, wukvb], [sps[bi]])
            kf, vm = kf_t[b], vm_t[b]
            for bi in range(2):
                src = sps[bi][:, :].rearrange("p (h d) -> p h d", h=4)
                p.op("act", lambda e: e.activation(kf[:, bi * 4:(bi + 1) * 4, 0:64], src[:, :, 0:64], AF.Copy,
                                                   scale=rstd[:, 1:2]), [sps[bi], rstd], [kf])
                p.op("act", lambda e: e.activation(vm[:, bi * 4:(bi + 1) * 4, :], src[:, :, 64:128], AF.Copy,
                                                   scale=rstd[:, 1:2]), [sps[bi], rstd], [vm])
            s16, c16 = sc[:, 0, 0:16], sc[:, 1, 0:16]
            p.op("dve", lambda e: e.tensor_mul(ra[:, 0, :], kr[:, 0:16], c16), [kr, sc], [ra])
            p.op("dve", lambda e: e.tensor_mul(rb[:, 0, :], kr[:, 16:32], s16), [kr, sc], [rb])
            p.op("dve", lambda e: e.tensor_sub(kro[:, 0:16], ra[:, 0, :], rb[:, 0, :]), [ra, rb], [kro])
            p.op("dve", lambda e: e.tensor_mul(ra[:, 0, :], kr[:, 16:32], c16), [kr, sc], [ra])
            p.op("dve", lambda e: e.tensor_mul(rb[:, 0, :], kr[:, 0:16], s16), [kr, sc], [rb])
            p.op("dve", lambda e: e.tensor_add(kro[:, 16:32], ra[:, 0, :], rb[:, 0, :]), [ra, rb], [kro])
            p.op("pool", lambda e: e.tensor_copy(kf[:, :, 64:96], kro[:].unsqueeze(1).to_broadcast([128, 8, 32])), [kro], [kf])
            p.op("pe", lambda e: e.matmul(sps[0][:, 0:256], fm[0:16, 3, tsl], wg2b[:], start=True, stop=False), [fm, wg2b], [sps[0]])
            p.op("pe", lambda e: e.matmul(sps[0][:, 0:256], onesb[:], bgb[:], start=False, stop=True), [onesb, bgb], [sps[0]])
            p.op("act", lambda e: e.activation(ez[:], sps[0][:, 0:256], AF.Exp, scale=-1.0), [sps[0]], [ez])
            p.op("act", lambda e: e.activation(ez[:], ez[:], AF.Ln, bias=1.0), [ez], [ez])
            p.op("pool", lambda e: e.tensor_scalar(ga_t[b][:], ez[:], -1.0 / 16, None, ALU.mult), [ez], [ga_t[b]])
            r0 = sub * 128
            p.dma(o_gq[r0:r0 + 128, :], gq_t[b][:], reads=[gq_t[b]])
            p.dma(o_gk[r0:r0 + 128, :], gk_t[b][:], reads=[gk_t[b]])
            p.dma(o_gv[r0:r0 + 128, :], gv_t[b][:], reads=[gv_t[b]])
            p.dma(o_ga[r0:r0 + 128, :], ga_t[b][:], reads=[ga_t[b]])
            p.dma(o_sr[r0:r0 + 128, :], sr_t[b][:], reads=[sr_t[b]])
            p.dma(o_qf[r0:r0 + 128, :], qf[:].rearrange("p h d -> p (h d)"), reads=[qf])
            p.dma(o_kf[r0:r0 + 128, :], kf[:].rearrange("p h d -> p (h d)"), reads=[kf])
            p.dma(o_vm[r0:r0 + 128, :], vm[:].rearrange("p h d -> p (h d)"), reads=[vm])
    p.finish()
    print('ka0 nops', p.nops, 'n_inst', p.n_inst)
    return nc


def invfreq_const():
    half = 16
    inv = (10000.0 ** (-np.arange(half, dtype=np.float32) / half)).astype(np.float32)
    return np.tile(inv, 8)[None, :].astype(np.float32)


def run_ka0(x, positions, W):
    xf = x.reshape(32768, D)
    in_maps = []
    for c in range(NCORES):
        xs = xf[c * TOK:(c + 1) * TOK]
        pos = positions.reshape(-1)[c * TOK:(c + 1) * TOK]
        in_maps.append(dict(
            xT=np.ascontiguousarray(xs.T),
            posl=np.ascontiguousarray(pos.reshape(32, 128).T),
            w_in=W["l0_w_in"], wg2=W["l0_gla_wg2"], bg=W["l0_gla_bg"].reshape(1, 256),
            qn=np.ascontiguousarray(W["l0_mla_q_norm"].reshape(2, 128).T),
            w_uq=W["l0_mla_w_uq"], kvn=W["l0_mla_kv_norm"].reshape(128, 1),
            w_ukv=W["l0_mla_w_ukv"], invf=invfreq_const()))
    return _run(build_ka0(), in_maps)


def _layernorm_tile(p, v, junkb, st, G, B, out_t):
    p.op("act", lambda e: e.activation(junkb[:], v[:], AF.Copy, accum_out=st[:, 0:1]), [v], [junkb, st])
    p.op("act", lambda e: e.activation(junkb[:], v[:], AF.Square, accum_out=st[:, 1:2]), [v], [junkb, st])
    p.op("dve", lambda e: e.tensor_scalar(st[:, 2:3], st[:, 0:1], 1.0 / D, None, ALU.mult), [st], [st])
    p.op("dve", lambda e: e.tensor_mul(st[:, 3:4], st[:, 2:3], st[:, 2:3]), [st], [st])
    p.op("dve", lambda e: e.scalar_tensor_tensor(st[:, 4:5], st[:, 1:2], 1.0 / D, st[:, 3:4], ALU.mult, ALU.subtract), [st], [st])
    p.op("dve", lambda e: e.tensor_scalar(st[:, 4:5], st[:, 4:5], LN_EPS, None, ALU.add), [st], [st])
    p.op("act", lambda e: e.sqrt(st[:, 4:5], st[:, 4:5]), [st], [st])
    p.op("dve", lambda e: e.reciprocal(st[:, 4:5], st[:, 4:5]), [st], [st])
    p.op("dve", lambda e: e.tensor_scalar(v[:], v[:], st[:, 2:3], st[:, 4:5], ALU.subtract, ALU.mult), [v, st], [v])
    p.op("pool", lambda e: e.tensor_mul(v[:], v[:], G[:]), [v, G], [v])
    p.op("dve", lambda e: e.tensor_add(out_t[:], v[:], B[:]), [v, B], [out_t])


def build_kc(glu=False, ngroups=8):
    nc = _newnc()
    ntok = ngroups * 512
    x = _din(nc, "x", [ntok, D])
    mixT = _din(nc, "mixT", [512 if glu else D, ntok])
    w_out = _din(nc, "w_out", [D, D])
    lnp = _din(nc, "lnp", [4, D])
    w1 = _din(nc, "w1", [D, 4 * D])
    w2 = _din(nc, "w2", [4 * D, D])
    ident = _din(nc, "ident", [128, 128])
    if glu:
        yT = _din(nc, "yT", [512, ntok])
        glu_w = _din(nc, "glu_w", [512, 512])
        glu_b = _din(nc, "glu_b", [128, 4])
    o_x = _dout(nc, "o_x", [ntok, D])
    p = Prog(nc)

    idf = p.sb("idf", [128, 128], F32)
    idb = p.sb("idb", [128, 128], BF16)
    p.dma(idf[:], ident[:, :], writes=[idf])
    p.op("dve", lambda e: e.tensor_copy(idb[:], idf[:]), [idf], [idb])
    GB = [p.sb("GB%d" % i, [128, D], F32) for i in range(4)]
    for i in range(4):
        p.dma(GB[i][:], lnp[i:i + 1, :].to_broadcast([128, D]), writes=[GB[i]])
    stg = [p.sb("stg%d" % i, [128, 1024], F32) for i in range(2)]
    woutb = p.sb("woutb", [128, 8, D], BF16)
    w2b = p.sb("w2b", [128, 32, D], BF16)
    wo_v = w_out.rearrange("(c p) n -> p c n", p=128)
    w2_v = w2.rearrange("(c p) n -> p c n", p=128)
    k = 0
    for c in range(8):
        st_ = stg[k % 2]
        p.dma(st_[:], wo_v[:, c, :], writes=[st_])
        p.op(("pool", "dve")[k % 2], lambda e: e.tensor_copy(woutb[:, c, :], st_[:]), [st_], [woutb], add=True)
        k += 1
    for c in range(32):
        st_ = stg[k % 2]
        p.dma(st_[:], w2_v[:, c, :], writes=[st_])
        p.op(("pool", "dve")[k % 2], lambda e: e.tensor_copy(w2b[:, c, :], st_[:]), [st_], [w2b], add=True)
        k += 1
    if glu:
        glub = p.sb("glub", [128, 4, 512], BF16)
        gbias = p.sb("gbias", [128, 4], F32)
        p.dma(gbias[:], glu_b[:, :], writes=[gbias])
        gw_v = glu_w.rearrange("(c p) n -> p c n", p=128)
        for c in range(4):
            st_ = stg[k % 2]
            p.dma(st_[:, 0:512], gw_v[:, c, :], writes=[st_])
            p.op(("pool", "dve")[k % 2], lambda e: e.tensor_copy(glub[:, c, :], st_[:, 0:512]), [st_], [glub], add=True)
            k += 1

    mixb = p.sb("mixb", [128, 8, 512], BF16)
    x1T = p.sb("x1T", [128, 8, 512], BF16)
    x1a = p.sb("x1a", [128, 4, D], F32)
    hT = p.sb("hT", [128, 32, 512], BF16)
    xs = [p.sb("xs%d" % i, [128, D], F32) for i in range(2)]
    vb0 = p.sb("vb0", [128, D], F32)
    vb = [vb0, vb0]
    x1b = p.sb("x1b", [128, D], BF16)
    stt = p.sb("stt", [128, 8], F32)
    w1s = [p.sb("w1s%d" % i, [128, 8, 128], F32) for i in range(2)]
    w1b = [p.sb("w1b%d" % i, [128, 8, 128], BF16) for i in range(2)]
    rl = [p.sb("rl%d" % i, [128, 512], F32) for i in range(2)]
    yps = [p.ps("yps%d" % i, [128, 512]) for i in range(4)]
    tps = p.ps("tps", [128, 8, 128], BF16)
    hps = [p.ps("hps%d" % i, [128, 512]) for i in range(2)]
    if glu:
        ygb = p.sb("ygb", [128, 4, 512], BF16)
    print("kc sbuf remaining", nc.sbuf_bytes_remaining)

    mix_v = mixT.rearrange("(c p) t -> p c t", p=128)
    w1_v = w1.rearrange("(c p) n -> p c n", p=128)
    if glu:
        y_v = yT.rearrange("(c p) t -> p c t", p=128)
    for g in range(ngroups):
        gs = slice(g * 512, (g + 1) * 512)
        nmix = 4 if glu else 8
        for c in range(nmix):
            st_ = stg[k % 2]
            p.dma(st_[:, 0:512], mix_v[:, c, gs], writes=[st_])
            p.op(("pool", "dve")[k % 2], lambda e: e.tensor_copy(mixb[:, c, :], st_[:, 0:512]), [st_], [mixb], add=(c > 0))
            k += 1
        if glu:
            for c in range(4):
                ys_, yg_ = rl[0], rl[1]
                p.dma(ys_[:], y_v[:, c, gs], writes=[ys_])
                p.op("act", lambda e: e.activation(yg_[:], ys_[:], AF.Square), [ys_], [yg_])
                p.op("dve", lambda e: e.tensor_scalar(yg_[:], yg_[:], 0.044715, 1.0, ALU.mult, ALU.add), [yg_], [yg_])
                p.op("pool", lambda e: e.tensor_mul(yg_[:], yg_[:], ys_[:]), [yg_, ys_], [yg_])
                p.op("act", lambda e: e.activation(yg_[:], yg_[:], AF.Sigmoid, scale=1.5957691216), [yg_], [yg_])
                p.op("dve", lambda e: e.tensor_mul(ygb[:, c, :], yg_[:], ys_[:]), [yg_, ys_], [ygb], add=(c > 0))
            for m in range(4):
                hp = hps[m % 2]
                for c in range(4):
                    p.op("pe", lambda e: e.matmul(hp[:], glub[:, c, m * 128:(m + 1) * 128], ygb[:, c, :],
                                                  start=(c == 0), stop=(c == 3)), [glub, ygb], [hp])
                sg = rl[m % 2]
                p.op("act", lambda e: e.activation(sg[:], hp[:], AF.Sigmoid, bias=gbias[:, m:m + 1]), [hp, gbias], [sg])
                p.op("dve", lambda e: e.tensor_mul(mixb[:, 4 + m, :], sg[:], ygb[:, m, :]), [sg, ygb], [mixb], add=True)
        for s in range(4):
            tsl = slice(s * 128, (s + 1) * 128)
            r0 = g * 512 + s * 128
            xt, v = xs[s % 2], vb[s % 2]
            p.dma(xt[:], x[r0:r0 + 128, :], writes=[xt])
            for nh in range(2):
                yp = yps[nh]
                for c in range(8):
                    p.op("pe", lambda e: e.matmul(yp[:], mixb[:, c, tsl], woutb[:, c, nh * 512:(nh + 1) * 512],
                                                  start=(c == 0), stop=(c == 7)), [mixb, woutb], [yp])
                p.op("dve", lambda e: e.scalar_tensor_tensor(v[:, nh * 512:(nh + 1) * 512], xt[:, nh * 512:(nh + 1) * 512],
                                                              ALPHA, yp[:], ALU.mult, ALU.add), [xt, yp], [v], add=(nh > 0))
            _layernorm_tile(p, v, x1b, stt, GB[0], GB[1], _View(x1a, x1a[:, s, :]))
            p.op("pool", lambda e: e.tensor_copy(x1b[:], x1a[:, s, :]), [x1a], [x1b])
            for c in range(8):
                p.op("pe", lambda e: e.transpose(tps[:, c, :], x1b[:, c * 128:(c + 1) * 128], idb[:]), [x1b, idb], [tps])
            p.op("act", lambda e: e.copy(x1T[:, :, tsl], tps[:]), [tps], [x1T], add=(s > 0))
        for j in range(32):
            ws_, wb_ = w1s[j % 2], w1b[j % 2]
            p.dma(ws_[:], w1_v[:, :, j * 128:(j + 1) * 128], writes=[ws_])
            p.op(("pool", "dve")[j % 2], lambda e: e.tensor_copy(wb_[:], ws_[:]), [ws_], [wb_])
            hp = hps[j % 2]
            for c in range(8):
                p.op("pe", lambda e: e.matmul(hp[:], wb_[:, c, :], x1T[:, c, :], start=(c == 0), stop=(c == 7)), [wb_, x1T], [hp])
            r_ = rl[j % 2]
            p.op("act", lambda e: e.activation(r_[:], hp[:], AF.Relu), [hp], [r_])
            p.op(("dve", "pool")[j % 2], lambda e: e.tensor_mul(hT[:, j, :], r_[:], r_[:]), [r_], [hT], add=(j > 0))
        for s in range(4):
            tsl = slice(s * 128, (s + 1) * 128)
            r0 = g * 512 + s * 128
            v = vb[s % 2]
            for nh in range(2):
                yp = yps[2 + nh]
                for j in range(32):
                    p.op("pe", lambda e: e.matmul(yp[:], hT[:, j, tsl], w2b[:, j, nh * 512:(nh + 1) * 512],
                                                  start=(j == 0), stop=(j == 31)), [hT, w2b], [yp])
                p.op("dve", lambda e: e.scalar_tensor_tensor(v[:, nh * 512:(nh + 1) * 512], x1a[:, s, nh * 512:(nh + 1) * 512],
                                                              ALPHA, yp[:], ALU.mult, ALU.add), [x1a, yp], [v], add=(nh > 0))
            ot = xs[s % 2]
            _layernorm_tile(p, v, x1b, stt, GB[2], GB[3], ot)
            p.dma(o_x[r0:r0 + 128, :], ot[:], reads=[ot])
    p.finish()
    print("kc nops", p.nops, "n_inst", p.n_inst)
    return nc


class _View:
    def __init__(self, parent, ap):
        self.parent = parent
        self.ap = ap

    def __getitem__(self, idx):
        return self.ap

    @property
    def w(self):
        return self.parent.w

    @w.setter
    def w(self, v):
        self.parent.w = v

    @property
    def wb(self):
        return self.parent.wb

    @wb.setter
    def wb(self, v):
        self.parent.wb = v

    @property
    def r(self):
        return self.parent.r

    @r.setter
    def r(self, v):
        self.parent.r = v

    @property
    def pr(self):
        return self.parent.pr

    @pr.setter
    def pr(self, v):
        self.parent.pr = v

    @property
    def psum(self):
        return self.parent.psum


def kc_inputs(x_tok, mixT, W, pre, glu_in=None):
    d = dict(x=np.ascontiguousarray(x_tok), mixT=np.ascontiguousarray(mixT), w_out=W[pre + "w_out"],
             lnp=np.stack([W[pre + "ln1_g"], W[pre + "ln1_b"], W[pre + "ln2_g"], W[pre + "ln2_b"]]).astype(np.float32),
             w1=W[pre + "mlp_w1"], w2=W[pre + "mlp_w2"], ident=np.eye(128, dtype=np.float32))
    if glu_in is not None:
        d["yT"] = np.ascontiguousarray(glu_in)
        d["glu_w"] = W["l1_glu_w"]
        d["glu_b"] = np.ascontiguousarray(W["l1_glu_b"].reshape(4, 128).T)
    return d


L1_IN = 2632


def build_ka1(ngroups=8):
    nc = _newnc()
    ntok = ngroups * 512
    xT = _din(nc, "xT", [D, ntok])
    w_in = _din(nc, "w_in", [D, L1_IN])
    o_dq = _dout(nc, "o_dq", [ntok, 512], BF16)
    o_dk = _dout(nc, "o_dk", [ntok, 512], BF16)
    o_dv = _dout(nc, "o_dv", [ntok, 512], BF16)
    o_qi = _dout(nc, "o_qi", [ntok, 512], BF16)
    o_ki = _dout(nc, "o_ki", [ntok, 64], BF16)
    o_sg = _dout(nc, "o_sg", [ntok, 8], F32)
    o_u = _dout(nc, "o_u", [ntok, 512], F32)
    p = Prog(nc)
    winb = p.sb("winb", [128, 8, L1_IN], BF16)
    wst = [p.sb("wst%d" % i, [128, L1_IN], F32) for i in range(2)]
    w_in_v = w_in.rearrange("(c p) n -> p c n", p=128)
    for c in range(8):
        st = wst[c % 2]
        p.dma(st[:], w_in_v[:, c, :], writes=[st])
        p.op(("pool", "dve")[c % 2], lambda e: e.tensor_copy(winb[:, c, :], st[:]), [st], [winb], add=True)
    xs = [p.sb("xs%d" % i, [128, 8, 512], F32) for i in range(2)]
    xb = [p.sb("xb%d" % i, [128, 8, 512], BF16) for i in range(2)]
    hps = [p.ps("hps%d" % i, [128, 512]) for i in range(6)]
    NB = 2
    T = {}
    for nm, w, dt in (("dq", 512, BF16), ("dk", 512, BF16), ("dv", 512, BF16), ("qi", 512, BF16),
                      ("ki", 64, BF16), ("sg", 8, F32), ("u", 512, F32)):
        T[nm] = [p.sb("%s_t%d" % (nm, i), [128, w], dt) for i in range(NB)]
    aw = p.sb("aw", [128, 8], F32)
    xT_v = xT.rearrange("(c p) t -> p c t", p=128)
    for g in range(ngroups):
        xsg, xbg = xs[g % 2], xb[g % 2]
        for c in range(8):
            p.dma(xsg[:, c, :], xT_v[:, c, g * 512:(g + 1) * 512], writes=[xsg], add=(c > 0))
        for c in range(8):
            p.op(("pool", "dve")[c % 2], lambda e: e.tensor_copy(xbg[:, c, :], xsg[:, c, :]), [xsg], [xbg], add=(c > 0))
        for s in range(4):
            sub = g * 4 + s
            b = sub % NB
            tsl = slice(s * 128, (s + 1) * 128)
            for bi in range(6):
                n0 = bi * 512
                n = min(512, L1_IN - n0)
                for c in range(8):
                    p.op("pe", lambda e: e.matmul(hps[bi][:, 0:n], xbg[:, c, tsl], winb[:, c, n0:n0 + n],
                                                  start=(c == 0), stop=(c == 7)), [xbg, winb], [hps[bi]])
            p.op("act", lambda e: e.activation(T["dq"][b][:], hps[0][:], AF.Copy, scale=0.125), [hps[0]], [T["dq"][b]])
            p.op("dve", lambda e: e.tensor_copy(T["dk"][b][:], hps[1][:]), [hps[1]], [T["dk"][b]])
            p.op("act", lambda e: e.copy(T["dv"][b][:], hps[2][:]), [hps[2]], [T["dv"][b]])
            p.op("act", lambda e: e.activation(aw[:], hps[4][:, 64:72], AF.Abs, scale=(8 ** -0.5) * 0.125), [hps[4]], [aw])
            p.op("act", lambda e: e.activation(T["sg"][b][:], hps[4][:, 64:72], AF.Sign), [hps[4]], [T["sg"][b]])
            p.op("act", lambda e: e.copy(T["ki"][b][:], hps[4][:, 0:64]), [hps[4]], [T["ki"][b]])
            p.op("act", lambda e: e.copy(T["u"][b][:, 0:440], hps[4][:, 72:512]), [hps[4]], [T["u"][b]])
            p.op("dve", lambda e: e.tensor_copy(T["u"][b][:, 440:512], hps[5][:, 0:72]), [hps[5]], [T["u"][b]], add=True)
            p.op("dve", lambda e: e.tensor_tensor(T["qi"][b][:].rearrange("p (h d) -> p h d", h=8),
                                                  hps[3][:].rearrange("p (h d) -> p h d", h=8),
                                                  aw[:].unsqueeze(2).to_broadcast([128, 8, 64]), ALU.mult),
                 [hps[3], aw], [T["qi"][b]])
            r0 = sub * 128
            for nm, o in (("dq", o_dq), ("dk", o_dk), ("dv", o_dv), ("qi", o_qi), ("ki", o_ki), ("sg", o_sg), ("u", o_u)):
                p.dma(o[r0:r0 + 128, :], T[nm][b][:], reads=[T[nm][b]])
    p.finish()
    print("ka1 nops", p.nops, "n_inst", p.n_inst)
    return nc


def build_gla(nchunks=128):
    nc = _newnc()
    S = nchunks * 64
    g2 = _din(nc, "g2", [S, 128])
    qT2 = _din(nc, "qT2", [64, 2, S], BF16)
    kT2 = _din(nc, "kT2", [64, 2, S], BF16)
    k2 = _din(nc, "k2", [S, 128], BF16)
    v2 = _din(nc, "v2", [S, 256], BF16)
    sr2 = _din(nc, "sr2", [S, 256])
    gain = _din(nc, "gain", [1, 256])
    tri = _din(nc, "tri", [64, 64])
    upp = _din(nc, "upp", [64, 64])
    o_gla = _dout(nc, "o_gla", [S, 256])
    p = Prog(nc)
    Tri = p.sb("Tri", [64, 64], F32)
    Upp = p.sb("Upp", [64, 64], F32)
    TriM = p.sb("TriM", [64, 2, 64], F32)
    gbc = p.sb("gbc", [64, 256], F32)
    p.dma(Tri[:], tri[:, :], writes=[Tri])
    p.dma(Upp[:], upp[:, :], writes=[Upp])
    p.dma(gbc[:], gain[0:1, :].to_broadcast([64, 256]), writes=[gbc])
    for h in range(2):
        p.op("dve", lambda e: e.tensor_copy(TriM[:, h, :], Tri[:]), [Tri], [TriM], add=(h > 0))
    St = p.sb("St", [64, 2, 128], F32)
    Sb = [p.sb("Sb%d" % i, [64, 2, 128], BF16) for i in range(2)]
    p.op("dve", lambda e: e.memset(St[:], 0.0), [], [St])
    p.op("dve", lambda e: e.memset(Sb[0][:], 0.0), [], [Sb[0]])
    NBLK = 2
    gB = [p.sb("gB%d" % i, [64, 8, 128], F32) for i in range(NBLK)]
    kB = [p.sb("kB%d" % i, [64, 8, 128], BF16) for i in range(NBLK)]
    vB = [p.sb("vB%d" % i, [64, 8, 256], BF16) for i in range(NBLK)]
    sB = [p.sb("sB%d" % i, [64, 8, 256], F32) for i in range(NBLK)]
    qTB = [p.sb("qTB%d" % i, [64, 2, 512], BF16) for i in range(NBLK)]
    kTB = [p.sb("kTB%d" % i, [64, 2, 512], BF16) for i in range(NBLK)]
    oB = [p.sb("oB%d" % i, [64, 8, 256], F32) for i in range(NBLK)]
    E1 = [p.sb("E1_%d" % i, [64, 2, 64], F32) for i in range(2)]
    E2 = [p.sb("E2_%d" % i, [64, 2, 64], F32) for i in range(2)]
    E3 = [p.sb("E3_%d" % i, [64, 128], F32) for i in range(2)]
    qd = [p.sb("qd%d" % i, [64, 2, 64], BF16) for i in range(2)]
    ki = [p.sb("ki%d" % i, [64, 2, 64], BF16) for i in range(2)]
    KE = [p.sb("KE%d" % i, [64, 128], BF16) for i in range(2)]
    attm = [p.sb("attm%d" % i, [64, 2, 64], BF16) for i in range(2)]
    GS = [p.sb("GS%d" % i, [64, 256], F32) for i in range(2)]
    junk = p.sb("junk", [64, 128], F32)
    ss = [p.sb("ss%d" % i, [64, 2], F32) for i in range(2)]
    cum_ps = p.ps("cum_ps", [128, 512])
    rc_ps = p.ps("rc_ps", [128, 512])
    att_ps = p.ps("att_ps", [128, 512])
    kv_ps = p.ps("kv_ps", [128, 512])
    o_ps = [p.ps("o_ps%d" % i, [128, 512]) for i in range(2)]
    g_v = g2.rearrange("(n t) c -> t n c", t=64)
    k_v = k2.rearrange("(n t) c -> t n c", t=64)
    v_v = v2.rearrange("(n t) c -> t n c", t=64)
    s_v = sr2.rearrange("(n t) c -> t n c", t=64)
    o_v = o_gla.rearrange("(n t) c -> t n c", t=64)
    nblk = nchunks // 8
    f2 = lambda ap: ap.rearrange("p h i -> p (h i)")
    for blk in range(nblk):
        bb = blk % NBLK
        ns = slice(blk * 8, (blk + 1) * 8)
        ts_ = slice(blk * 512, (blk + 1) * 512)
        p.dma(gB[bb][:], g_v[:, ns, :], writes=[gB[bb]])
        p.dma(kB[bb][:], k_v[:, ns, :], writes=[kB[bb]])
        p.dma(vB[bb][:], v_v[:, ns, :], writes=[vB[bb]])
        p.dma(sB[bb][:], s_v[:, ns, :], writes=[sB[bb]])
        p.dma(qTB[bb][:], qT2[:, :, ts_], writes=[qTB[bb]])
        p.dma(kTB[bb][:], kT2[:, :, ts_], writes=[kTB[bb]])
        for ci in range(8):
            n = blk * 8 + ci
            a = n % 2
            cs = slice(ci * 64, (ci + 1) * 64)
            for h in range(2):
                p.op("pe", lambda e: e.matmul(cum_ps[0:64, h * 64:(h + 1) * 64], gB[bb][:, ci, h * 64:(h + 1) * 64], Tri[:],
                                              start=True, stop=True), [gB[bb], Tri], [cum_ps])
            p.op("pe", lambda e: e.matmul(rc_ps[0:64, 0:128], Upp[:], gB[bb][:, ci, :], start=True, stop=True), [gB[bb], Upp], [rc_ps])
            p.op("act", lambda e: e.activation(f2(E1[a][:]), cum_ps[0:64, 0:128], AF.Exp), [cum_ps], [E1[a]])
            p.op("act", lambda e: e.activation(f2(E2[a][:]), cum_ps[0:64, 0:128], AF.Exp, scale=-1.0), [cum_ps], [E2[a]])
            p.op("act", lambda e: e.activation(E3[a][:], rc_ps[0:64, 0:128], AF.Exp), [rc_ps], [E3[a]])
            p.op("dve", lambda e: e.tensor_mul(qd[a][:], qTB[bb][:, :, cs], E1[a][:]), [qTB[bb], E1[a]], [qd[a]])
            p.op("pool", lambda e: e.tensor_mul(ki[a][:], kTB[bb][:, :, cs], E2[a][:]), [kTB[bb], E2[a]], [ki[a]])
            p.op("dve", lambda e: e.tensor_mul(KE[a][:], kB[bb][:, ci, :], E3[a][:]), [kB[bb], E3[a]], [KE[a]])
            for h in range(2):
                p.op("pe", lambda e: e.matmul(att_ps[0:64, h * 64:(h + 1) * 64], ki[a][:, h, :], qd[a][:, h, :], start=True, stop=True),
                     [ki[a], qd[a]], [att_ps])
            p.op("dve", lambda e: e.tensor_mul(f2(attm[a][:]), att_ps[0:64, 0:128], f2(TriM[:])), [att_ps, TriM], [attm[a]])
            for h in range(2):
                p.op("pe", lambda e: e.matmul(kv_ps[0:64, h * 128:(h + 1) * 128], KE[a][:, h * 64:(h + 1) * 64],
                                              vB[bb][:, ci, h * 128:(h + 1) * 128], start=True, stop=True), [KE[a], vB[bb]], [kv_ps])
            op_ = o_ps[a]
            for h in range(2):
                p.op("pe", lambda e: e.matmul(op_[0:64, h * 128:(h + 1) * 128], attm[a][:, h, :], vB[bb][:, ci, h * 128:(h + 1) * 128],
                                              start=True, stop=False), [attm[a], vB[bb]], [op_])
                p.op("pe", lambda e: e.matmul(op_[0:64, h * 128:(h + 1) * 128], qd[a][:, h, :], Sb[a][:, h, :],
                                              start=False, stop=True), [qd[a], Sb[a]], [op_])
            for h in range(2):
                p.op("dve", lambda e: e.scalar_tensor_tensor(St[:, h, :], St[:, h, :], E1[a][:, h, 63:64],
                                                              kv_ps[0:64, h * 128:(h + 1) * 128], ALU.mult, ALU.add),
                     [St, E1[a], kv_ps], [St])
            p.op("act", lambda e: e.copy(Sb[1 - a][:], St[:]), [St], [Sb[1 - a]])
            for h in range(2):
                p.op("act", lambda e: e.activation(junk[:], op_[0:64, h * 128:(h + 1) * 128], AF.Square, accum_out=ss[a][:, h:h + 1]),
                     [op_], [junk, ss[a]], add=(h > 0))
            p.op("dve", lambda e: e.tensor_scalar(ss[a][:], ss[a][:], 1.0 / 128, LN_EPS, ALU.mult, ALU.add), [ss[a]], [ss[a]])
            p.op("act", lambda e: e.sqrt(ss[a][:], ss[a][:]), [ss[a]], [ss[a]])
            p.op("dve", lambda e: e.reciprocal(ss[a][:], ss[a][:]), [ss[a]], [ss[a]])
            p.op("pool", lambda e: e.tensor_mul(GS[a][:], gbc[:], sB[bb][:, ci, :]), [gbc, sB[bb]], [GS[a]])
            for h in range(2):
                p.op("dve", lambda e: e.scalar_tensor_tensor(oB[bb][:, ci, h * 128:(h + 1) * 128], op_[0:64, h * 128:(h + 1) * 128],
                                                              ss[a][:, h:h + 1], GS[a][:, h * 128:(h + 1) * 128], ALU.mult, ALU.mult),
                     [op_, ss[a], GS[a]], [oB[bb]], add=True)
        p.dma(o_v[:, ns, :], oB[bb][:], reads=[oB[bb]])
    p.finish()
    print("gla nops", p.nops, "n_inst", p.n_inst)
    return nc


def gla_consts():
    t = np.arange(64)
    tri = (t[:, None] <= t[None, :]).astype(np.float32)
    upp = (t[:, None] > t[None, :]).astype(np.float32)
    return tri, upp


def qgroups(ng, j):
    sel = (0, 3) if j == 0 else (1, 2)
    return [g for g in range(ng) if (g % 4) in sel]


def build_mla(ng=16, nheads=8):
    nc = _newnc()
    S = ng * 512
    nlg = ng // 2
    nq = nlg * 512
    qT = _din(nc, "qT", [nheads, 96, nq], BF16)
    kT = _din(nc, "kT", [nheads, 96, S], BF16)
    vP = _din(nc, "vP", [nheads, 128, S // 128, 64], BF16)
    dmask = _din(nc, "dmask", [2, 8, 128, 512], BF16)
    o_T = _dout(nc, "o_T", [nheads, 64, nq])
    p = Prog(nc)
    masks = p.sb("masks", [128, 2, 8, 512], BF16)
    for a_ in range(2):
        p.dma(masks[:, a_, :, :], dmask[a_].rearrange("r p f -> p r f"), writes=[masks], add=(a_ > 0))
    Sel = p.sb("Sel", [65, 64], F32)
    R = p.sb("R", [65, 512], F32)
    p.op("dve", lambda e: e.memset(Sel[:], 0.0), [], [Sel])
    p.op("dve", lambda e: e.memset(Sel[64:65, :], 1.0), [], [Sel])
    p.op("dve", lambda e: e.memset(R[:], 0.0), [], [R])
    nkb_tot = S // 128
    kTs = [p.sb("kTs%d" % i, [96, S], BF16) for i in range(2)]
    qTs = [p.sb("qTs%d" % i, [96, nq], BF16) for i in range(2)]
    Va = [p.sb("Va%d" % i, [128, nkb_tot, 65], BF16) for i in range(2)]
    for i in range(2):
        p.op("pool", lambda e: e.memset(Va[i][:, :, 64:65], 1.0), [], [Va[i]])
    NST = 3
    st_ps = [p.ps("st_ps%d" % i, [128, 512]) for i in range(NST)]
    pt = [p.sb("pt%d" % i, [128, 512], BF16) for i in range(NST)]
    ptf = [p.sb("ptf%d" % i, [128, 512], BF16) for i in range(2)]
    ot_ps = [p.ps("ot_ps%d" % i, [128, 512]) for i in range(2)]
    bc_ps = p.ps("bc_ps", [128, 512])
    OTs = [p.sb("OTs%d" % i, [65, 512], F32) for i in range(2)]
    oo = [p.sb("oo%d" % i, [64, 512], F32) for i in range(2)]
    it = 0
    ep = 0
    for h in range(nheads):
        hb = h % 2
        p.dma(kTs[hb][:], kT[h, :, :], writes=[kTs[hb]])
        p.dma(qTs[hb][:], qT[h, :, :], writes=[qTs[hb]])
        p.dma(Va[hb][:, :, 0:64], vP[h, :, :, :], writes=[Va[hb]], add=True)
        for gi in range(nlg):
            nkb = 8 * gi + 8
            otp = ot_ps[ep % 2]
            for kb in range(nkb):
                sp_, pt_ = st_ps[it % NST], pt[it % NST]
                p.op("pe", lambda e: e.matmul(sp_[:], kTs[hb][:, kb * 128:(kb + 1) * 128], qTs[hb][:, gi * 512:(gi + 1) * 512],
                                              start=True, stop=True), [kTs[hb], qTs[hb]], [sp_])
                if kb < 8 * gi:
                    p.op("act", lambda e: e.activation(pt_[:], sp_[:], AF.Exp), [sp_], [pt_])
                else:
                    r = kb - 8 * gi
                    pf = ptf[it % 2]
                    p.op("act", lambda e: e.activation(pf[:], sp_[:], AF.Exp), [sp_], [pf])
                    p.op(("dve", "pool")[r % 2], lambda e: e.tensor_mul(pt_[:], pf[:], masks[:, gi % 2, r, :]), [pf, masks], [pt_])
                p.op("pe", lambda e: e.matmul(otp[0:65, :], Va[hb][:, kb, :], pt_[:], start=(kb == 0), stop=(kb == nkb - 1)),
                     [Va[hb], pt_], [otp])
                it += 1
            ots, o_ = OTs[ep % 2], oo[ep % 2]
            p.op("act", lambda e: e.copy(ots[:], otp[0:65, :]), [otp], [ots])
            p.op("dve", lambda e: e.reciprocal(R[64:65, :], ots[64:65, :]), [ots], [R])
            p.op("pe", lambda e: e.matmul(bc_ps[0:64, :], Sel[:], R[:], start=True, stop=True), [Sel, R], [bc_ps])
            p.op("dve", lambda e: e.tensor_mul(o_[:], ots[0:64, :], bc_ps[0:64, :]), [ots, bc_ps], [o_])
            p.dma(o_T[h, :, gi * 512:(gi + 1) * 512], o_[:], reads=[o_])
            ep += 1
    p.finish()
    print("mla nops", p.nops, "n_inst", p.n_inst)
    return nc


def diag_masks(j):
    pp = np.arange(128)[:, None]
    f = np.arange(512)[None, :]
    dg = np.stack([(128 * r + pp <= f) for r in range(4)]).astype(np.float32)
    low = np.concatenate([dg, np.zeros_like(dg)])
    high = np.concatenate([np.ones_like(dg), dg])
    m = np.stack([low, high]) if j == 0 else np.stack([high, low])
    return m.astype(NPBF)


def mla_inputs(qf, kf, vm, j, ng):
    S = ng * 512
    groups = qgroups(ng, j)
    qsel = np.concatenate([qf[g * 512:(g + 1) * 512] for g in groups], 0)
    return dict(qT=np.ascontiguousarray(qsel.transpose(1, 2, 0)), kT=np.ascontiguousarray(kf.transpose(1, 2, 0)),
                vP=np.ascontiguousarray(vm.reshape(S // 128, 128, 8, 64).transpose(2, 1, 0, 3)), dmask=diag_masks(j))


S5T = 512


def build_s5(nchunks=16):
    nc = _newnc()
    S = nchunks * S5T
    uT = _din(nc, "uT", [256, S])
    a_re = _din(nc, "a_re", [128, 8])
    a_im = _din(nc, "a_im", [128, 8])
    lstep = _din(nc, "lstep", [128, 8])
    b_re = _din(nc, "b_re", [8, 128, 16])
    b_im = _din(nc, "b_im", [8, 128, 16])
    c_re = _din(nc, "c_re", [8, 128, 16])
    c_im = _din(nc, "c_im", [8, 128, 16])
    dsk = _din(nc, "dsk", [128, 2])
    tvals = _din(nc, "tvals", [1, S5T])
    ident = _din(nc, "ident", [128, 128])
    o_y = _dout(nc, "o_y", [256, S])
    p = Prog(nc)
    T = S5T
    def sm(name, w=8):
        return p.sb(name, [128, w], F32)
    are, aim, lst, dt_, lre, th, thc = sm("are"), sm("aim"), sm("lst"), sm("dt_"), sm("lre"), sm("th"), sm("thc")
    p.dma(are[:], a_re[:, :], writes=[are])
    p.dma(aim[:], a_im[:, :], writes=[aim])
    p.dma(lst[:], lstep[:, :], writes=[lst])
    dk = sm("dk", 2)
    p.dma(dk[:], dsk[:, :], writes=[dk])
    idf = p.sb("idf", [128, 128], F32)
    idb = p.sb("idb", [128, 128], BF16)
    p.dma(idf[:], ident[:, :], writes=[idf])
    p.op("dve", lambda e: e.tensor_copy(idb[:], idf[:]), [idf], [idb])
    p.op("act", lambda e: e.activation(dt_[:], lst[:], AF.Exp), [lst], [dt_])
    p.op("dve", lambda e: e.tensor_scalar(lre[:], are[:], -1e-4, None, ALU.min), [are], [lre])
    rmag = sm("rmag")
    tmp = sm("tmp")
    p.op("dve", lambda e: e.tensor_mul(tmp[:], lre[:], dt_[:]), [lre, dt_], [tmp])
    p.op("act", lambda e: e.activation(rmag[:], tmp[:], AF.Exp), [tmp], [rmag])
    p.op("dve", lambda e: e.tensor_mul(th[:], aim[:], dt_[:]), [aim, dt_], [th])
    p.op("dve", lambda e: e.tensor_scalar(thc[:], th[:], 1.0 / (2 * np.pi), None, ALU.mult), [th], [thc])

    def sincos(src_cyc, w, nm):
        t2 = p.sb(nm + "t2", [128, 2, w], F32)
        ti = p.sb(nm + "ti", [128, 2, w], I32)
        tf = p.sb(nm + "tf", [128, 2, w], F32)
        sc = p.sb(nm + "sc", [128, 2, w], F32)
        p.op("dve", lambda e: e.tensor_copy(t2[:, 0, :], src_cyc[:]), [src_cyc], [t2])
        p.op("dve", lambda e: e.tensor_scalar(t2[:, 1, :], src_cyc[:], 0.25, None, ALU.add), [src_cyc], [t2], add=True)
        p.op("dve", lambda e: e.tensor_copy(ti[:], t2[:]), [t2], [ti])
        p.op("dve", lambda e: e.tensor_copy(tf[:], ti[:]), [ti], [tf])
        p.op("dve", lambda e: e.tensor_sub(t2[:], t2[:], tf[:]), [t2, tf], [t2])
        p.op("act", lambda e: e.activation(sc[:], t2[:], AF.Sin, scale=6.28318), [t2], [sc])
        return sc

    sc1 = sincos(thc, 8, "p1")
    abr, abi = sm("abr"), sm("abi")
    p.op("dve", lambda e: e.tensor_mul(abr[:], rmag[:], sc1[:, 1, :]), [rmag, sc1], [abr])
    p.op("dve", lambda e: e.tensor_mul(abi[:], rmag[:], sc1[:, 0, :]), [rmag, sc1], [abi])
    nr, den, t1, t2_, cre, cim, ncim = sm("nr"), sm("den"), sm("t1"), sm("t2_"), sm("cre"), sm("cim"), sm("ncim")
    p.op("dve", lambda e: e.tensor_scalar(nr[:], abr[:], -1.0, None, ALU.add), [abr], [nr])
    p.op("dve", lambda e: e.tensor_mul(den[:], lre[:], lre[:]), [lre], [den])
    p.op("dve", lambda e: e.tensor_mul(t1[:], aim[:], aim[:]), [aim], [t1])
    p.op("dve", lambda e: e.tensor_add(den[:], den[:], t1[:]), [den, t1], [den])
    p.op("dve", lambda e: e.reciprocal(den[:], den[:]), [den], [den])
    p.op("dve", lambda e: e.tensor_mul(t1[:], nr[:], lre[:]), [nr, lre], [t1])
    p.op("dve", lambda e: e.tensor_mul(t2_[:], abi[:], aim[:]), [abi, aim], [t2_])
    p.op("dve", lambda e: e.tensor_add(t1[:], t1[:], t2_[:]), [t1, t2_], [t1])
    p.op("dve", lambda e: e.tensor_mul(cre[:], t1[:], den[:]), [t1, den], [cre])
    p.op("dve", lambda e: e.tensor_mul(t1[:], abi[:], lre[:]), [abi, lre], [t1])
    p.op("dve", lambda e: e.tensor_mul(t2_[:], nr[:], aim[:]), [nr, aim], [t2_])
    p.op("dve", lambda e: e.tensor_sub(t1[:], t1[:], t2_[:]), [t1, t2_], [t1])
    p.op("dve", lambda e: e.tensor_mul(cim[:], t1[:], den[:]), [t1, den], [cim])
    p.op("dve", lambda e: e.tensor_scalar(ncim[:], cim[:], -1.0, None, ALU.mult), [cim], [ncim])

    BTr = p.sb("BTr", [128, 8, 128], BF16)
    BTi = p.sb("BTi", [128, 8, 128], BF16)
    CPr = p.sb("CPr", [128, 8, 128], BF16)
    CPi = p.sb("CPi", [128, 8, 128], BF16)
    BP = [p.sb("BP%d" % i, [128, 128], BF16) for i in range(2)]
    bre_t = p.sb("bre_t", [128, 8, 16], F32)
    bim_t = p.sb("bim_t", [128, 8, 16], F32)
    cre_t = p.sb("cre_t", [128, 8, 16], F32)
    cim_t = p.sb("cim_t", [128, 8, 16], F32)
    for src, dst in ((b_re, bre_t), (b_im, bim_t), (c_re, cre_t), (c_im, cim_t)):
        p.dma(dst[:], src.rearrange("i p c -> p i c"), writes=[dst])
    p.op("pool", lambda e: e.memset(CPr[:], 0.0), [], [CPr])
    p.op("pool", lambda e: e.memset(CPi[:], 0.0), [], [CPi])
    tb = p.sb("tb", [128, 16], F32)
    tr_ps = p.ps("tr_ps", [128, 8, 128], BF16)
    for i in range(8):
        c0 = (i % 4) * 32
        for (half, ps_) in ((0, slice(0, 64)), (1, slice(64, 128))):
            cs = slice(c0 + half * 16, c0 + half * 16 + 16)
            p.op("dve", lambda e: e.tensor_copy(CPr[ps_, i, cs], cre_t[ps_, i, :]), [cre_t], [CPr], add=True)
            p.op("dve", lambda e: e.tensor_scalar(CPi[ps_, i, cs], cim_t[ps_, i, :], -1.0, None, ALU.mult), [cim_t], [CPi], add=True)
        for which, BT in ((0, BTr), (1, BTi)):
            bp = BP[which]
            p.op("pool", lambda e: e.memset(bp[:], 0.0), [], [bp])
            if which == 0:
                p.op("dve", lambda e: e.tensor_scalar(tb[:], bre_t[:, i, :], cre[:, i:i + 1], None, ALU.mult), [bre_t, cre], [tb])
                src2, sc2 = bim_t, ncim
            else:
                p.op("dve", lambda e: e.tensor_scalar(tb[:], bim_t[:, i, :], cre[:, i:i + 1], None, ALU.mult), [bim_t, cre], [tb])
                src2, sc2 = bre_t, cim
            for (half, ps_) in ((0, slice(0, 64)), (1, slice(64, 128))):
                cs = slice(c0 + half * 16, c0 + half * 16 + 16)
                p.op("dve", lambda e: e.scalar_tensor_tensor(bp[ps_, cs], src2[ps_, i, :], sc2[ps_, i:i + 1], tb[ps_, :], ALU.mult, ALU.add),
                     [src2, sc2, tb], [bp], add=True)
            p.op("pe", lambda e: e.transpose(tr_ps[:, i, :], bp[:], idb[:]), [bp, idb], [tr_ps])
            p.op("act", lambda e: e.copy(BT[:, i, :], tr_ps[:, i, :]), [tr_ps], [BT], add=True)

    tv = p.sb("tv", [128, T], F32)
    p.dma(tv[:], tvals[0:1, :].to_broadcast([128, T]), writes=[tv])
    CS = p.sb("CS", [128, 8, 2, T], F32)
    Rt = p.sb("Rt", [128, 8, T], F32)
    ang = p.sb("ang", [128, T], F32)
    t2 = p.sb("rt2", [128, 2, T], F32)
    ti = p.sb("rti", [128, 2, T], I32)
    tf = p.sb("rtf", [128, 2, T], F32)
    for i in range(8):
        p.op("dve", lambda e: e.tensor_scalar(t2[:, 0, :], tv[:], thc[:, i:i + 1], None, ALU.mult), [tv, thc], [t2])
        p.op("dve", lambda e: e.tensor_scalar(t2[:, 1, :], t2[:, 0, :], 0.25, None, ALU.add), [t2], [t2])
        p.op("dve", lambda e: e.tensor_copy(ti[:], t2[:]), [t2], [ti])
        p.op("dve", lambda e: e.tensor_copy(tf[:], ti[:]), [ti], [tf])
        p.op("dve", lambda e: e.tensor_sub(t2[:], t2[:], tf[:]), [t2, tf], [t2])
        p.op("act", lambda e: e.activation(CS[:, i, :, :], t2[:], AF.Sin, scale=6.28318), [t2], [CS], add=True)
        p.op("pool", lambda e: e.memset(Rt[:, i, :], 1.0), [], [Rt], add=True)
        p.op("pool", lambda e: e.tensor_scalar(Rt[:, i, :], Rt[:, i, :], rmag[:, i:i + 1], None, ALU.mult), [Rt, rmag], [Rt], add=True)

    carry = p.sb("carry", [128, 8, 2], F32)
    p.op("dve", lambda e: e.memset(carry[:], 0.0), [], [carry])
    uf = [p.sb("uf%d" % i, [128, 2, T], F32) for i in range(2)]
    ub = [p.sb("ub%d" % i, [128, 2, T], BF16) for i in range(2)]
    NW = 2
    W = {nm: [p.sb("%s%d" % (nm, i), [128, T], F32) for i in range(NW)]
         for nm in ("br", "bi", "m1", "m2", "m3", "m4", "wr", "wi", "sr", "si")}
    Xr = [p.sb("Xr%d" % i, [128, T], BF16) for i in range(4)]
    Xi = [p.sb("Xi%d" % i, [128, T], BF16) for i in range(4)]
    cz = p.sb("cz", [128, 4], F32)
    yo = [p.sb("yo%d" % i, [128, T], F32) for i in range(2)]
    bu_ps = [p.ps("bu_ps%d" % i, [128, 512]) for i in range(4)]
    y_ps = [p.ps("y_ps%d" % i, [128, 512]) for i in range(2)]
    u_v = uT.rearrange("(c p) t -> p c t", p=128)
    o_v = o_y.rearrange("(c p) t -> p c t", p=128)
    unit = 0
    for ch in range(nchunks):
        tsl = slice(ch * T, (ch + 1) * T)
        ufc, ubc = uf[ch % 2], ub[ch % 2]
        p.dma(ufc[:], u_v[:, :, tsl], writes=[ufc])
        p.op("pool", lambda e: e.tensor_copy(ubc[:], ufc[:]), [ufc], [ubc])
        for ct in range(2):
            yp = y_ps[ct]
            for li in range(4):
                i = ct * 4 + li
                w = unit % NW
                unit += 1
                bpr, bpi = bu_ps[(2 * unit) % 4], bu_ps[(2 * unit + 1) % 4]
                p.op("pe", lambda e: e.matmul(bpr[:], BTr[:, i, :], ubc[:, ct, :], start=True, stop=True), [BTr, ubc], [bpr])
                p.op("pe", lambda e: e.matmul(bpi[:], BTi[:, i, :], ubc[:, ct, :], start=True, stop=True), [BTi, ubc], [bpi])
                br, bi_, m1, m2, m3, m4 = W["br"][w], W["bi"][w], W["m1"][w], W["m2"][w], W["m3"][w], W["m4"][w]
                wr, wi_, sr, si = W["wr"][w], W["wi"][w], W["sr"][w], W["si"][w]
                Sn, Cs = CS[:, i, 0, :], CS[:, i, 1, :]
                p.op("act", lambda e: e.copy(br[:], bpr[:]), [bpr], [br])
                p.op("act", lambda e: e.copy(bi_[:], bpi[:]), [bpi], [bi_])
                p.op("dve", lambda e: e.tensor_mul(m1[:], br[:], Cs), [br, CS], [m1])
                p.op("pool", lambda e: e.tensor_mul(m2[:], bi_[:], Sn), [bi_, CS], [m2])
                p.op("dve", lambda e: e.tensor_mul(m3[:], bi_[:], Cs), [bi_, CS], [m3])
                p.op("pool", lambda e: e.tensor_mul(m4[:], br[:], Sn), [br, CS], [m4])
                p.op("pool", lambda e: e.tensor_add(wr[:], m1[:], m2[:]), [m1, m2], [wr])
                p.op("pool", lambda e: e.tensor_sub(wi_[:], m3[:], m4[:]), [m3, m4], [wi_])
                p.op("dve", lambda e: e.tensor_tensor_scan(sr[:], Rt[:, i, :], wr[:], carry[:, i, 0:1], ALU.mult, ALU.add),
                     [Rt, wr, carry], [sr])
                p.op("dve", lambda e: e.tensor_tensor_scan(si[:], Rt[:, i, :], wi_[:], carry[:, i, 1:2], ALU.mult, ALU.add),
                     [Rt, wi_, carry], [si])
                p.op("dve", lambda e: e.tensor_mul(m1[:], sr[:], Cs), [sr, CS], [m1])
                p.op("pool", lambda e: e.tensor_mul(m2[:], si[:], Sn), [si, CS], [m2])
                p.op("dve", lambda e: e.tensor_mul(m3[:], sr[:], Sn), [sr, CS], [m3])
                p.op("pool", lambda e: e.tensor_mul(m4[:], si[:], Cs), [si, CS], [m4])
                xr, xi = Xr[li], Xi[li]
                p.op("dve", lambda e: e.tensor_sub(xr[:], m1[:], m2[:]), [m1, m2], [xr])
                p.op("pool", lambda e: e.tensor_add(xi[:], m3[:], m4[:]), [m3, m4], [xi])
                p.op("dve", lambda e: e.tensor_sub(carry[:, i, 0:1], m1[:, T - 1:T], m2[:, T - 1:T]), [m1, m2], [carry])
                p.op("dve", lambda e: e.tensor_add(carry[:, i, 1:2], m3[:, T - 1:T], m4[:, T - 1:T]), [m3, m4], [carry])
                p.op("pe", lambda e: e.matmul(yp[:], CPr[:, i, :], xr[:], start=(li == 0), stop=False), [CPr, xr], [yp])
                p.op("pe", lambda e: e.matmul(yp[:], CPi[:, i, :], xi[:], start=False, stop=(li == 3)), [CPi, xi], [yp])
            yo_ = yo[ct]
            p.op("dve", lambda e: e.scalar_tensor_tensor(yo_[:], ufc[:, ct, :], dk[:, ct:ct + 1], yp[:], ALU.mult, ALU.add),
                 [ufc, dk, yp], [yo_])
            p.dma(o_v[:, ct, tsl], yo_[:], reads=[yo_])
    p.finish()
    print("s5 nops", p.nops, "n_inst", p.n_inst, "sbuf left", nc.sbuf_bytes_remaining)
    return nc


def s5_inputs(uT, W, j):
    gs = slice(16 * j, 16 * j + 16)
    r8 = lambda a: np.ascontiguousarray(a.reshape(8, 128).T)
    return dict(uT=np.ascontiguousarray(uT), a_re=r8(W["l1_s5_a_re"][gs]), a_im=r8(W["l1_s5_a_im"][gs]),
                lstep=r8(np.repeat(W["l1_s5_log_step"][gs], 64)),
                b_re=np.ascontiguousarray(W["l1_s5_b_re"][gs].reshape(8, 128, 16)),
                b_im=np.ascontiguousarray(W["l1_s5_b_im"][gs].reshape(8, 128, 16)),
                c_re=np.ascontiguousarray(W["l1_s5_c_re"][gs].transpose(0, 2, 1).reshape(8, 128, 16)),
                c_im=np.ascontiguousarray(W["l1_s5_c_im"][gs].transpose(0, 2, 1).reshape(8, 128, 16)),
                dsk=np.ascontiguousarray(W["l1_s5_d"][256 * j:256 * j + 256].reshape(2, 128).T),
                tvals=np.arange(1, S5T + 1, dtype=np.float32)[None, :], ident=np.eye(128, dtype=np.float32))


DSA_K = 256
DSA_NIT = 22
BIG = 1.0e30


def build_dsa(ng=16, nit=DSA_NIT, nhg=4):
    nc = _newnc()
    S = ng * 512
    nlg = ng // 2
    QBs = [(i, r) for i in range(nlg) for r in range(4)]
    nqb = len(QBs)
    nq = nqb * 128
    qiT = _din(nc, "qiT", [8, 64, nq], BF16)
    kiT = _din(nc, "kiT", [64, S], BF16)
    sgn = _din(nc, "sgn", [128, nqb, 8])
    qT = _din(nc, "qT", [8, 64, nq], BF16)
    kT = _din(nc, "kT", [8, 64, S], BF16)
    vP = _din(nc, "vP", [8, 128, S // 128, 64], BF16)
    ident = _din(nc, "ident", [128, 128], BF16)
    cbig = _din(nc, "cbig", [2, 4, 128, 1024])
    p2row = _din(nc, "p2row", [2, nit])
    o_dsa = _dout(nc, "o_dsa", [nq, 512])
    mscr = nc.dram_tensor("mscr", [nqb, 128, S], BF16).ap()
    p = Prog(nc)
    idb = p.sb("idb", [128, 128], BF16)
    CBs = [p.sb("CB%d" % i, [128, 1024], F32) for i in range(2)]
    P2 = p.sb("P2", [128, 2, nit], F32)
    p.dma(idb[:], ident[:, :], writes=[idb])

    for i in range(2):
        p.dma(P2[:, i, :], p2row[i:i + 1, :].to_broadcast([128, nit]), writes=[P2], add=(i > 0))
    kis = p.sb("kis", [64, S], BF16)
    p.dma(kis[:], kiT[:, :], writes=[kis])
    sg = p.sb("sg", [128, nqb, 8], F32)
    p.dma(sg[:], sgn[:, :, :], writes=[sg])
    Score = p.sb("Score", [128, S], F32)
    Mj = p.sb("Mj", [128, S], BF16)
    qis = [p.sb("qis%d" % i, [64, 8, 128], BF16) for i in range(2)]
    Rh = [p.sb("Rh%d" % i, [128, 512], BF16) for i in range(8)]
    Rl = [p.sb("Rl%d" % i, [128, 512], BF16) for i in range(8)]
    Dsg = [p.sb("Dsg%d" % i, [128, 8, 128], BF16) for i in range(2)]
    l_ps = [p.ps("l_ps%d" % i, [128, 512]) for i in range(2)]
    sc_ps = p.ps("sc_ps", [128, 512])
    st = p.sb("st", [128, 8], F32)
    Wt = p.sb("Wt", [128, 2, nit], F32)
    mreg = [Buf(None, "mreg%d" % i) for i in range(nqb)]
    qi_v = qiT.rearrange("h d q -> d h q")
    for lb, (gi, rr) in enumerate(QBs):
        L = (2 * gi + 1) * 512 + (rr + 1) * 128
        nch = 2 * gi + 2
        qs = qis[lb % 2]
        dsg = Dsg[lb % 2]
        p.dma(qs[:], qi_v[:, :, lb * 128:(lb + 1) * 128], writes=[qs])
        for h in range(8):
            p.op(("dve", "pool")[h % 2], lambda e: e.tensor_scalar(dsg[:, h, :], idb[:], sg[:, lb, h:h + 1], None, ALU.mult),
                 [idb, sg], [dsg], add=(h > 0))
        for c in range(nch):
            w = 512 if c < nch - 1 else (rr + 1) * 128
            ks = slice(c * 512, c * 512 + w)
            for h in range(8):
                lp = l_ps[h % 2]
                p.op("pe", lambda e: e.matmul(lp[:, 0:w], qs[:, h, :], kis[:, ks], start=True, stop=True), [qs, kis], [lp])
                p.op("act", lambda e: e.activation(Rh[h][:, 0:w], lp[:, 0:w], AF.Relu), [lp], [Rh[h]])
                p.op("dve", lambda e: e.scalar_tensor_tensor(Rl[h][:, 0:w], lp[:, 0:w], 0.0, Rh[h][:, 0:w], ALU.max, ALU.subtract),
                     [lp, Rh[h]], [Rl[h]])
            for h in range(8):
                p.op("pe", lambda e: e.matmul(sc_ps[:, 0:w], dsg[:, h, :], Rh[h][:, 0:w], start=(h == 0), stop=False),
                     [dsg, Rh[h]], [sc_ps])
                p.op("pe", lambda e: e.matmul(sc_ps[:, 0:w], dsg[:, h, :], Rl[h][:, 0:w], start=False, stop=(h == 7)),
                     [dsg, Rl[h]], [sc_ps])
            p.op("act", lambda e: e.copy(Score[:, ks], sc_ps[:, 0:w]), [sc_ps], [Score], add=(c > 0))
        p.op("dve", lambda e: e.tensor_reduce(st[:, 0:1], Score[:, 0:L], AX.X, ALU.max), [Score], [st])
        p.op("dve", lambda e: e.tensor_reduce(st[:, 1:2], Score[:, 0:L], AX.X, ALU.min), [Score], [st])
        t0_ = 2 * gi * 512
        CB = CBs[lb % 2]
        p.dma(CB[:], cbig[gi % 2, rr, :, :], writes=[CB])
        p.op("dve", lambda e: e.tensor_tensor(Score[:, t0_:L], Score[:, t0_:L], CB[:, 0:L - t0_], ALU.min), [Score, CB], [Score])
        p.op("dve", lambda e: e.tensor_sub(st[:, 4:5], st[:, 0:1], st[:, 1:2]), [st], [st])
        p.op("dve", lambda e: e.tensor_scalar(st[:, 4:5], st[:, 4:5], 2.0, 0.5, ALU.add, ALU.mult), [st], [st])
        p.op("dve", lambda e: e.tensor_scalar(Wt[:, 0, :], P2[:, 0, :], st[:, 4:5], None, ALU.mult), [P2, st], [Wt])
        p.op("dve", lambda e: e.tensor_scalar(Wt[:, 1, :], P2[:, 1, :], st[:, 4:5], None, ALU.mult), [P2, st], [Wt])
        p.op("dve", lambda e: e.scalar_tensor_tensor(st[:, 2:3], st[:, 1:2], -1.0, st[:, 4:5], ALU.add, ALU.add), [st], [st])
        for k in range(nit):
            p.op("dve", lambda e: e.tensor_scalar(Mj[:, 0:L], Score[:, 0:L], st[:, 2:3], None, ALU.is_ge, ALU.add,
                                                   accum_out=st[:, 3:4]), [Score, st], [Mj, st])
            p.op("dve", lambda e: e.tensor_scalar(st[:, 4:5], st[:, 3:4], DSA_K - 0.5, Wt[:, 0, k:k + 1], ALU.is_ge, ALU.mult),
                 [st, Wt], [st])
            p.op("dve", lambda e: e.scalar_tensor_tensor(st[:, 2:3], st[:, 4:5], Wt[:, 1, k:k + 1], st[:, 2:3], ALU.subtract, ALU.add),
                 [st, Wt], [st])
        p.op("dve", lambda e: e.tensor_scalar(Mj[:, 0:L], Score[:, 0:L], st[:, 2:3], None, ALU.is_ge), [Score, st], [Mj])
        p.dma(mscr[lb, :, 0:L], Mj[:, 0:L], reads=[Mj], writes=[mreg[lb]])
    hpg = 8 // nhg
    kTs = p.sb("kTs", [64, hpg, S], BF16)
    qTs = p.sb("qTs", [64, hpg, nq], BF16)
    Va = p.sb("Va", [128, hpg, S // 128, 65], BF16)
    p.op("pool", lambda e: e.memset(Va[:, :, :, 64:65], 1.0), [], [Va])
    Mq0 = p.sb("Mq0", [128, S], BF16)
    Mq = [Mq0, Mq0]
    Pe = [p.sb("Pe%d" % i, [128, 512], BF16) for i in range(2)]
    Pm = [p.sb("Pm%d" % i, [128, 512], BF16) for i in range(2)]
    PT = [p.sb("PT%d" % i, [128, 4, 128], BF16) for i in range(2)]
    ot = [p.sb("ot%d" % i, [128, 64], F32) for i in range(2)]
    rc = [p.sb("rc%d" % i, [128, 1], F32) for i in range(2)]
    s_ps = [p.ps("s_ps%d" % i, [128, 512]) for i in range(2)]
    tp_ps = p.ps("tp_ps", [128, 4, 128], BF16)
    o_ps = [p.ps("o_ps%d" % i, [128, 512]) for i in range(2)]
    kT_v = kT.rearrange("h d s -> d h s")
    qT_v = qT.rearrange("h d q -> d h q")
    it = 0
    ep = 0
    mi = 0
    for hg in range(nhg):
        hs0 = hg * hpg
        p.dma(kTs[:], kT_v[:, hs0:hs0 + hpg, :], writes=[kTs])
        p.dma(qTs[:], qT_v[:, hs0:hs0 + hpg, :], writes=[qTs])
        for hh in range(hpg):
            p.dma(Va[:, hh, :, 0:64], vP[hs0 + hh, :, :, :], writes=[Va], add=True)
        for lb, (gi, rr) in enumerate(QBs):
            L = (2 * gi + 1) * 512 + (rr + 1) * 128
            nch = 2 * gi + 2
            mq = Mq[mi % 2]
            mi += 1
            p.dma(mq[:, 0:L], mscr[lb, :, 0:L], reads=[mreg[lb]], writes=[mq])
            for hh in range(hpg):
                h = hs0 + hh
                op_ = o_ps[ep % 2]
                for c in range(nch):
                    w = 512 if c < nch - 1 else (rr + 1) * 128
                    nk = w // 128
                    ks = slice(c * 512, c * 512 + w)
                    sp_ = s_ps[it % 2]
                    pe_, pm_, pt_ = Pe[it % 2], Pm[it % 2], PT[it % 2]
                    p.op("pe", lambda e: e.matmul(sp_[:, 0:w], qTs[:, hh, lb * 128:(lb + 1) * 128], kTs[:, hh, ks],
                                                  start=True, stop=True), [qTs, kTs], [sp_])
                    p.op("act", lambda e: e.activation(pe_[:, 0:w], sp_[:, 0:w], AF.Exp), [sp_], [pe_])
                    p.op(("dve", "pool")[it % 2], lambda e: e.tensor_mul(pm_[:, 0:w], pe_[:, 0:w], mq[:, ks]), [pe_, mq], [pm_])
                    for kk in range(nk):
                        p.op("pe", lambda e: e.transpose(tp_ps[:, kk, :], pm_[:, kk * 128:(kk + 1) * 128], idb[:]), [pm_, idb], [tp_ps])
                    if it % 2 == 0:
                        p.op("act", lambda e: e.copy(pt_[:, 0:nk, :], tp_ps[:, 0:nk, :]), [tp_ps], [pt_])
                    else:
                        p.op("dve", lambda e: e.tensor_copy(pt_[:, 0:nk, :], tp_ps[:, 0:nk, :]), [tp_ps], [pt_])
                    for kk in range(nk):
                        p.op("pe", lambda e: e.matmul(op_[:, 0:65], pt_[:, kk, :], Va[:, hh, c * 4 + kk, :],
                                                      start=(c == 0 and kk == 0), stop=(c == nch - 1 and kk == nk - 1)),
                             [pt_, Va], [op_])
                    it += 1
                o_, r_ = ot[ep % 2], rc[ep % 2]
                p.op("dve", lambda e: e.reciprocal(r_[:], op_[:, 64:65]), [op_], [r_])
                p.op("dve", lambda e: e.tensor_scalar(o_[:], op_[:, 0:64], r_[:, 0:1], None, ALU.mult), [op_, r_], [o_])
                p.dma(o_dsa[lb * 128:(lb + 1) * 128, h * 64:(h + 1) * 64], o_[:], reads=[o_])
                ep += 1
    p.finish()
    print("dsa nops", p.nops, "n_inst", p.n_inst, "sbuf left", nc.sbuf_bytes_remaining)
    return nc


def dsa_consts(nit, j):
    q = np.arange(128)[:, None]
    k = np.arange(128)[None, :]
    dg = np.where(k <= q, BIG, -BIG).astype(np.float32)
    pos = np.full((128, 128), BIG, np.float32)
    neg = -pos
    cb = np.zeros((2, 4, 128, 1024), np.float32)
    for r in range(4):
        low = [pos] * r + [dg] + [neg] * (3 - r) + [neg] * 4
        high = [pos] * 4 + [pos] * r + [dg] + [neg] * (3 - r)
        lo_, hi_ = np.concatenate(low, 1), np.concatenate(high, 1)
        cb[0, r], cb[1, r] = (lo_, hi_) if j == 0 else (hi_, lo_)
    wk = 0.5 ** np.arange(nit)
    bk = wk * 0.5
    bk[-1] = wk[-1]
    return cb, np.stack([wk, bk]).astype(np.float32)


def dsa_inputs(dq, dk, dv, qi, ki, sg, j, ng, nit=DSA_NIT):
    S = ng * 512
    groups = qgroups(ng, j)
    sel = np.concatenate([np.arange(g * 512, (g + 1) * 512) for g in groups])
    nqb = len(sel) // 128
    cb, p2 = dsa_consts(nit, j)
    T8 = lambda a: np.ascontiguousarray(a.reshape(a.shape[0], 8, 64).transpose(1, 2, 0))
    return dict(qiT=T8(qi[sel]), kiT=np.ascontiguousarray(ki.T), sgn=np.ascontiguousarray(sg[sel].reshape(nqb, 128, 8).transpose(1, 0, 2)),
                qT=T8(dq[sel]), kT=T8(dk), vP=np.ascontiguousarray(dv.reshape(S // 128, 128, 8, 64).transpose(2, 1, 0, 3)),
                ident=np.eye(128, dtype=np.float32).astype(NPBF), cbig=cb, p2row=p2)


def _cat_batch(res, key, b):
    return np.concatenate([np.asarray(res[2 * b][key]), np.asarray(res[2 * b + 1][key])], 0)


def _tokens_of(j, ng=16):
    return np.concatenate([np.arange(g * 512, (g + 1) * 512) for g in qgroups(ng, j)])


def kernel(**inputs):
    W = {k: np.asarray(v) for k, v in inputs.items()}
    x = W["x"].astype(np.float32)
    positions = W["positions"]
    B = 4
    A0 = run_ka0(x, positions, W)
    tri, upp = gla_consts()
    a0 = {k: [_cat_batch(A0, "o_" + k, b) for b in range(B)] for k in ("gq", "gk", "gv", "ga", "sr", "qf", "kf", "vm")}
    del A0
    in_maps = []
    for c in range(NCORES):
        b, j = c // 2, c % 2
        hs = slice(128 * j, 128 * j + 128)
        vs = slice(256 * j, 256 * j + 256)
        in_maps.append(dict(
            g2=np.ascontiguousarray(a0["ga"][b][:, hs]),
            qT2=np.ascontiguousarray(a0["gq"][b][:, hs].reshape(SEQ, 2, 64).transpose(2, 1, 0)),
            kT2=np.ascontiguousarray(a0["gk"][b][:, hs].reshape(SEQ, 2, 64).transpose(2, 1, 0)),
            k2=np.ascontiguousarray(a0["gk"][b][:, hs]), v2=np.ascontiguousarray(a0["gv"][b][:, vs]),
            sr2=np.ascontiguousarray(a0["sr"][b][:, vs]), gain=np.ascontiguousarray(W["l0_gla_norm"][None, vs]),
            tri=tri, upp=upp))
    G = _run(build_gla(128), in_maps)
    o_gla = [np.concatenate([np.asarray(G[2 * b]["o_gla"]), np.asarray(G[2 * b + 1]["o_gla"])], 1) for b in range(B)]
    del G
    in_maps = []
    for c in range(NCORES):
        b, j = c // 2, c % 2
        in_maps.append(mla_inputs(a0["qf"][b].reshape(SEQ, 8, 96), a0["kf"][b].reshape(SEQ, 8, 96),
                                  a0["vm"][b].reshape(SEQ, 8, 64), j, 16))
    M = _run(build_mla(16), in_maps)
    o_mlaT = [np.zeros((512, SEQ), np.float32) for _ in range(B)]
    for c in range(NCORES):
        b, j = c // 2, c % 2
        o_mlaT[b][:, _tokens_of(j)] = np.asarray(M[c]["o_T"]).reshape(512, TOK)
    del M, a0
    xf = x.reshape(B * SEQ, D)
    in_maps = []
    for c in range(NCORES):
        b, hf = c // 2, c % 2
        ts = slice(hf * TOK, (hf + 1) * TOK)
        mixT = np.concatenate([o_gla[b][ts].T, o_mlaT[b][:, ts]], 0)
        in_maps.append(kc_inputs(xf[c * TOK:(c + 1) * TOK], mixT, W, "l0_"))
    C0 = _run(build_kc(False, 8), in_maps)
    x2 = [np.asarray(C0[c]["o_x"]) for c in range(NCORES)]
    del C0, o_gla, o_mlaT
    in_maps = [dict(xT=np.ascontiguousarray(x2[c].T), w_in=W["l1_w_in"]) for c in range(NCORES)]
    A1 = _run(build_ka1(8), in_maps)
    a1 = {k: [_cat_batch(A1, "o_" + k, b) for b in range(B)] for k in ("dq", "dk", "dv", "qi", "ki", "sg", "u")}
    del A1
    in_maps = []
    for c in range(NCORES):
        b, j = c // 2, c % 2
        in_maps.append(dsa_inputs(a1["dq"][b], a1["dk"][b], a1["dv"][b], a1["qi"][b], a1["ki"][b], a1["sg"][b], j, 16))
    Dr = _run(build_dsa(16), in_maps)
    o_dsa = [np.zeros((SEQ, 512), np.float32) for _ in range(B)]
    for c in range(NCORES):
        b, j = c // 2, c % 2
        o_dsa[b][_tokens_of(j)] = np.asarray(Dr[c]["o_dsa"])
    del Dr
    in_maps = []
    for c in range(NCORES):
        b, j = c // 2, c % 2
        in_maps.append(s5_inputs(a1["u"][b][:, 256 * j:256 * j + 256].T, W, j))
    Sr = _run(build_s5(16), in_maps)
    yT = [np.concatenate([np.asarray(Sr[2 * b]["o_y"]), np.asarray(Sr[2 * b + 1]["o_y"])], 0) for b in range(B)]
    del Sr, a1
    in_maps = []
    for c in range(NCORES):
        b, hf = c // 2, c % 2
        ts = slice(hf * TOK, (hf + 1) * TOK)
        in_maps.append(kc_inputs(x2[c], o_dsa[b][ts].T, W, "l1_", glu_in=yT[b][:, ts]))
    C1 = _run(build_kc(True, 8), in_maps)
    out = np.concatenate([np.asarray(C1[c]["o_x"]) for c in range(NCORES)], 0)
    return out.reshape(B, SEQ, D).astype(np.float32)
```

```python
import numpy as np
import ml_dtypes
import concourse.bass as bass
import concourse.mybir as mybir
from concourse.bass_utils import run_bass_kernel_spmd

F32 = mybir.dt.float32
BF16 = mybir.dt.bfloat16
I32 = mybir.dt.int32
AF = mybir.ActivationFunctionType
ALU = mybir.AluOpType
AX = mybir.AxisListType
NPBF = ml_dtypes.bfloat16

NCORES = 8
D = 1024
SEQ = 8192
TOK = 4096
LN_EPS = 1e-5
ALPHA = 4 ** 0.25


class Buf:
    __slots__ = ("t", "w", "wb", "r", "pr", "name", "psum")

    def __init__(self, t, name="", psum=False):
        self.psum = psum
        self.t = t
        self.w = {}
        self.wb = {}
        self.r = {}
        self.pr = {}
        self.name = name

    def __getitem__(self, idx):
        return self.t[idx]


class Prog:
    NDMA = 14

    def __init__(self, nc):
        self.nc = nc
        self.E = {"pe": nc.tensor, "act": nc.scalar, "dve": nc.vector,
                  "pool": nc.gpsimd, "sp": nc.sync}
        self.sem = {}
        self.cnt = {}
        for k in self.E:
            self.sem[k] = nc.alloc_semaphore("s_" + k)
            self.cnt[k] = 0
        for i in range(self.NDMA):
            k = "d%d" % i
            self.sem[k] = nc.alloc_semaphore("s_" + k)
            self.cnt[k] = 0
        self.waited = {}
        self.dma_rr = 0
        self.n_inst = 0
        self.dq = 0

    def sb(self, name, shape, dt):
        return Buf(self.nc.alloc_sbuf_tensor(name, list(shape), dt), name)

    def ps(self, name, shape, dt=F32):
        return Buf(self.nc.alloc_psum_tensor(name, list(shape), dt), name, psum=True)

    def _need(self, eng, deps):
        q = self.E[eng]
        for (k, v) in deps.items():
            if k == "pe" and eng == "pe":
                continue
            if self.waited.get((eng, k), 0) < v:
                q.wait_ge(self.sem[k], v)
                self.waited[(eng, k)] = v
                self.n_inst += 1

    @staticmethod
    def _mx(m, k, v):
        if m.get(k, 0) < v:
            m[k] = v

    def _deps(self, reads, writes, add):
        m = {}
        for b in reads:
            for k, v in b.w.items():
                self._mx(m, k, v)
            if b.psum:
                for k, v in b.r.items():
                    self._mx(m, k, v)
        for b in writes:
            for k, v in (b.wb if add else b.w).items():
                self._mx(m, k, v)
            for k, v in b.r.items():
                self._mx(m, k, v)
            if add:
                for k, v in b.pr.items():
                    self._mx(m, k, v)
        return m

    def _commit(self, key, val, reads, writes, add):
        for b in reads:
            b.r[key] = val
        for b in writes:
            if add:
                b.w[key] = val
            else:
                b.w = {key: val}
                b.wb = {key: val}
                b.pr = b.r
                b.r = {}

    def op(self, eng, fn, reads=(), writes=(), add=False):
        self.nops = getattr(self, "nops", 0) + 1
        if self.nops > getattr(self, "limit", 10 ** 9):
            return None
        self._need(eng, self._deps(reads, writes, add))
        ins = fn(self.E[eng])
        self.cnt[eng] += 1
        ins.then_inc(self.sem[eng], 1)
        self._commit(eng, self.cnt[eng], reads, writes, add)
        self.n_inst += 1
        return ins

    def dma(self, out, in_, reads=(), writes=(), q=None, add=False, **kw):
        if q is None:
            q = ("sp", "sp")[self.dq % 2]
            self.dq += 1
        i = self.dma_rr
        self.dma_rr = (self.dma_rr + 1) % self.NDMA
        k = "d%d" % i
        deps = self._deps(reads, writes, add)
        if self.cnt[k] > 0:
            self._mx(deps, k, self.cnt[k])
        self._need(q, deps)
        ins = self.E[q].dma_start(out=out, in_=in_, **kw)
        self.cnt[k] += 16
        ins.then_inc(self.sem[k], 16)
        self._commit(k, self.cnt[k], reads, writes, add)
        self.n_inst += 1
        return ins

    def finish(self):
        deps = {"d%d" % i: self.cnt["d%d" % i] for i in range(self.NDMA)
                if self.cnt["d%d" % i] > 0}
        self._need("sp", deps)


def _newnc():
    return bass.Bass("TRN2", target_bir_lowering=False)


def _din(nc, name, shape, dt=F32):
    return nc.dram_tensor(name, list(shape), dt, kind="ExternalInput").ap()


def _dout(nc, name, shape, dt=F32):
    return nc.dram_tensor(name, list(shape), dt, kind="ExternalOutput").ap()


def _run(nc, in_maps):
    res = run_bass_kernel_spmd(nc, in_maps, core_ids=list(range(NCORES)))
    return res.results


def load_cast(p, dst_bf, dst_ap, src_ap, stage, stage_ap, eng="pool"):
    p.dma(stage_ap, src_ap, writes=[stage])
    p.op(eng, lambda e: e.tensor_copy(dst_ap, stage_ap), [stage], [dst_bf])


L0_IN = 1968


def build_ka0():
    nc = _newnc()
    xT = _din(nc, "xT", [D, TOK])
    posl = _din(nc, "posl", [128, 32], I32)
    w_in = _din(nc, "w_in", [D, L0_IN])
    wg2 = _din(nc, "wg2", [16, 256])
    bg = _din(nc, "bg", [1, 256])
    qn = _din(nc, "qn", [128, 2])
    w_uq = _din(nc, "w_uq", [256, 768])
    kvn = _din(nc, "kvn", [128, 1])
    w_ukv = _din(nc, "w_ukv", [128, 1024])
    invf = _din(nc, "invf", [1, 128])
    o_gq = _dout(nc, "o_gq", [TOK, 256], BF16)
    o_gk = _dout(nc, "o_gk", [TOK, 256], BF16)
    o_gv = _dout(nc, "o_gv", [TOK, 512], BF16)
    o_ga = _dout(nc, "o_ga", [TOK, 256], F32)
    o_sr = _dout(nc, "o_sr", [TOK, 512], F32)
    o_qf = _dout(nc, "o_qf", [TOK, 768], BF16)
    o_kf = _dout(nc, "o_kf", [TOK, 768], BF16)
    o_vm = _dout(nc, "o_vm", [TOK, 512], BF16)
    p = Prog(nc)
    import os
    p.limit = int(os.environ.get('KA0_LIM', '1000000000'))

    winb = p.sb("winb", [128, 8, L0_IN], BF16)
    wst = [p.sb("wst%d" % i, [128, L0_IN], F32) for i in range(2)]
    w_in_v = w_in.rearrange("(c p) n -> p c n", p=128)
    for c in range(8):
        st = wst[c % 2]
        p.dma(st[:], w_in_v[:, c, :], writes=[st])
        p.op(("pool", "dve")[c % 2], lambda e: e.tensor_copy(winb[:, c, :], st[:]), [st], [winb], add=True)
    qn_s = p.sb("qn_s", [128, 2], F32)
    kvn_s = p.sb("kvn_s", [128, 1], F32)
    p.dma(qn_s[:], qn[:, :], writes=[qn_s])
    p.dma(kvn_s[:], kvn[:, :], writes=[kvn_s])
    wuqb = p.sb("wuqb", [128, 2, 768], BF16)
    wukvb = p.sb("wukvb", [128, 1024], BF16)
    st = wst[0]
    p.dma(st[:, 0:1536].rearrange("p (c n) -> p c n", c=2), w_uq.rearrange("(c p) n -> p c n", p=128), writes=[st])
    for c in range(2):
        p.op("dve", lambda e: e.tensor_scalar(wuqb[:, c, :], st[:, c * 768:(c + 1) * 768], qn_s[:, c:c + 1],
                                               96 ** -0.5, ALU.mult, ALU.mult), [st, qn_s], [wuqb])
    st = wst[1]
    p.dma(st[:, 0:1024], w_ukv[:, :], writes=[st])
    p.op("dve", lambda e: e.tensor_scalar(wukvb[:], st[:, 0:1024], kvn_s[:, 0:1], None, ALU.mult), [st, kvn_s], [wukvb])
    wg2s = p.sb("wg2s", [16, 256], F32)
    wg2b = p.sb("wg2b", [16, 256], BF16)
    bgs = p.sb("bgs", [1, 256], F32)
    bgb = p.sb("bgb", [1, 256], BF16)
    onesb = p.sb("onesb", [1, 128], BF16)
    p.dma(wg2s[:], wg2[:, :], writes=[wg2s])
    p.dma(bgs[:], bg[:, :], writes=[bgs])
    p.op("dve", lambda e: e.tensor_copy(wg2b[:], wg2s[:]), [wg2s], [wg2b])
    p.op("dve", lambda e: e.tensor_copy(bgb[:], bgs[:]), [bgs], [bgb])
    p.op("dve", lambda e: e.memset(onesb[:], 1.0), [], [onesb])
    invf8 = p.sb("invf8", [128, 128], F32)
    p.dma(invf8[:], invf[0:1, :].to_broadcast([128, 128]), writes=[invf8])
    p.op("dve", lambda e: e.tensor_scalar(invf8[:], invf8[:], 1.0 / (2 * np.pi), None, ALU.mult), [invf8], [invf8])
    posi = p.sb("posi", [128, 32], I32)
    posf = p.sb("posf", [128, 32], F32)
    p.dma(posi[:], posl[:, :], writes=[posi])
    p.op("dve", lambda e: e.tensor_copy(posf[:], posi[:]), [posi], [posf])

    xs = [p.sb("xs%d" % i, [128, 8, 512], F32) for i in range(2)]
    xb = [p.sb("xb%d" % i, [128, 8, 512], BF16) for i in range(2)]
    fmb = [p.sb("fmb%d" % i, [128, 4, 512], BF16) for i in range(2)]
    hps = [p.ps("hps%d" % i, [128, 512]) for i in range(4)]
    fps = [p.ps("fps%d" % i, [128, 512]) for i in range(2)]
    sps = [p.ps("sps%d" % i, [128, 512]) for i in range(2)]
    NB = 2
    gq_t = [p.sb("gq_t%d" % i, [128, 256], BF16) for i in range(NB)]
    gk_t = [p.sb("gk_t%d" % i, [128, 256], BF16) for i in range(NB)]
    gv_t = [p.sb("gv_t%d" % i, [128, 512], BF16) for i in range(NB)]
    ga_t = [p.sb("ga_t%d" % i, [128, 256], F32) for i in range(NB)]
    sr_t = [p.sb("sr_t%d" % i, [128, 512], F32) for i in range(NB)]
    qf_t = [p.sb("qf_t%d" % i, [128, 8, 96], BF16) for i in range(NB)]
    kf_t = [p.sb("kf_t%d" % i, [128, 8, 96], BF16) for i in range(NB)]
    vm_t = [p.sb("vm_t%d" % i, [128, 8, 64], BF16) for i in range(NB)]
    junk = p.sb("junk", [128, 256], F32)
    ss = p.sb("ss", [128, 2], F32)
    rstd = p.sb("rstd", [128, 2], F32)
    t2 = p.sb("t2", [128, 2, 128], F32)
    ti = p.sb("ti", [128, 2, 128], I32)
    tf = p.sb("tf", [128, 2, 128], F32)
    sc = p.sb("sc", [128, 2, 128], F32)
    Qs = p.sb("Qs", [128, 8, 96], F32)
    ra = p.sb("ra", [128, 8, 16], F32)
    rb = p.sb("rb", [128, 8, 16], F32)
    kr = p.sb("kr", [128, 32], F32)
    kro = p.sb("kro", [128, 32], F32)
    ez = p.sb("ez", [128, 256], F32)

    xT_v = xT.rearrange("(c p) t -> p c t", p=128)
    fm_cols = [(1552, 128), (1680, 128), (1808, 128), (1024, 16)]
    import os
    for g in range(int(os.environ.get('KA0_G', '8'))):
        xsg, xbg, fm = xs[g % 2], xb[g % 2], fmb[g % 2]
        for c in range(8):
            p.dma(xsg[:, c, :], xT_v[:, c, g * 512:(g + 1) * 512], writes=[xsg], add=(c > 0))
        for c in range(8):
            p.op(("pool", "dve")[c % 2], lambda e: e.tensor_copy(xbg[:, c, :], xsg[:, c, :]), [xsg], [xbg], add=(c > 0))
        for mi, (c0, m) in enumerate(fm_cols):
            fp_ = fps[mi % 2]
            for c in range(8):
                p.op("pe", lambda e: e.matmul(fp_[0:m, :], winb[:, c, c0:c0 + m], xbg[:, c, :],
                                              start=(c == 0), stop=(c == 7)), [winb, xbg], [fp_])
            p.op("act", lambda e: e.copy(fm[0:m, mi, :], fp_[0:m, :]), [fp_], [fm])
        for s in range(int(os.environ.get('KA0_S', '4'))):
            sub = g * 4 + s
            b = sub % NB
            tsl = slice(s * 128, (s + 1) * 128)
            for bi in range(4):
                n0 = bi * 512
                n = min(512, L0_IN - n0)
                for c in range(8):
                    p.op("pe", lambda e: e.matmul(hps[bi][:, 0:n], xbg[:, c, tsl], winb[:, c, n0:n0 + n],
                                                  start=(c == 0), stop=(c == 7)), [xbg, winb], [hps[bi]])
            if os.environ.get('KA0_NOROPE', '0') == '1':
                p.op('act', lambda e: e.activation(sc[:], posf[:, 0:1].to_broadcast([128, 256]).rearrange('p (a b) -> p a b', a=2), AF.Copy), [posf], [sc])
            else:
                p.op("dve", lambda e: e.tensor_scalar(t2[:, 0, :], invf8[:], posf[:, sub:sub + 1], None, ALU.mult),
                     [invf8, posf], [t2])
                p.op("dve", lambda e: e.tensor_scalar(t2[:, 1, :], t2[:, 0, :], 0.25, None, ALU.add), [t2], [t2])
                p.op("dve", lambda e: e.tensor_copy(ti[:], t2[:]), [t2], [ti])
                p.op("dve", lambda e: e.tensor_copy(tf[:], ti[:]), [ti], [tf])
                p.op("dve", lambda e: e.tensor_sub(t2[:], t2[:], tf[:]), [t2, tf], [t2])
                p.op("act", lambda e: e.activation(sc[:], t2[:], AF.Sin, scale=6.28318), [t2], [sc])
            p.op("act", lambda e: e.mul(gq_t[b][:], hps[0][:, 0:256], 0.125), [hps[0]], [gq_t[b]])
            if os.environ.get('KA0_V', '0') == '1':
                p.op("act", lambda e: e.copy(gk_t[b][:], hps[0][:, 256:512]), [hps[0]], [gk_t[b]])
            elif os.environ.get('KA0_V', '0') == '4':
                p.op("dve", lambda e: e.tensor_copy(gk_t[b][:], hps[0][:, 256:512]), [hps[0], gq_t[b]], [gk_t[b]])
            elif os.environ.get('KA0_V', '0') == '2':
                p.op("pool", lambda e: e.tensor_copy(kr[:], kr[:]), [kr], [kr])
            else:
                p.op("dve", lambda e: e.tensor_copy(gk_t[b][:], hps[0][:, 256:512]), [hps[0]], [gk_t[b]])
            p.op("act", lambda e: e.copy(gv_t[b][:], hps[1][:, :]), [hps[1]], [gv_t[b]])
            p.op("act", lambda e: e.activation(sr_t[b][:, 0:496], hps[2][:, 16:512], AF.Silu), [hps[2]], [sr_t[b]])
            p.op("act", lambda e: e.activation(sr_t[b][:, 496:512], hps[3][:, 0:16], AF.Silu), [hps[3]], [sr_t[b]])
            p.op("act", lambda e: e.activation(junk[:, 0:256], hps[3][:, 16:272], AF.Square, accum_out=ss[:, 0:1]),
                 [hps[3]], [junk, ss])
            p.op("act", lambda e: e.activation(junk[:, 0:128], hps[3][:, 272:400], AF.Square, accum_out=ss[:, 1:2]),
                 [hps[3]], [junk, ss])
            p.op("dve", lambda e: e.tensor_scalar(rstd[:, 0:1], ss[:, 0:1], 1.0 / 256, LN_EPS, ALU.mult, ALU.add), [ss], [rstd])
            p.op("dve", lambda e: e.tensor_scalar(rstd[:, 1:2], ss[:, 1:2], 1.0 / 128, LN_EPS, ALU.mult, ALU.add), [ss], [rstd])
            p.op("act", lambda e: e.sqrt(rstd[:], rstd[:]), [rstd], [rstd])
            p.op("dve", lambda e: e.reciprocal(rstd[:], rstd[:]), [rstd], [rstd])
            p.op("dve", lambda e: e.tensor_copy(kr[:], hps[3][:, 400:432]), [hps[3]], [kr])
            for bi, (n0, n) in enumerate(((0, 512), (512, 256))):
                for c in range(2):
                    p.op("pe", lambda e: e.matmul(sps[bi][:, 0:n], fm[:, c, tsl], wuqb[:, c, n0:n0 + n],
                                                  start=(c == 0), stop=(c == 1)), [fm, wuqb], [sps[bi]])
            Qf = Qs[:].rearrange("p h d -> p (h d)")
            p.op("act", lambda e: e.activation(Qf[:, 0:512], sps[0][:, :], AF.Copy, scale=rstd[:, 0:1]), [sps[0], rstd], [Qs])
            p.op("act", lambda e: e.activation(Qf[:, 512:768], sps[1][:, 0:256], AF.Copy, scale=rstd[:, 0:1]), [sps[1], rstd], [Qs])
            qf = qf_t[b]
            p.op("pool", lambda e: e.tensor_copy(qf[:, :, 0:64], Qs[:, :, 0:64]), [Qs], [qf])
            sin8 = sc[:, 0, :].rearrange("p (h j) -> p h j", h=8)
            cos8 = sc[:, 1, :].rearrange("p (h j) -> p h j", h=8)
            p.op("dve", lambda e: e.tensor_mul(ra[:], Qs[:, :, 64:80], cos8), [Qs, sc], [ra])
            p.op("pool", lambda e: e.tensor_mul(rb[:], Qs[:, :, 80:96], sin8), [Qs, sc], [rb])
            p.op("dve", lambda e: e.tensor_sub(qf[:, :, 64:80], ra[:], rb[:]), [ra, rb], [qf])
            p.op("dve", lambda e: e.tensor_mul(ra[:], Qs[:, :, 80:96], cos8), [Qs, sc], [ra])
            p.op("pool", lambda e: e.tensor_mul(rb[:], Qs[:, :, 64:80], sin8), [Qs, sc], [rb])
            p.op("dve", lambda e: e.tensor_add(qf[:, :, 80:96], ra[:], rb[:]), [ra, rb], [qf])
            for bi in range(2):
                p.op("pe", lambda e: e.matmul(sps[bi][:, :], fm[:, 2, tsl], wukvb[:, bi * 512:(bi + 1) * 512],
                                              start=True, stop=True), [fm, wukvb], [sps[bi]])
            kf, vm = kf_t[b], vm_t[b]
            for bi in range(2):
                src = sps[bi][:, :].rearrange("p (h d) -> p h d", h=4)
                p.op("act", lambda e: e.activation(kf[:, bi * 4:(bi + 1) * 4, 0:64], src[:, :, 0:64], AF.Copy,
                                                   scale=rstd[:, 1:2]), [sps[bi], rstd], [kf])
                p.op("act", lambda e: e.activation(vm[:, bi * 4:(bi + 1) * 4, :], src[:, :, 64:128], AF.Copy,
                                                   scale=rstd[:, 1:2]), [sps[bi], rstd], [vm])
            s16, c16 = sc[:, 0, 0:16], sc[:, 1, 0:16]
            p.op("dve", lambda e: e.tensor_mul(ra[:, 0, :], kr[:, 0:16], c16), [kr, sc], [ra])
            p.op("dve", lambda e: e.tensor_mul(rb[:, 0, :], kr[:, 16:32], s16), [kr, sc], [rb])
            p.op("dve", lambda e: e.tensor_sub(kro[:, 0:16], ra[:, 0, :], rb[:, 0, :]), [ra, rb], [kro])
            p.op("dve", lambda e: e.tensor_mul(ra[:, 0, :], kr[:, 16:32], c16), [kr, sc], [ra])
            p.op("dve", lambda e: e.tensor_mul(rb[:, 0, :], kr[:, 0:16], s16), [kr, sc], [rb])
            p.op("dve", lambda e: e.tensor_add(kro[:, 16:32], ra[:, 0, :], rb[:, 0, :]), [ra, rb], [kro])
            p.op("pool", lambda e: e.tensor_copy(kf[:, :, 64:96], kro[:].unsqueeze(1).to_broadcast([128, 8, 32])), [kro], [kf])
            p.op("pe", lambda e: e.matmul(sps[0][:, 0:256], fm[0:16, 3, tsl], wg2b[:], start=True, stop=False), [fm, wg2b], [sps[0]])
            p.op("pe", lambda e: e.matmul(sps[0][:, 0:256], onesb[:], bgb[:], start=False, stop=True), [onesb, bgb], [sps[0]])
            p.op("act", lambda e: e.activation(ez[:], sps[0][:, 0:256], AF.Exp, scale=-1.0), [sps[0]], [ez])
            p.op("act", lambda e: e.activation(ez[:], ez[:], AF.Ln, bias=1.0), [ez], [ez])
            p.op("pool", lambda e: e.tensor_scalar(ga_t[b][:], ez[:], -1.0 / 16, None, ALU.mult), [ez], [ga_t[b]])
            r0 = sub * 128
            p.dma(o_gq[r0:r0 + 128, :], gq_t[b][:], reads=[gq_t[b]])
            p.dma(o_gk[r0:r0 + 128, :], gk_t[b][:], reads=[gk_t[b]])
            p.dma(o_gv[r0:r0 + 128, :], gv_t[b][:], reads=[gv_t[b]])
            p.dma(o_ga[r0:r0 + 128, :], ga_t[b][:], reads=[ga_t[b]])
            p.dma(o_sr[r0:r0 + 128, :], sr_t[b][:], reads=[sr_t[b]])
            p.dma(o_qf[r0:r0 + 128, :], qf[:].rearrange("p h d -> p (h d)"), reads=[qf])
            p.dma(o_kf[r0:r0 + 128, :], kf[:].rearrange("p h d -> p (h d)"), reads=[kf])
            p.dma(o_vm[r0:r0 + 128, :], vm[:].rearrange("p h d -> p (h d)"), reads=[vm])
    p.finish()
    print('ka0 nops', p.nops, 'n_inst', p.n_inst)
    return nc


def invfreq_const():
    half = 16
    inv = (10000.0 ** (-np.arange(half, dtype=np.float32) / half)).astype(np.float32)
    return np.tile(inv, 8)[None, :].astype(np.float32)


def run_ka0(x, positions, W):
    xf = x.reshape(32768, D)
    in_maps = []
    for c in range(NCORES):
        xs = xf[c * TOK:(c + 1) * TOK]
        pos = positions.reshape(-1)[c * TOK:(c + 1) * TOK]
        in_maps.append(dict(
            xT=np.ascontiguousarray(xs.T),
            posl=np.ascontiguousarray(pos.reshape(32, 128).T),
            w_in=W["l0_w_in"], wg2=W["l0_gla_wg2"], bg=W["l0_gla_bg"].reshape(1, 256),
            qn=np.ascontiguousarray(W["l0_mla_q_norm"].reshape(2, 128).T),
            w_uq=W["l0_mla_w_uq"], kvn=W["l0_mla_kv_norm"].reshape(128, 1),
            w_ukv=W["l0_mla_w_ukv"], invf=invfreq_const()))
    return _run(build_ka0(), in_maps)


def _layernorm_tile(p, v, junkb, st, G, B, out_t):
    p.op("act", lambda e: e.activation(junkb[:], v[:], AF.Copy, accum_out=st[:, 0:1]), [v], [junkb, st])
    p.op("act", lambda e: e.activation(junkb[:], v[:], AF.Square, accum_out=st[:, 1:2]), [v], [junkb, st])
    p.op("dve", lambda e: e.tensor_scalar(st[:, 2:3], st[:, 0:1], 1.0 / D, None, ALU.mult), [st], [st])
    p.op("dve", lambda e: e.tensor_mul(st[:, 3:4], st[:, 2:3], st[:, 2:3]), [st], [st])
    p.op("dve", lambda e: e.scalar_tensor_tensor(st[:, 4:5], st[:, 1:2], 1.0 / D, st[:, 3:4], ALU.mult, ALU.subtract), [st], [st])
    p.op("dve", lambda e: e.tensor_scalar(st[:, 4:5], st[:, 4:5], LN_EPS, None, ALU.add), [st], [st])
    p.op("act", lambda e: e.sqrt(st[:, 4:5], st[:, 4:5]), [st], [st])
    p.op("dve", lambda e: e.reciprocal(st[:, 4:5], st[:, 4:5]), [st], [st])
    p.op("dve", lambda e: e.tensor_scalar(v[:], v[:], st[:, 2:3], st[:, 4:5], ALU.subtract, ALU.mult), [v, st], [v])
    p.op("pool", lambda e: e.tensor_mul(v[:], v[:], G[:]), [v, G], [v])
    p.op("dve", lambda e: e.tensor_add(out_t[:], v[:], B[:]), [v, B], [out_t])


def build_kc(glu=False, ngroups=8):
    nc = _newnc()
    ntok = ngroups * 512
    x = _din(nc, "x", [ntok, D])
    mixT = _din(nc, "mixT", [512 if glu else D, ntok])
    w_out = _din(nc, "w_out", [D, D])
    lnp = _din(nc, "lnp", [4, D])
    w1 = _din(nc, "w1", [D, 4 * D])
    w2 = _din(nc, "w2", [4 * D, D])
    ident = _din(nc, "ident", [128, 128])
    if glu:
        yT = _din(nc, "yT", [512, ntok])
        glu_w = _din(nc, "glu_w", [512, 512])
        glu_b = _din(nc, "glu_b", [128, 4])
    o_x = _dout(nc, "o_x", [ntok, D])
    p = Prog(nc)

    idf = p.sb("idf", [128, 128], F32)
    idb = p.sb("idb", [128, 128], BF16)
    p.dma(idf[:], ident[:, :], writes=[idf])
    p.op("dve", lambda e: e.tensor_copy(idb[:], idf[:]), [idf], [idb])
    GB = [p.sb("GB%d" % i, [128, D], F32) for i in range(4)]
    for i in range(4):
        p.dma(GB[i][:], lnp[i:i + 1, :].to_broadcast([128, D]), writes=[GB[i]])
    stg = [p.sb("stg%d" % i, [128, 1024], F32) for i in range(2)]
    woutb = p.sb("woutb", [128, 8, D], BF16)
    w2b = p.sb("w2b", [128, 32, D], BF16)
    wo_v = w_out.rearrange("(c p) n -> p c n", p=128)
    w2_v = w2.rearrange("(c p) n -> p c n", p=128)
    k = 0
    for c in range(8):
        st_ = stg[k % 2]
        p.dma(st_[:], wo_v[:, c, :], writes=[st_])
        p.op(("pool", "dve")[k % 2], lambda e: e.tensor_copy(woutb[:, c, :], st_[:]), [st_], [woutb], add=True)
        k += 1
    for c in range(32):
        st_ = stg[k % 2]
        p.dma(st_[:], w2_v[:, c, :], writes=[st_])
        p.op(("pool", "dve")[k % 2], lambda e: e.tensor_copy(w2b[:, c, :], st_[:]), [st_], [w2b], add=True)
        k += 1
    if glu:
        glub = p.sb("glub", [128, 4, 512], BF16)
        gbias = p.sb("gbias", [128, 4], F32)
        p.dma(gbias[:], glu_b[:, :], writes=[gbias])
        gw_v = glu_w.rearrange("(c p) n -> p c n", p=128)
        for c in range(4):
            st_ = stg[k % 2]
            p.dma(st_[:, 0:512], gw_v[:, c, :], writes=[st_])
            p.op(("pool", "dve")[k % 2], lambda e: e.tensor_copy(glub[:, c, :], st_[:, 0:512]), [st_], [glub], add=True)
            k += 1

    mixb = p.sb("mixb", [128, 8, 512], BF16)
    x1T = p.sb("x1T", [128, 8, 512], BF16)
    x1a = p.sb("x1a", [128, 4, D], F32)
    hT = p.sb("hT", [128, 32, 512], BF16)
    xs = [p.sb("xs%d" % i, [128, D], F32) for i in range(2)]
    vb0 = p.sb("vb0", [128, D], F32)
    vb = [vb0, vb0]
    x1b = p.sb("x1b", [128, D], BF16)
    stt = p.sb("stt", [128, 8], F32)
    w1s = [p.sb("w1s%d" % i, [128, 8, 128], F32) for i in range(2)]
    w1b = [p.sb("w1b%d" % i, [128, 8, 128], BF16) for i in range(2)]
    rl = [p.sb("rl%d" % i, [128, 512], F32) for i in range(2)]
    yps = [p.ps("yps%d" % i, [128, 512]) for i in range(4)]
    tps = p.ps("tps", [128, 8, 128], BF16)
    hps = [p.ps("hps%d" % i, [128, 512]) for i in range(2)]
    if glu:
        ygb = p.sb("ygb", [128, 4, 512], BF16)
    print("kc sbuf remaining", nc.sbuf_bytes_remaining)

    mix_v = mixT.rearrange("(c p) t -> p c t", p=128)
    w1_v = w1.rearrange("(c p) n -> p c n", p=128)
    if glu:
        y_v = yT.rearrange("(c p) t -> p c t", p=128)
    for g in range(ngroups):
        gs = slice(g * 512, (g + 1) * 512)
        nmix = 4 if glu else 8
        for c in range(nmix):
            st_ = stg[k % 2]
            p.dma(st_[:, 0:512], mix_v[:, c, gs], writes=[st_])
            p.op(("pool", "dve")[k % 2], lambda e: e.tensor_copy(mixb[:, c, :], st_[:, 0:512]), [st_], [mixb], add=(c > 0))
            k += 1
        if glu:
            for c in range(4):
                ys_, yg_ = rl[0], rl[1]
                p.dma(ys_[:], y_v[:, c, gs], writes=[ys_])
                p.op("act", lambda e: e.activation(yg_[:], ys_[:], AF.Square), [ys_], [yg_])
                p.op("dve", lambda e: e.tensor_scalar(yg_[:], yg_[:], 0.044715, 1.0, ALU.mult, ALU.add), [yg_], [yg_])
                p.op("pool", lambda e: e.tensor_mul(yg_[:], yg_[:], ys_[:]), [yg_, ys_], [yg_])
                p.op("act", lambda e: e.activation(yg_[:], yg_[:], AF.Sigmoid, scale=1.5957691216), [yg_], [yg_])
                p.op("dve", lambda e: e.tensor_mul(ygb[:, c, :], yg_[:], ys_[:]), [yg_, ys_], [ygb], add=(c > 0))
            for m in range(4):
                hp = hps[m % 2]
                for c in range(4):
                    p.op("pe", lambda e: e.matmul(hp[:], glub[:, c, m * 128:(m + 1) * 128], ygb[:, c, :],
                                                  start=(c == 0), stop=(c == 3)), [glub, ygb], [hp])
                sg = rl[m % 2]
                p.op("act", lambda e: e.activation(sg[:], hp[:], AF.Sigmoid, bias=gbias[:, m:m + 1]), [hp, gbias], [sg])
                p.op("dve", lambda e: e.tensor_mul(mixb[:, 4 + m, :], sg[:], ygb[:, m, :]), [sg, ygb], [mixb], add=True)
        for s in range(4):
            tsl = slice(s * 128, (s + 1) * 128)
            r0 = g * 512 + s * 128
            xt, v = xs[s % 2], vb[s % 2]
            p.dma(xt[:], x[r0:r0 + 128, :], writes=[xt])
            for nh in range(2):
                yp = yps[nh]
                for c in range(8):
                    p.op("pe", lambda e: e.matmul(yp[:], mixb[:, c, tsl], woutb[:, c, nh * 512:(nh + 1) * 512],
                                                  start=(c == 0), stop=(c == 7)), [mixb, woutb], [yp])
                p.op("dve", lambda e: e.scalar_tensor_tensor(v[:, nh * 512:(nh + 1) * 512], xt[:, nh * 512:(nh + 1) * 512],
                                                              ALPHA, yp[:], ALU.mult, ALU.add), [xt, yp], [v], add=(nh > 0))
            _layernorm_tile(p, v, x1b, stt, GB[0], GB[1], _View(x1a, x1a[:, s, :]))
            p.op("pool", lambda e: e.tensor_copy(x1b[:], x1a[:, s, :]), [x1a], [x1b])
            for c in range(8):
                p.op("pe", lambda e: e.transpose(tps[:, c, :], x1b[:, c * 128:(c + 1) * 128], idb[:]), [x1b, idb], [tps])
            p.op("act", lambda e: e.copy(x1T[:, :, tsl], tps[:]), [tps], [x1T], add=(s > 0))
        for j in range(32):
            ws_, wb_ = w1s[j % 2], w1b[j % 2]
            p.dma(ws_[:], w1_v[:, :, j * 128:(j + 1) * 128], writes=[ws_])
            p.op(("pool", "dve")[j % 2], lambda e: e.tensor_copy(wb_[:], ws_[:]), [ws_], [wb_])
            hp = hps[j % 2]
            for c in range(8):
                p.op("pe", lambda e: e.matmul(hp[:], wb_[:, c, :], x1T[:, c, :], start=(c == 0), stop=(c == 7)), [wb_, x1T], [hp])
            r_ = rl[j % 2]
            p.op("act", lambda e: e.activation(r_[:], hp[:], AF.Relu), [hp], [r_])
            p.op(("dve", "pool")[j % 2], lambda e: e.tensor_mul(hT[:, j, :], r_[:], r_[:]), [r_], [hT], add=(j > 0))
        for s in range(4):
            tsl = slice(s * 128, (s + 1) * 128)
            r0 = g * 512 + s * 128
            v = vb[s % 2]
            for nh in range(2):
                yp = yps[2 + nh]
                for j in range(32):
                    p.op("pe", lambda e: e.matmul(yp[:], hT[:, j, tsl], w2b[:, j, nh * 512:(nh + 1) * 512],
                                                  start=(j == 0), stop=(j == 31)), [hT, w2b], [yp])
                p.op("dve", lambda e: e.scalar_tensor_tensor(v[:, nh * 512:(nh + 1) * 512], x1a[:, s, nh * 512:(nh + 1) * 512],
                                                              ALPHA, yp[:], ALU.mult, ALU.add), [x1a, yp], [v], add=(nh > 0))
            ot = xs[s % 2]
            _layernorm_tile(p, v, x1b, stt, GB[2], GB[3], ot)
            p.dma(o_x[r0:r0 + 128, :], ot[:], reads=[ot])
    p.finish()
    print("kc nops", p.nops, "n_inst", p.n_inst)
    return nc


class _View:
    def __init__(self, parent, ap):
        self.parent = parent
        self.ap = ap

    def __getitem__(self, idx):
        return self.ap

    @property
    def w(self):
        return self.parent.w

    @w.setter
    def w(self, v):
        self.parent.w = v

    @property
    def wb(self):
        return self.parent.wb

    @wb.setter
    def wb(self, v):
        self.parent.wb = v

    @property
    def r(self):
        return self.parent.r

    @r.setter
    def r(self, v):
        self.parent.r = v

    @property
    def pr(self):
        return self.parent.pr

    @pr.setter
    def pr(self, v):
        self.parent.pr = v

    @property
    def psum(self):
        return self.parent.psum


def kc_inputs(x_tok, mixT, W, pre, glu_in=None):
    d = dict(x=np.ascontiguousarray(x_tok), mixT=np.ascontiguousarray(mixT), w_out=W[pre + "w_out"],
             lnp=np.stack([W[pre + "ln1_g"], W[pre + "ln1_b"], W[pre + "ln2_g"], W[pre + "ln2_b"]]).astype(np.float32),
             w1=W[pre + "mlp_w1"], w2=W[pre + "mlp_w2"], ident=np.eye(128, dtype=np.float32))
    if glu_in is not None:
        d["yT"] = np.ascontiguousarray(glu_in)
        d["glu_w"] = W["l1_glu_w"]
        d["glu_b"] = np.ascontiguousarray(W["l1_glu_b"].reshape(4, 128).T)
    return d


L1_IN = 2632


def build_ka1(ngroups=8):
    nc = _newnc()
    ntok = ngroups * 512
    xT = _din(nc, "xT", [D, ntok])
    w_in = _din(nc, "w_in", [D, L1_IN])
    o_dq = _dout(nc, "o_dq", [ntok, 512], BF16)
    o_dk = _dout(nc, "o_dk", [ntok, 512], BF16)
    o_dv = _dout(nc, "o_dv", [ntok, 512], BF16)
    o_qi = _dout(nc, "o_qi", [ntok, 512], BF16)
    o_ki = _dout(nc, "o_ki", [ntok, 64], BF16)
    o_sg = _dout(nc, "o_sg", [ntok, 8], F32)
    o_u = _dout(nc, "o_u", [ntok, 512], F32)
    p = Prog(nc)
    winb = p.sb("winb", [128, 8, L1_IN], BF16)
    wst = [p.sb("wst%d" % i, [128, L1_IN], F32) for i in range(2)]
    w_in_v = w_in.rearrange("(c p) n -> p c n", p=128)
    for c in range(8):
        st = wst[c % 2]
        p.dma(st[:], w_in_v[:, c, :], writes=[st])
        p.op(("pool", "dve")[c % 2], lambda e: e.tensor_copy(winb[:, c, :], st[:]), [st], [winb], add=True)
    xs = [p.sb("xs%d" % i, [128, 8, 512], F32) for i in range(2)]
    xb = [p.sb("xb%d" % i, [128, 8, 512], BF16) for i in range(2)]
    hps = [p.ps("hps%d" % i, [128, 512]) for i in range(6)]
    NB = 2
    T = {}
    for nm, w, dt in (("dq", 512, BF16), ("dk", 512, BF16), ("dv", 512, BF16), ("qi", 512, BF16),
                      ("ki", 64, BF16), ("sg", 8, F32), ("u", 512, F32)):
        T[nm] = [p.sb("%s_t%d" % (nm, i), [128, w], dt) for i in range(NB)]
    aw = p.sb("aw", [128, 8], F32)
    xT_v = xT.rearrange("(c p) t -> p c t", p=128)
    for g in range(ngroups):
        xsg, xbg = xs[g % 2], xb[g % 2]
        for c in range(8):
            p.dma(xsg[:, c, :], xT_v[:, c, g * 512:(g + 1) * 512], writes=[xsg], add=(c > 0))
        for c in range(8):
            p.op(("pool", "dve")[c % 2], lambda e: e.tensor_copy(xbg[:, c, :], xsg[:, c, :]), [xsg], [xbg], add=(c > 0))
        for s in range(4):
            sub = g * 4 + s
            b = sub % NB
            tsl = slice(s * 128, (s + 1) * 128)
            for bi in range(6):
                n0 = bi * 512
                n = min(512, L1_IN - n0)
                for c in range(8):
                    p.op("pe", lambda e: e.matmul(hps[bi][:, 0:n], xbg[:, c, tsl], winb[:, c, n0:n0 + n],
                                                  start=(c == 0), stop=(c == 7)), [xbg, winb], [hps[bi]])
            p.op("act", lambda e: e.activation(T["dq"][b][:], hps[0][:], AF.Copy, scale=0.125), [hps[0]], [T["dq"][b]])
            p.op("dve", lambda e: e.tensor_copy(T["dk"][b][:], hps[1][:]), [hps[1]], [T["dk"][b]])
            p.op("act", lambda e: e.copy(T["dv"][b][:], hps[2][:]), [hps[2]], [T["dv"][b]])
            p.op("act", lambda e: e.activation(aw[:], hps[4][:, 64:72], AF.Abs, scale=(8 ** -0.5) * 0.125), [hps[4]], [aw])
            p.op("act", lambda e: e.activation(T["sg"][b][:], hps[4][:, 64:72], AF.Sign), [hps[4]], [T["sg"][b]])
            p.op("act", lambda e: e.copy(T["ki"][b][:], hps[4][:, 0:64]), [hps[4]], [T["ki"][b]])
            p.op("act", lambda e: e.copy(T["u"][b][:, 0:440], hps[4][:, 72:512]), [hps[4]], [T["u"][b]])
            p.op("dve", lambda e: e.tensor_copy(T["u"][b][:, 440:512], hps[5][:, 0:72]), [hps[5]], [T["u"][b]], add=True)
            p.op("dve", lambda e: e.tensor_tensor(T["qi"][b][:].rearrange("p (h d) -> p h d", h=8),
                                                  hps[3][:].rearrange("p (h d) -> p h d", h=8),
                                                  aw[:].unsqueeze(2).to_broadcast([128, 8, 64]), ALU.mult),
                 [hps[3], aw], [T["qi"][b]])
            r0 = sub * 128
            for nm, o in (("dq", o_dq), ("dk", o_dk), ("dv", o_dv), ("qi", o_qi), ("ki", o_ki), ("sg", o_sg), ("u", o_u)):
                p.dma(o[r0:r0 + 128, :], T[nm][b][:], reads=[T[nm][b]])
    p.finish()
    print("ka1 nops", p.nops, "n_inst", p.n_inst)
    return nc


def build_gla(nchunks=128):
    nc = _newnc()
    S = nchunks * 64
    g2 = _din(nc, "g2", [S, 128])
    qT2 = _din(nc, "qT2", [64, 2, S], BF16)
    kT2 = _din(nc, "kT2", [64, 2, S], BF16)
    k2 = _din(nc, "k2", [S, 128], BF16)
    v2 = _din(nc, "v2", [S, 256], BF16)
    sr2 = _din(nc, "sr2", [S, 256])
    gain = _din(nc, "gain", [1, 256])
    tri = _din(nc, "tri", [64, 64])
    upp = _din(nc, "upp", [64, 64])
    o_gla = _dout(nc, "o_gla", [S, 256])
    p = Prog(nc)
    Tri = p.sb("Tri", [64, 64], F32)
    Upp = p.sb("Upp", [64, 64], F32)
    TriM = p.sb("TriM", [64, 2, 64], F32)
    gbc = p.sb("gbc", [64, 256], F32)
    p.dma(Tri[:], tri[:, :], writes=[Tri])
    p.dma(Upp[:], upp[:, :], writes=[Upp])
    p.dma(gbc[:], gain[0:1, :].to_broadcast([64, 256]), writes=[gbc])
    for h in range(2):
        p.op("dve", lambda e: e.tensor_copy(TriM[:, h, :], Tri[:]), [Tri], [TriM], add=(h > 0))
    St = p.sb("St", [64, 2, 128], F32)
    Sb = [p.sb("Sb%d" % i, [64, 2, 128], BF16) for i in range(2)]
    p.op("dve", lambda e: e.memset(St[:], 0.0), [], [St])
    p.op("dve", lambda e: e.memset(Sb[0][:], 0.0), [], [Sb[0]])
    NBLK = 2
    gB = [p.sb("gB%d" % i, [64, 8, 128], F32) for i in range(NBLK)]
    kB = [p.sb("kB%d" % i, [64, 8, 128], BF16) for i in range(NBLK)]
    vB = [p.sb("vB%d" % i, [64, 8, 256], BF16) for i in range(NBLK)]
    sB = [p.sb("sB%d" % i, [64, 8, 256], F32) for i in range(NBLK)]
    qTB = [p.sb("qTB%d" % i, [64, 2, 512], BF16) for i in range(NBLK)]
    kTB = [p.sb("kTB%d" % i, [64, 2, 512], BF16) for i in range(NBLK)]
    oB = [p.sb("oB%d" % i, [64, 8, 256], F32) for i in range(NBLK)]
    E1 = [p.sb("E1_%d" % i, [64, 2, 64], F32) for i in range(2)]
    E2 = [p.sb("E2_%d" % i, [64, 2, 64], F32) for i in range(2)]
    E3 = [p.sb("E3_%d" % i, [64, 128], F32) for i in range(2)]
    qd = [p.sb("qd%d" % i, [64, 2, 64], BF16) for i in range(2)]
    ki = [p.sb("ki%d" % i, [64, 2, 64], BF16) for i in range(2)]
    KE = [p.sb("KE%d" % i, [64, 128], BF16) for i in range(2)]
    attm = [p.sb("attm%d" % i, [64, 2, 64], BF16) for i in range(2)]
    GS = [p.sb("GS%d" % i, [64, 256], F32) for i in range(2)]
    junk = p.sb("junk", [64, 128], F32)
    ss = [p.sb("ss%d" % i, [64, 2], F32) for i in range(2)]
    cum_ps = p.ps("cum_ps", [128, 512])
    rc_ps = p.ps("rc_ps", [128, 512])
    att_ps = p.ps("att_ps", [128, 512])
    kv_ps = p.ps("kv_ps", [128, 512])
    o_ps = [p.ps("o_ps%d" % i, [128, 512]) for i in range(2)]
    g_v = g2.rearrange("(n t) c -> t n c", t=64)
    k_v = k2.rearrange("(n t) c -> t n c", t=64)
    v_v = v2.rearrange("(n t) c -> t n c", t=64)
    s_v = sr2.rearrange("(n t) c -> t n c", t=64)
    o_v = o_gla.rearrange("(n t) c -> t n c", t=64)
    nblk = nchunks // 8
    f2 = lambda ap: ap.rearrange("p h i -> p (h i)")
    for blk in range(nblk):
        bb = blk % NBLK
        ns = slice(blk * 8, (blk + 1) * 8)
        ts_ = slice(blk * 512, (blk + 1) * 512)
        p.dma(gB[bb][:], g_v[:, ns, :], writes=[gB[bb]])
        p.dma(kB[bb][:], k_v[:, ns, :], writes=[kB[bb]])
        p.dma(vB[bb][:], v_v[:, ns, :], writes=[vB[bb]])
        p.dma(sB[bb][:], s_v[:, ns, :], writes=[sB[bb]])
        p.dma(qTB[bb][:], qT2[:, :, ts_], writes=[qTB[bb]])
        p.dma(kTB[bb][:], kT2[:, :, ts_], writes=[kTB[bb]])
        for ci in range(8):
            n = blk * 8 + ci
            a = n % 2
            cs = slice(ci * 64, (ci + 1) * 64)
            for h in range(2):
                p.op("pe", lambda e: e.matmul(cum_ps[0:64, h * 64:(h + 1) * 64], gB[bb][:, ci, h * 64:(h + 1) * 64], Tri[:],
                                              start=True, stop=True), [gB[bb], Tri], [cum_ps])
            p.op("pe", lambda e: e.matmul(rc_ps[0:64, 0:128], Upp[:], gB[bb][:, ci, :], start=True, stop=True), [gB[bb], Upp], [rc_ps])
            p.op("act", lambda e: e.activation(f2(E1[a][:]), cum_ps[0:64, 0:128], AF.Exp), [cum_ps], [E1[a]])
            p.op("act", lambda e: e.activation(f2(E2[a][:]), cum_ps[0:64, 0:128], AF.Exp, scale=-1.0), [cum_ps], [E2[a]])
            p.op("act", lambda e: e.activation(E3[a][:], rc_ps[0:64, 0:128], AF.Exp), [rc_ps], [E3[a]])
            p.op("dve", lambda e: e.tensor_mul(qd[a][:], qTB[bb][:, :, cs], E1[a][:]), [qTB[bb], E1[a]], [qd[a]])
            p.op("pool", lambda e: e.tensor_mul(ki[a][:], kTB[bb][:, :, cs], E2[a][:]), [kTB[bb], E2[a]], [ki[a]])
            p.op("dve", lambda e: e.tensor_mul(KE[a][:], kB[bb][:, ci, :], E3[a][:]), [kB[bb], E3[a]], [KE[a]])
            for h in range(2):
                p.op("pe", lambda e: e.matmul(att_ps[0:64, h * 64:(h + 1) * 64], ki[a][:, h, :], qd[a][:, h, :], start=True, stop=True),
                     [ki[a], qd[a]], [att_ps])
            p.op("dve", lambda e: e.tensor_mul(f2(attm[a][:]), att_ps[0:64, 0:128], f2(TriM[:])), [att_ps, TriM], [attm[a]])
            for h in range(2):
                p.op("pe", lambda e: e.matmul(kv_ps[0:64, h * 128:(h + 1) * 128], KE[a][:, h * 64:(h + 1) * 64],
                                              vB[bb][:, ci, h * 128:(h + 1) * 128], start=True, stop=True), [KE[a], vB[bb]], [kv_ps])
            op_ = o_ps[a]
            for h in range(2):
                p.op("pe", lambda e: e.matmul(op_[0:64, h * 128:(h + 1) * 128], attm[a][:, h, :], vB[bb][:, ci, h * 128:(h + 1) * 128],
                                              start=True, stop=False), [attm[a], vB[bb]], [op_])
                p.op("pe", lambda e: e.matmul(op_[0:64, h * 128:(h + 1) * 128], qd[a][:, h, :], Sb[a][:, h, :],
                                              start=False, stop=True), [qd[a], Sb[a]], [op_])
            for h in range(2):
                p.op("dve", lambda e: e.scalar_tensor_tensor(St[:, h, :], St[:, h, :], E1[a][:, h, 63:64],
                                                              kv_ps[0:64, h * 128:(h + 1) * 128], ALU.mult, ALU.add),
                     [St, E1[a], kv_ps], [St])
            p.op("act", lambda e: e.copy(Sb[1 - a][:], St[:]), [St], [Sb[1 - a]])
            for h in range(2):
                p.op("act", lambda e: e.activation(junk[:], op_[0:64, h * 128:(h + 1) * 128], AF.Square, accum_out=ss[a][:, h:h + 1]),
                     [op_], [junk, ss[a]], add=(h > 0))
            p.op("dve", lambda e: e.tensor_scalar(ss[a][:], ss[a][:], 1.0 / 128, LN_EPS, ALU.mult, ALU.add), [ss[a]], [ss[a]])
            p.op("act", lambda e: e.sqrt(ss[a][:], ss[a][:]), [ss[a]], [ss[a]])
            p.op("dve", lambda e: e.reciprocal(ss[a][:], ss[a][:]), [ss[a]], [ss[a]])
            p.op("pool", lambda e: e.tensor_mul(GS[a][:], gbc[:], sB[bb][:, ci, :]), [gbc, sB[bb]], [GS[a]])
            for h in range(2):
                p.op("dve", lambda e: e.scalar_tensor_tensor(oB[bb][:, ci, h * 128:(h + 1) * 128], op_[0:64, h * 128:(h + 1) * 128],
                                                              ss[a][:, h:h + 1], GS[a][:, h * 128:(h + 1) * 128], ALU.mult, ALU.mult),
                     [op_, ss[a], GS[a]], [oB[bb]], add=True)
        p.dma(o_v[:, ns, :], oB[bb][:], reads=[oB[bb]])
    p.finish()
    print("gla nops", p.nops, "n_inst", p.n_inst)
    return nc


def gla_consts():
    t = np.arange(64)
    tri = (t[:, None] <= t[None, :]).astype(np.float32)
    upp = (t[:, None] > t[None, :]).astype(np.float32)
    return tri, upp


def qgroups(ng, j):
    sel = (0, 3) if j == 0 else (1, 2)
    return [g for g in range(ng) if (g % 4) in sel]


def build_mla(ng=16, nheads=8):
    nc = _newnc()
    S = ng * 512
    nlg = ng // 2
    nq = nlg * 512
    qT = _din(nc, "qT", [nheads, 96, nq], BF16)
    kT = _din(nc, "kT", [nheads, 96, S], BF16)
    vP = _din(nc, "vP", [nheads, 128, S // 128, 64], BF16)
    dmask = _din(nc, "dmask", [2, 8, 128, 512], BF16)
    o_T = _dout(nc, "o_T", [nheads, 64, nq])
    p = Prog(nc)
    masks = p.sb("masks", [128, 2, 8, 512], BF16)
    for a_ in range(2):
        p.dma(masks[:, a_, :, :], dmask[a_].rearrange("r p f -> p r f"), writes=[masks], add=(a_ > 0))
    Sel = p.sb("Sel", [65, 64], F32)
    R = p.sb("R", [65, 512], F32)
    p.op("dve", lambda e: e.memset(Sel[:], 0.0), [], [Sel])
    p.op("dve", lambda e: e.memset(Sel[64:65, :], 1.0), [], [Sel])
    p.op("dve", lambda e: e.memset(R[:], 0.0), [], [R])
    nkb_tot = S // 128
    kTs = [p.sb("kTs%d" % i, [96, S], BF16) for i in range(2)]
    qTs = [p.sb("qTs%d" % i, [96, nq], BF16) for i in range(2)]
    Va = [p.sb("Va%d" % i, [128, nkb_tot, 65], BF16) for i in range(2)]
    for i in range(2):
        p.op("pool", lambda e: e.memset(Va[i][:, :, 64:65], 1.0), [], [Va[i]])
    NST = 5
    st_ps = [p.ps("st_ps%d" % i, [128, 512]) for i in range(NST)]
    pt = [p.sb("pt%d" % i, [128, 512], BF16) for i in range(NST)]
    ptf = [p.sb("ptf%d" % i, [128, 512], BF16) for i in range(2)]
    ot_ps = [p.ps("ot_ps%d" % i, [128, 512]) for i in range(2)]
    bc_ps = p.ps("bc_ps", [128, 512])
    OTs = [p.sb("OTs%d" % i, [65, 512], F32) for i in range(2)]
    oo = [p.sb("oo%d" % i, [64, 512], F32) for i in range(2)]
    it = 0
    ep = 0
    for h in range(nheads):
        hb = h % 2
        p.dma(kTs[hb][:], kT[h, :, :], writes=[kTs[hb]])
        p.dma(qTs[hb][:], qT[h, :, :], writes=[qTs[hb]])
        p.dma(Va[hb][:, :, 0:64], vP[h, :, :, :], writes=[Va[hb]], add=True)
        for gi in range(nlg):
            nkb = 8 * gi + 8
            otp = ot_ps[ep % 2]
            for kb in range(nkb):
                sp_, pt_ = st_ps[it % NST], pt[it % NST]
                p.op("pe", lambda e: e.matmul(sp_[:], kTs[hb][:, kb * 128:(kb + 1) * 128], qTs[hb][:, gi * 512:(gi + 1) * 512],
                                              start=True, stop=True), [kTs[hb], qTs[hb]], [sp_])
                if kb < 8 * gi:
                    p.op("act", lambda e: e.activation(pt_[:], sp_[:], AF.Exp), [sp_], [pt_])
                else:
                    r = kb - 8 * gi
                    pf = ptf[it % 2]
                    p.op("act", lambda e: e.activation(pf[:], sp_[:], AF.Exp), [sp_], [pf])
                    p.op(("dve", "pool")[r % 2], lambda e: e.tensor_mul(pt_[:], pf[:], masks[:, gi % 2, r, :]), [pf, masks], [pt_])
                p.op("pe", lambda e: e.matmul(otp[0:65, :], Va[hb][:, kb, :], pt_[:], start=(kb == 0), stop=(kb == nkb - 1)),
                     [Va[hb], pt_], [otp])
                it += 1
            ots, o_ = OTs[ep % 2], oo[ep % 2]
            p.op("act", lambda e: e.copy(ots[:], otp[0:65, :]), [otp], [ots])
            p.op("dve", lambda e: e.reciprocal(R[64:65, :], ots[64:65, :]), [ots], [R])
            p.op("pe", lambda e: e.matmul(bc_ps[0:64, :], Sel[:], R[:], start=True, stop=True), [Sel, R], [bc_ps])
            p.op("dve", lambda e: e.tensor_mul(o_[:], ots[0:64, :], bc_ps[0:64, :]), [ots, bc_ps], [o_])
            p.dma(o_T[h, :, gi * 512:(gi + 1) * 512], o_[:], reads=[o_])
            ep += 1
    p.finish()
    print("mla nops", p.nops, "n_inst", p.n_inst)
    return nc


def diag_masks(j):
    pp = np.arange(128)[:, None]
    f = np.arange(512)[None, :]
    dg = np.stack([(128 * r + pp <= f) for r in range(4)]).astype(np.float32)
    low = np.concatenate([dg, np.zeros_like(dg)])
    high = np.concatenate([np.ones_like(dg), dg])
    m = np.stack([low, high]) if j == 0 else np.stack([high, low])
    return m.astype(NPBF)


def mla_inputs(qf, kf, vm, j, ng):
    S = ng * 512
    groups = qgroups(ng, j)
    qsel = np.concatenate([qf[g * 512:(g + 1) * 512] for g in groups], 0)
    return dict(qT=np.ascontiguousarray(qsel.transpose(1, 2, 0)), kT=np.ascontiguousarray(kf.transpose(1, 2, 0)),
                vP=np.ascontiguousarray(vm.reshape(S // 128, 128, 8, 64).transpose(2, 1, 0, 3)), dmask=diag_masks(j))


S5T = 512


def build_s5(nchunks=16):
    nc = _newnc()
    S = nchunks * S5T
    uT = _din(nc, "uT", [256, S])
    a_re = _din(nc, "a_re", [128, 8])
    a_im = _din(nc, "a_im", [128, 8])
    lstep = _din(nc, "lstep", [128, 8])
    b_re = _din(nc, "b_re", [8, 128, 16])
    b_im = _din(nc, "b_im", [8, 128, 16])
    c_re = _din(nc, "c_re", [8, 128, 16])
    c_im = _din(nc, "c_im", [8, 128, 16])
    dsk = _din(nc, "dsk", [128, 2])
    tvals = _din(nc, "tvals", [1, S5T])
    ident = _din(nc, "ident", [128, 128])
    o_y = _dout(nc, "o_y", [256, S])
    p = Prog(nc)
    T = S5T
    def sm(name, w=8):
        return p.sb(name, [128, w], F32)
    are, aim, lst, dt_, lre, th, thc = sm("are"), sm("aim"), sm("lst"), sm("dt_"), sm("lre"), sm("th"), sm("thc")
    p.dma(are[:], a_re[:, :], writes=[are])
    p.dma(aim[:], a_im[:, :], writes=[aim])
    p.dma(lst[:], lstep[:, :], writes=[lst])
    dk = sm("dk", 2)
    p.dma(dk[:], dsk[:, :], writes=[dk])
    idf = p.sb("idf", [128, 128], F32)
    idb = p.sb("idb", [128, 128], BF16)
    p.dma(idf[:], ident[:, :], writes=[idf])
    p.op("dve", lambda e: e.tensor_copy(idb[:], idf[:]), [idf], [idb])
    p.op("act", lambda e: e.activation(dt_[:], lst[:], AF.Exp), [lst], [dt_])
    p.op("dve", lambda e: e.tensor_scalar(lre[:], are[:], -1e-4, None, ALU.min), [are], [lre])
    rmag = sm("rmag")
    tmp = sm("tmp")
    p.op("dve", lambda e: e.tensor_mul(tmp[:], lre[:], dt_[:]), [lre, dt_], [tmp])
    p.op("act", lambda e: e.activation(rmag[:], tmp[:], AF.Exp), [tmp], [rmag])
    p.op("dve", lambda e: e.tensor_mul(th[:], aim[:], dt_[:]), [aim, dt_], [th])
    p.op("dve", lambda e: e.tensor_scalar(thc[:], th[:], 1.0 / (2 * np.pi), None, ALU.mult), [th], [thc])

    def sincos(src_cyc, w, nm):
        t2 = p.sb(nm + "t2", [128, 2, w], F32)
        ti = p.sb(nm + "ti", [128, 2, w], I32)
        tf = p.sb(nm + "tf", [128, 2, w], F32)
        sc = p.sb(nm + "sc", [128, 2, w], F32)
        p.op("dve", lambda e: e.tensor_copy(t2[:, 0, :], src_cyc[:]), [src_cyc], [t2])
        p.op("dve", lambda e: e.tensor_scalar(t2[:, 1, :], src_cyc[:], 0.25, None, ALU.add), [src_cyc], [t2], add=True)
        p.op("dve", lambda e: e.tensor_copy(ti[:], t2[:]), [t2], [ti])
        p.op("dve", lambda e: e.tensor_copy(tf[:], ti[:]), [ti], [tf])
        p.op("dve", lambda e: e.tensor_sub(t2[:], t2[:], tf[:]), [t2, tf], [t2])
        p.op("act", lambda e: e.activation(sc[:], t2[:], AF.Sin, scale=6.28318), [t2], [sc])
        return sc

    sc1 = sincos(thc, 8, "p1")
    abr, abi = sm("abr"), sm("abi")
    p.op("dve", lambda e: e.tensor_mul(abr[:], rmag[:], sc1[:, 1, :]), [rmag, sc1], [abr])
    p.op("dve", lambda e: e.tensor_mul(abi[:], rmag[:], sc1[:, 0, :]), [rmag, sc1], [abi])
    nr, den, t1, t2_, cre, cim, ncim = sm("nr"), sm("den"), sm("t1"), sm("t2_"), sm("cre"), sm("cim"), sm("ncim")
    p.op("dve", lambda e: e.tensor_scalar(nr[:], abr[:], -1.0, None, ALU.add), [abr], [nr])
    p.op("dve", lambda e: e.tensor_mul(den[:], lre[:], lre[:]), [lre], [den])
    p.op("dve", lambda e: e.tensor_mul(t1[:], aim[:], aim[:]), [aim], [t1])
    p.op("dve", lambda e: e.tensor_add(den[:], den[:], t1[:]), [den, t1], [den])
    p.op("dve", lambda e: e.reciprocal(den[:], den[:]), [den], [den])
    p.op("dve", lambda e: e.tensor_mul(t1[:], nr[:], lre[:]), [nr, lre], [t1])
    p.op("dve", lambda e: e.tensor_mul(t2_[:], abi[:], aim[:]), [abi, aim], [t2_])
    p.op("dve", lambda e: e.tensor_add(t1[:], t1[:], t2_[:]), [t1, t2_], [t1])
    p.op("dve", lambda e: e.tensor_mul(cre[:], t1[:], den[:]), [t1, den], [cre])
    p.op("dve", lambda e: e.tensor_mul(t1[:], abi[:], lre[:]), [abi, lre], [t1])
    p.op("dve", lambda e: e.tensor_mul(t2_[:], nr[:], aim[:]), [nr, aim], [t2_])
    p.op("dve", lambda e: e.tensor_sub(t1[:], t1[:], t2_[:]), [t1, t2_], [t1])
    p.op("dve", lambda e: e.tensor_mul(cim[:], t1[:], den[:]), [t1, den], [cim])
    p.op("dve", lambda e: e.tensor_scalar(ncim[:], cim[:], -1.0, None, ALU.mult), [cim], [ncim])

    BTr = p.sb("BTr", [128, 8, 128], BF16)
    BTi = p.sb("BTi", [128, 8, 128], BF16)
    CPr = p.sb("CPr", [128, 8, 128], BF16)
    CPi = p.sb("CPi", [128, 8, 128], BF16)
    BP = [p.sb("BP%d" % i, [128, 128], BF16) for i in range(2)]
    bre_t = p.sb("bre_t", [128, 8, 16], F32)
    bim_t = p.sb("bim_t", [128, 8, 16], F32)
    cre_t = p.sb("cre_t", [128, 8, 16], F32)
    cim_t = p.sb("cim_t", [128, 8, 16], F32)
    for src, dst in ((b_re, bre_t), (b_im, bim_t), (c_re, cre_t), (c_im, cim_t)):
        p.dma(dst[:], src.rearrange("i p c -> p i c"), writes=[dst])
    p.op("pool", lambda e: e.memset(CPr[:], 0.0), [], [CPr])
    p.op("pool", lambda e: e.memset(CPi[:], 0.0), [], [CPi])
    tb = p.sb("tb", [128, 16], F32)
    tr_ps = p.ps("tr_ps", [128, 8, 128], BF16)
    for i in range(8):
        c0 = (i % 4) * 32
        for (half, ps_) in ((0, slice(0, 64)), (1, slice(64, 128))):
            cs = slice(c0 + half * 16, c0 + half * 16 + 16)
            p.op("dve", lambda e: e.tensor_copy(CPr[ps_, i, cs], cre_t[ps_, i, :]), [cre_t], [CPr], add=True)
            p.op("dve", lambda e: e.tensor_scalar(CPi[ps_, i, cs], cim_t[ps_, i, :], -1.0, None, ALU.mult), [cim_t], [CPi], add=True)
        for which, BT in ((0, BTr), (1, BTi)):
            bp = BP[which]
            p.op("pool", lambda e: e.memset(bp[:], 0.0), [], [bp])
            if which == 0:
                p.op("dve", lambda e: e.tensor_scalar(tb[:], bre_t[:, i, :], cre[:, i:i + 1], None, ALU.mult), [bre_t, cre], [tb])
                src2, sc2 = bim_t, ncim
            else:
                p.op("dve", lambda e: e.tensor_scalar(tb[:], bim_t[:, i, :], cre[:, i:i + 1], None, ALU.mult), [bim_t, cre], [tb])
                src2, sc2 = bre_t, cim
            for (half, ps_) in ((0, slice(0, 64)), (1, slice(64, 128))):
                cs = slice(c0 + half * 16, c0 + half * 16 + 16)
                p.op("dve", lambda e: e.scalar_tensor_tensor(bp[ps_, cs], src2[ps_, i, :], sc2[ps_, i:i + 1], tb[ps_, :], ALU.mult, ALU.add),
                     [src2, sc2, tb], [bp], add=True)
            p.op("pe", lambda e: e.transpose(tr_ps[:, i, :], bp[:], idb[:]), [bp, idb], [tr_ps])
            p.op("act", lambda e: e.copy(BT[:, i, :], tr_ps[:, i, :]), [tr_ps], [BT], add=True)

    tv = p.sb("tv", [128, T], F32)
    p.dma(tv[:], tvals[0:1, :].to_broadcast([128, T]), writes=[tv])
    CS = p.sb("CS", [128, 8, 2, T], F32)
    Rt = p.sb("Rt", [128, 8, T], F32)
    ang = p.sb("ang", [128, T], F32)
    t2 = p.sb("rt2", [128, 2, T], F32)
    ti = p.sb("rti", [128, 2, T], I32)
    tf = p.sb("rtf", [128, 2, T], F32)
    for i in range(8):
        p.op("dve", lambda e: e.tensor_scalar(t2[:, 0, :], tv[:], thc[:, i:i + 1], None, ALU.mult), [tv, thc], [t2])
        p.op("dve", lambda e: e.tensor_scalar(t2[:, 1, :], t2[:, 0, :], 0.25, None, ALU.add), [t2], [t2])
        p.op("dve", lambda e: e.tensor_copy(ti[:], t2[:]), [t2], [ti])
        p.op("dve", lambda e: e.tensor_copy(tf[:], ti[:]), [ti], [tf])
        p.op("dve", lambda e: e.tensor_sub(t2[:], t2[:], tf[:]), [t2, tf], [t2])
        p.op("act", lambda e: e.activation(CS[:, i, :, :], t2[:], AF.Sin, scale=6.28318), [t2], [CS], add=True)
        p.op("pool", lambda e: e.memset(Rt[:, i, :], 1.0), [], [Rt], add=True)
        p.op("pool", lambda e: e.tensor_scalar(Rt[:, i, :], Rt[:, i, :], rmag[:, i:i + 1], None, ALU.mult), [Rt, rmag], [Rt], add=True)

    carry = p.sb("carry", [128, 8, 2], F32)
    p.op("dve", lambda e: e.memset(carry[:], 0.0), [], [carry])
    uf = [p.sb("uf%d" % i, [128, 2, T], F32) for i in range(2)]
    ub = [p.sb("ub%d" % i, [128, 2, T], BF16) for i in range(2)]
    NW = 2
    W = {nm: [p.sb("%s%d" % (nm, i), [128, T], F32) for i in range(NW)]
         for nm in ("br", "bi", "m1", "m2", "m3", "m4", "wr", "wi", "sr", "si")}
    Xr = [p.sb("Xr%d" % i, [128, T], BF16) for i in range(4)]
    Xi = [p.sb("Xi%d" % i, [128, T], BF16) for i in range(4)]
    cz = p.sb("cz", [128, 4], F32)
    yo = [p.sb("yo%d" % i, [128, T], F32) for i in range(2)]
    bu_ps = [p.ps("bu_ps%d" % i, [128, 512]) for i in range(4)]
    y_ps = [p.ps("y_ps%d" % i, [128, 512]) for i in range(2)]
    u_v = uT.rearrange("(c p) t -> p c t", p=128)
    o_v = o_y.rearrange("(c p) t -> p c t", p=128)
    unit = 0
    for ch in range(nchunks):
        tsl = slice(ch * T, (ch + 1) * T)
        ufc, ubc = uf[ch % 2], ub[ch % 2]
        p.dma(ufc[:], u_v[:, :, tsl], writes=[ufc])
        p.op("pool", lambda e: e.tensor_copy(ubc[:], ufc[:]), [ufc], [ubc])
        for ct in range(2):
            yp = y_ps[ct]
            for li in range(4):
                i = ct * 4 + li
                w = unit % NW
                unit += 1
                bpr, bpi = bu_ps[(2 * unit) % 4], bu_ps[(2 * unit + 1) % 4]
                p.op("pe", lambda e: e.matmul(bpr[:], BTr[:, i, :], ubc[:, ct, :], start=True, stop=True), [BTr, ubc], [bpr])
                p.op("pe", lambda e: e.matmul(bpi[:], BTi[:, i, :], ubc[:, ct, :], start=True, stop=True), [BTi, ubc], [bpi])
                br, bi_, m1, m2, m3, m4 = W["br"][w], W["bi"][w], W["m1"][w], W["m2"][w], W["m3"][w], W["m4"][w]
                wr, wi_, sr, si = W["wr"][w], W["wi"][w], W["sr"][w], W["si"][w]
                Sn, Cs = CS[:, i, 0, :], CS[:, i, 1, :]
                p.op("act", lambda e: e.copy(br[:], bpr[:]), [bpr], [br])
                p.op("act", lambda e: e.copy(bi_[:], bpi[:]), [bpi], [bi_])
                p.op("dve", lambda e: e.tensor_mul(m1[:], br[:], Cs), [br, CS], [m1])
                p.op("pool", lambda e: e.tensor_mul(m2[:], bi_[:], Sn), [bi_, CS], [m2])
                p.op("dve", lambda e: e.tensor_mul(m3[:], bi_[:], Cs), [bi_, CS], [m3])
                p.op("pool", lambda e: e.tensor_mul(m4[:], br[:], Sn), [br, CS], [m4])
                p.op("pool", lambda e: e.tensor_add(wr[:], m1[:], m2[:]), [m1, m2], [wr])
                p.op("pool", lambda e: e.tensor_sub(wi_[:], m3[:], m4[:]), [m3, m4], [wi_])
                p.op("dve", lambda e: e.tensor_tensor_scan(sr[:], Rt[:, i, :], wr[:], carry[:, i, 0:1], ALU.mult, ALU.add),
                     [Rt, wr, carry], [sr])
                p.op("dve", lambda e: e.tensor_tensor_scan(si[:], Rt[:, i, :], wi_[:], carry[:, i, 1:2], ALU.mult, ALU.add),
                     [Rt, wi_, carry], [si])
                p.op("dve", lambda e: e.tensor_mul(m1[:], sr[:], Cs), [sr, CS], [m1])
                p.op("pool", lambda e: e.tensor_mul(m2[:], si[:], Sn), [si, CS], [m2])
                p.op("dve", lambda e: e.tensor_mul(m3[:], sr[:], Sn), [sr, CS], [m3])
                p.op("pool", lambda e: e.tensor_mul(m4[:], si[:], Cs), [si, CS], [m4])
                xr, xi = Xr[li], Xi[li]
                p.op("dve", lambda e: e.tensor_sub(xr[:], m1[:], m2[:]), [m1, m2], [xr])
                p.op("pool", lambda e: e.tensor_add(xi[:], m3[:], m4[:]), [m3, m4], [xi])
                p.op("dve", lambda e: e.tensor_sub(carry[:, i, 0:1], m1[:, T - 1:T], m2[:, T - 1:T]), [m1, m2], [carry])
                p.op("dve", lambda e: e.tensor_add(carry[:, i, 1:2], m3[:, T - 1:T], m4[:, T - 1:T]), [m3, m4], [carry])
                p.op("pe", lambda e: e.matmul(yp[:], CPr[:, i, :], xr[:], start=(li == 0), stop=False), [CPr, xr], [yp])
                p.op("pe", lambda e: e.matmul(yp[:], CPi[:, i, :], xi[:], start=False, stop=(li == 3)), [CPi, xi], [yp])
            yo_ = yo[ct]
            p.op("dve", lambda e: e.scalar_tensor_tensor(yo_[:], ufc[:, ct, :], dk[:, ct:ct + 1], yp[:], ALU.mult, ALU.add),
                 [ufc, dk, yp], [yo_])
            p.dma(o_v[:, ct, tsl], yo_[:], reads=[yo_])
    p.finish()
    print("s5 nops", p.nops, "n_inst", p.n_inst, "sbuf left", nc.sbuf_bytes_remaining)
    return nc


def s5_inputs(uT, W, j):
    gs = slice(16 * j, 16 * j + 16)
    r8 = lambda a: np.ascontiguousarray(a.reshape(8, 128).T)
    return dict(uT=np.ascontiguousarray(uT), a_re=r8(W["l1_s5_a_re"][gs]), a_im=r8(W["l1_s5_a_im"][gs]),
                lstep=r8(np.repeat(W["l1_s5_log_step"][gs], 64)),
                b_re=np.ascontiguousarray(W["l1_s5_b_re"][gs].reshape(8, 128, 16)),
                b_im=np.ascontiguousarray(W["l1_s5_b_im"][gs].reshape(8, 128, 16)),
                c_re=np.ascontiguousarray(W["l1_s5_c_re"][gs].transpose(0, 2, 1).reshape(8, 128, 16)),
                c_im=np.ascontiguousarray(W["l1_s5_c_im"][gs].transpose(0, 2, 1).reshape(8, 128, 16)),
                dsk=np.ascontiguousarray(W["l1_s5_d"][256 * j:256 * j + 256].reshape(2, 128).T),
                tvals=np.arange(1, S5T + 1, dtype=np.float32)[None, :], ident=np.eye(128, dtype=np.float32))


DSA_K = 256
DSA_NIT = 22
BIG = 1.0e30


def build_dsa(ng=16, nit=DSA_NIT, nhg=4):
    nc = _newnc()
    S = ng * 512
    nlg = ng // 2
    QBs = [(i, r) for i in range(nlg) for r in range(4)]
    nqb = len(QBs)
    nq = nqb * 128
    qiT = _din(nc, "qiT", [8, 64, nq], BF16)
    kiT = _din(nc, "kiT", [64, S], BF16)
    sgn = _din(nc, "sgn", [128, nqb, 8])
    qT = _din(nc, "qT", [8, 64, nq], BF16)
    kT = _din(nc, "kT", [8, 64, S], BF16)
    vP = _din(nc, "vP", [8, 128, S // 128, 64], BF16)
    ident = _din(nc, "ident", [128, 128], BF16)
    cbig = _din(nc, "cbig", [2, 4, 128, 1024])
    p2row = _din(nc, "p2row", [2, nit])
    o_dsa = _dout(nc, "o_dsa", [nq, 512])
    mscr = nc.dram_tensor("mscr", [nqb, 128, S], BF16).ap()
    p = Prog(nc)
    idb = p.sb("idb", [128, 128], BF16)
    CBs = [p.sb("CB%d" % i, [128, 1024], F32) for i in range(2)]
    P2 = p.sb("P2", [128, 2, nit], F32)
    p.dma(idb[:], ident[:, :], writes=[idb])

    for i in range(2):
        p.dma(P2[:, i, :], p2row[i:i + 1, :].to_broadcast([128, nit]), writes=[P2], add=(i > 0))
    kis = p.sb("kis", [64, S], BF16)
    p.dma(kis[:], kiT[:, :], writes=[kis])
    sg = p.sb("sg", [128, nqb, 8], F32)
    p.dma(sg[:], sgn[:, :, :], writes=[sg])
    Score = p.sb("Score", [128, S], F32)
    Mj = p.sb("Mj", [128, S], BF16)
    qis = [p.sb("qis%d" % i, [64, 8, 128], BF16) for i in range(2)]
    Rh = [p.sb("Rh%d" % i, [128, 512], BF16) for i in range(8)]
    Rl = [p.sb("Rl%d" % i, [128, 512], BF16) for i in range(8)]
    Dsg = [p.sb("Dsg%d" % i, [128, 8, 128], BF16) for i in range(2)]
    l_ps = [p.ps("l_ps%d" % i, [128, 512]) for i in range(2)]
    sc_ps = p.ps("sc_ps", [128, 512])
    st = p.sb("st", [128, 8], F32)
    Wt = p.sb("Wt", [128, 2, nit], F32)
    mreg = [Buf(None, "mreg%d" % i) for i in range(nqb)]
    qi_v = qiT.rearrange("h d q -> d h q")
    for lb, (gi, rr) in enumerate(QBs):
        L = (2 * gi + 1) * 512 + (rr + 1) * 128
        nch = 2 * gi + 2
        qs = qis[lb % 2]
        dsg = Dsg[lb % 2]
        p.dma(qs[:], qi_v[:, :, lb * 128:(lb + 1) * 128], writes=[qs])
        for h in range(8):
            p.op(("dve", "pool")[h % 2], lambda e: e.tensor_scalar(dsg[:, h, :], idb[:], sg[:, lb, h:h + 1], None, ALU.mult),
                 [idb, sg], [dsg], add=(h > 0))
        for c in range(nch):
            w = 512 if c < nch - 1 else (rr + 1) * 128
            ks = slice(c * 512, c * 512 + w)
            for h in range(8):
                lp = l_ps[h % 2]
                p.op("pe", lambda e: e.matmul(lp[:, 0:w], qs[:, h, :], kis[:, ks], start=True, stop=True), [qs, kis], [lp])
                p.op("act", lambda e: e.activation(Rh[h][:, 0:w], lp[:, 0:w], AF.Relu), [lp], [Rh[h]])
                p.op("dve", lambda e: e.scalar_tensor_tensor(Rl[h][:, 0:w], lp[:, 0:w], 0.0, Rh[h][:, 0:w], ALU.max, ALU.subtract),
                     [lp, Rh[h]], [Rl[h]])
            for h in range(8):
                p.op("pe", lambda e: e.matmul(sc_ps[:, 0:w], dsg[:, h, :], Rh[h][:, 0:w], start=(h == 0), stop=False),
                     [dsg, Rh[h]], [sc_ps])
                p.op("pe", lambda e: e.matmul(sc_ps[:, 0:w], dsg[:, h, :], Rl[h][:, 0:w], start=False, stop=(h == 7)),
                     [dsg, Rl[h]], [sc_ps])
            p.op("act", lambda e: e.copy(Score[:, ks], sc_ps[:, 0:w]), [sc_ps], [Score], add=(c > 0))
        p.op("dve", lambda e: e.tensor_reduce(st[:, 0:1], Score[:, 0:L], AX.X, ALU.max), [Score], [st])
        p.op("dve", lambda e: e.tensor_reduce(st[:, 1:2], Score[:, 0:L], AX.X, ALU.min), [Score], [st])
        t0_ = 2 * gi * 512
        CB = CBs[lb % 2]
        p.dma(CB[:], cbig[gi % 2, rr, :, :], writes=[CB])
        p.op("dve", lambda e: e.tensor_tensor(Score[:, t0_:L], Score[:, t0_:L], CB[:, 0:L - t0_], ALU.min), [Score, CB], [Score])
        p.op("dve", lambda e: e.tensor_sub(st[:, 4:5], st[:, 0:1], st[:, 1:2]), [st], [st])
        p.op("dve", lambda e: e.tensor_scalar(st[:, 4:5], st[:, 4:5], 2.0, 0.5, ALU.add, ALU.mult), [st], [st])
        p.op("dve", lambda e: e.tensor_scalar(Wt[:, 0, :], P2[:, 0, :], st[:, 4:5], None, ALU.mult), [P2, st], [Wt])
        p.op("dve", lambda e: e.tensor_scalar(Wt[:, 1, :], P2[:, 1, :], st[:, 4:5], None, ALU.mult), [P2, st], [Wt])
        p.op("dve", lambda e: e.scalar_tensor_tensor(st[:, 2:3], st[:, 1:2], -1.0, st[:, 4:5], ALU.add, ALU.add), [st], [st])
        for k in range(nit):
            p.op("dve", lambda e: e.tensor_scalar(Mj[:, 0:L], Score[:, 0:L], st[:, 2:3], None, ALU.is_ge, ALU.add,
                                                   accum_out=st[:, 3:4]), [Score, st], [Mj, st])
            p.op("dve", lambda e: e.tensor_scalar(st[:, 4:5], st[:, 3:4], DSA_K - 0.5, Wt[:, 0, k:k + 1], ALU.is_ge, ALU.mult),
                 [st, Wt], [st])
            p.op("dve", lambda e: e.scalar_tensor_tensor(st[:, 2:3], st[:, 4:5], Wt[:, 1, k:k + 1], st[:, 2:3], ALU.subtract, ALU.add),
                 [st, Wt], [st])
        p.op("dve", lambda e: e.tensor_scalar(Mj[:, 0:L], Score[:, 0:L], st[:, 2:3], None, ALU.is_ge), [Score, st], [Mj])
        p.dma(mscr[lb, :, 0:L], Mj[:, 0:L], reads=[Mj], writes=[mreg[lb]])
    hpg = 8 // nhg
    kTs = p.sb("kTs", [64, hpg, S], BF16)
    qTs = p.sb("qTs", [64, hpg, nq], BF16)
    Va = p.sb("Va", [128, hpg, S // 128, 65], BF16)
    p.op("pool", lambda e: e.memset(Va[:, :, :, 64:65], 1.0), [], [Va])
    NP = 4
    Pe = [p.sb("Pe%d" % i, [128, 512], BF16) for i in range(NP)]
    Pm = [p.sb("Pm%d" % i, [128, 512], BF16) for i in range(NP)]
    PT = [p.sb("PT%d" % i, [128, 4, 128], BF16) for i in range(NP)]
    Mq0 = p.sb("Mq0", [128, S], BF16)
    if nc.sbuf_bytes_remaining >= 2 * S + 1200:
        Mq = [Mq0, p.sb("Mq1", [128, S], BF16)]
    else:
        Mq = [Mq0, Mq0]
    ot = [p.sb("ot%d" % i, [128, 64], F32) for i in range(2)]
    rc = [p.sb("rc%d" % i, [128, 1], F32) for i in range(2)]
    s_ps = [p.ps("s_ps%d" % i, [128, 512]) for i in range(2)] + [l_ps[0], l_ps[1]]
    tp_pss = [p.ps("tp_ps%d" % i, [128, 4, 128], BF16) for i in range(2)]
    o_ps = [p.ps("o_ps0", [128, 512]), sc_ps]
    kT_v = kT.rearrange("h d s -> d h s")
    qT_v = qT.rearrange("h d q -> d h q")
    it = 0
    ep = 0
    mi = 0
    for hg in range(nhg):
        hs0 = hg * hpg
        p.dma(kTs[:], kT_v[:, hs0:hs0 + hpg, :], writes=[kTs])
        p.dma(qTs[:], qT_v[:, hs0:hs0 + hpg, :], writes=[qTs])
        for hh in range(hpg):
            p.dma(Va[:, hh, :, 0:64], vP[hs0 + hh, :, :, :], writes=[Va], add=True)
        for lb, (gi, rr) in enumerate(QBs):
            L = (2 * gi + 1) * 512 + (rr + 1) * 128
            nch = 2 * gi + 2
            mq = Mq[mi % 2]
            mi += 1
            p.dma(mq[:, 0:L], mscr[lb, :, 0:L], reads=[mreg[lb]], writes=[mq])
            for hh in range(hpg):
                h = hs0 + hh
                op_ = o_ps[ep % 2]
                for c in range(nch):
                    w = 512 if c < nch - 1 else (rr + 1) * 128
                    nk = w // 128
                    ks = slice(c * 512, c * 512 + w)
                    sp_ = s_ps[it % 4]
                    tp_ps = tp_pss[it % 2]
                    pe_, pm_, pt_ = Pe[it % NP], Pm[it % NP], PT[it % NP]
                    p.op("pe", lambda e: e.matmul(sp_[:, 0:w], qTs[:, hh, lb * 128:(lb + 1) * 128], kTs[:, hh, ks],
                                                  start=True, stop=True), [qTs, kTs], [sp_])
                    p.op("act", lambda e: e.activation(pe_[:, 0:w], sp_[:, 0:w], AF.Exp), [sp_], [pe_])
                    p.op(("dve", "pool")[it % 2], lambda e: e.tensor_mul(pm_[:, 0:w], pe_[:, 0:w], mq[:, ks]), [pe_, mq], [pm_])
                    for kk in range(nk):
                        p.op("pe", lambda e: e.transpose(tp_ps[:, kk, :], pm_[:, kk * 128:(kk + 1) * 128], idb[:]), [pm_, idb], [tp_ps])
                    if it % 2 == 0:
                        p.op("act", lambda e: e.copy(pt_[:, 0:nk, :], tp_ps[:, 0:nk, :]), [tp_ps], [pt_])
                    else:
                        p.op("dve", lambda e: e.tensor_copy(pt_[:, 0:nk, :], tp_ps[:, 0:nk, :]), [tp_ps], [pt_])
                    for kk in range(nk):
                        p.op("pe", lambda e: e.matmul(op_[:, 0:65], pt_[:, kk, :], Va[:, hh, c * 4 + kk, :],
                                                      start=(c == 0 and kk == 0), stop=(c == nch - 1 and kk == nk - 1)),
                             [pt_, Va], [op_])
                    it += 1
                o_, r_ = ot[ep % 2], rc[ep % 2]
                p.op("dve", lambda e: e.reciprocal(r_[:], op_[:, 64:65]), [op_], [r_])
                p.op("dve", lambda e: e.tensor_scalar(o_[:], op_[:, 0:64], r_[:, 0:1], None, ALU.mult), [op_, r_], [o_])
                p.dma(o_dsa[lb * 128:(lb + 1) * 128, h * 64:(h + 1) * 64], o_[:], reads=[o_])
                ep += 1
    p.finish()
    print("dsa nops", p.nops, "n_inst", p.n_inst, "sbuf left", nc.sbuf_bytes_remaining)
    return nc


def dsa_consts(nit, j):
    q = np.arange(128)[:, None]
    k = np.arange(128)[None, :]
    dg = np.where(k <= q, BIG, -BIG).astype(np.float32)
    pos = np.full((128, 128), BIG, np.float32)
    neg = -pos
    cb = np.zeros((2, 4, 128, 1024), np.float32)
    for r in range(4):
        low = [pos] * r + [dg] + [neg] * (3 - r) + [neg] * 4
        high = [pos] * 4 + [pos] * r + [dg] + [neg] * (3 - r)
        lo_, hi_ = np.concatenate(low, 1), np.concatenate(high, 1)
        cb[0, r], cb[1, r] = (lo_, hi_) if j == 0 else (hi_, lo_)
    wk = 0.5 ** np.arange(nit)
    bk = wk * 0.5
    bk[-1] = wk[-1]
    return cb, np.stack([wk, bk]).astype(np.float32)


def dsa_inputs(dq, dk, dv, qi, ki, sg, j, ng, nit=DSA_NIT):
    S = ng * 512
    groups = qgroups(ng, j)
    sel = np.concatenate([np.arange(g * 512, (g + 1) * 512) for g in groups])
    nqb = len(sel) // 128
    cb, p2 = dsa_consts(nit, j)
    T8 = lambda a: np.ascontiguousarray(a.reshape(a.shape[0], 8, 64).transpose(1, 2, 0))
    return dict(qiT=T8(qi[sel]), kiT=np.ascontiguousarray(ki.T), sgn=np.ascontiguousarray(sg[sel].reshape(nqb, 128, 8).transpose(1, 0, 2)),
                qT=T8(dq[sel]), kT=T8(dk), vP=np.ascontiguousarray(dv.reshape(S // 128, 128, 8, 64).transpose(2, 1, 0, 3)),
                ident=np.eye(128, dtype=np.float32).astype(NPBF), cbig=cb, p2row=p2)


def _cat_batch(res, key, b):
    return np.concatenate([np.asarray(res[2 * b][key]), np.asarray(res[2 * b + 1][key])], 0)


def _tokens_of(j, ng=16):
    return np.concatenate([np.arange(g * 512, (g + 1) * 512) for g in qgroups(ng, j)])


def kernel(**inputs):
    W = {k: np.asarray(v) for k, v in inputs.items()}
    x = W["x"].astype(np.float32)
    positions = W["positions"]
    B = 4
    A0 = run_ka0(x, positions, W)
    tri, upp = gla_consts()
    a0 = {k: [_cat_batch(A0, "o_" + k, b) for b in range(B)] for k in ("gq", "gk", "gv", "ga", "sr", "qf", "kf", "vm")}
    del A0
    in_maps = []
    for c in range(NCORES):
        b, j = c // 2, c % 2
        hs = slice(128 * j, 128 * j + 128)
        vs = slice(256 * j, 256 * j + 256)
        in_maps.append(dict(
            g2=np.ascontiguousarray(a0["ga"][b][:, hs]),
            qT2=np.ascontiguousarray(a0["gq"][b][:, hs].reshape(SEQ, 2, 64).transpose(2, 1, 0)),
            kT2=np.ascontiguousarray(a0["gk"][b][:, hs].reshape(SEQ, 2, 64).transpose(2, 1, 0)),
            k2=np.ascontiguousarray(a0["gk"][b][:, hs]), v2=np.ascontiguousarray(a0["gv"][b][:, vs]),
            sr2=np.ascontiguousarray(a0["sr"][b][:, vs]), gain=np.ascontiguousarray(W["l0_gla_norm"][None, vs]),
            tri=tri, upp=upp))
    G = _run(build_gla(128), in_maps)
    o_gla = [np.concatenate([np.asarray(G[2 * b]["o_gla"]), np.asarray(G[2 * b + 1]["o_gla"])], 1) for b in range(B)]
    del G
    in_maps = []
    for c in range(NCORES):
        b, j = c // 2, c % 2
        in_maps.append(mla_inputs(a0["qf"][b].reshape(SEQ, 8, 96), a0["kf"][b].reshape(SEQ, 8, 96),
                                  a0["vm"][b].reshape(SEQ, 8, 64), j, 16))
    M = _run(build_mla(16), in_maps)
    o_mlaT = [np.zeros((512, SEQ), np.float32) for _ in range(B)]
    for c in range(NCORES):
        b, j = c // 2, c % 2
        o_mlaT[b][:, _tokens_of(j)] = np.asarray(M[c]["o_T"]).reshape(512, TOK)
    del M, a0
    xf = x.reshape(B * SEQ, D)
    in_maps = []
    for c in range(NCORES):
        b, hf = c // 2, c % 2
        ts = slice(hf * TOK, (hf + 1) * TOK)
        mixT = np.concatenate([o_gla[b][ts].T, o_mlaT[b][:, ts]], 0)
        in_maps.append(kc_inputs(xf[c * TOK:(c + 1) * TOK], mixT, W, "l0_"))
    C0 = _run(build_kc(False, 8), in_maps)
    x2 = [np.asarray(C0[c]["o_x"]) for c in range(NCORES)]
    del C0, o_gla, o_mlaT
    in_maps = [dict(xT=np.ascontiguousarray(x2[c].T), w_in=W["l1_w_in"]) for c in range(NCORES)]
    A1 = _run(build_ka1(8), in_maps)
    a1 = {k: [_cat_batch(A1, "o_" + k, b) for b in range(B)] for k in ("dq", "dk", "dv", "qi", "ki", "sg", "u")}
    del A1
    in_maps = []
    for c in range(NCORES):
        b, j = c // 2, c % 2
        in_maps.append(dsa_inputs(a1["dq"][b], a1["dk"][b], a1["dv"][b], a1["qi"][b], a1["ki"][b], a1["sg"][b], j, 16))
    Dr = _run(build_dsa(16), in_maps)
    o_dsa = [np.zeros((SEQ, 512), np.float32) for _ in range(B)]
    for c in range(NCORES):
        b, j = c // 2, c % 2
        o_dsa[b][_tokens_of(j)] = np.asarray(Dr[c]["o_dsa"])
    del Dr
    in_maps = []
    for c in range(NCORES):
        b, j = c // 2, c % 2
        in_maps.append(s5_inputs(a1["u"][b][:, 256 * j:256 * j + 256].T, W, j))
    Sr = _run(build_s5(16), in_maps)
    yT = [np.concatenate([np.asarray(Sr[2 * b]["o_y"]), np.asarray(Sr[2 * b + 1]["o_y"])], 0) for b in range(B)]
    del Sr, a1
    in_maps = []
    for c in range(NCORES):
        b, hf = c // 2, c % 2
        ts = slice(hf * TOK, (hf + 1) * TOK)
        in_maps.append(kc_inputs(x2[c], o_dsa[b][ts].T, W, "l1_", glu_in=yT[b][:, ts]))
    C1 = _run(build_kc(True, 8), in_maps)
    out = np.concatenate([np.asarray(C1[c]["o_x"]) for c in range(NCORES)], 0)
    return out.reshape(B, SEQ, D).astype(np.float32)
```

```python
import numpy as np
import ml_dtypes
import concourse.bass as bass
import concourse.mybir as mybir
from concourse.bass_utils import run_bass_kernel_spmd

F32 = mybir.dt.float32
BF16 = mybir.dt.bfloat16
I32 = mybir.dt.int32
AF = mybir.ActivationFunctionType
ALU = mybir.AluOpType
AX = mybir.AxisListType
NPBF = ml_dtypes.bfloat16

NCORES = 8
D = 1024
SEQ = 8192
TOK = 4096
LN_EPS = 1e-5
ALPHA = 4 ** 0.25


class Buf:
    __slots__ = ("t", "w", "wb", "r", "pr", "name", "psum")

    def __init__(self, t, name="", psum=False):
        self.psum = psum
        self.t = t
        self.w = {}
        self.wb = {}
        self.r = {}
        self.pr = {}
        self.name = name

    def __getitem__(self, idx):
        return self.t[idx]


class Prog:
    NDMA = 14

    def __init__(self, nc):
        self.nc = nc
        self.E = {"pe": nc.tensor, "act": nc.scalar, "dve": nc.vector,
                  "pool": nc.gpsimd, "sp": nc.sync}
        self.sem = {}
        self.cnt = {}
        for k in self.E:
            self.sem[k] = nc.alloc_semaphore("s_" + k)
            self.cnt[k] = 0
        for i in range(self.NDMA):
            k = "d%d" % i
            self.sem[k] = nc.alloc_semaphore("s_" + k)
            self.cnt[k] = 0
        self.waited = {}
        self.dma_rr = 0
        self.n_inst = 0
        self.dq = 0

    def sb(self, name, shape, dt):
        return Buf(self.nc.alloc_sbuf_tensor(name, list(shape), dt), name)

    def ps(self, name, shape, dt=F32):
        return Buf(self.nc.alloc_psum_tensor(name, list(shape), dt), name, psum=True)

    def _need(self, eng, deps):
        q = self.E[eng]
        for (k, v) in deps.items():
            if k == "pe" and eng == "pe":
                continue
            if self.waited.get((eng, k), 0) < v:
                q.wait_ge(self.sem[k], v)
                self.waited[(eng, k)] = v
                self.n_inst += 1

    @staticmethod
    def _mx(m, k, v):
        if m.get(k, 0) < v:
            m[k] = v

    def _deps(self, reads, writes, add):
        m = {}
        for b in reads:
            for k, v in b.w.items():
                self._mx(m, k, v)
            if b.psum:
                for k, v in b.r.items():
                    self._mx(m, k, v)
        for b in writes:
            for k, v in (b.wb if add else b.w).items():
                self._mx(m, k, v)
            for k, v in b.r.items():
                self._mx(m, k, v)
            if add:
                for k, v in b.pr.items():
                    self._mx(m, k, v)
        return m

    def _commit(self, key, val, reads, writes, add):
        for b in reads:
            b.r[key] = val
        for b in writes:
            if add:
                b.w[key] = val
            else:
                b.w = {key: val}
                b.wb = {key: val}
                b.pr = b.r
                b.r = {}

    def op(self, eng, fn, reads=(), writes=(), add=False):
        self.nops = getattr(self, "nops", 0) + 1
        if self.nops > getattr(self, "limit", 10 ** 9):
            return None
        self._need(eng, self._deps(reads, writes, add))
        ins = fn(self.E[eng])
        self.cnt[eng] += 1
        ins.then_inc(self.sem[eng], 1)
        self._commit(eng, self.cnt[eng], reads, writes, add)
        self.n_inst += 1
        return ins

    def dma(self, out, in_, reads=(), writes=(), q=None, add=False, **kw):
        if q is None:
            q = ("sp", "sp")[self.dq % 2]
            self.dq += 1
        i = self.dma_rr
        self.dma_rr = (self.dma_rr + 1) % self.NDMA
        k = "d%d" % i
        deps = self._deps(reads, writes, add)
        if self.cnt[k] > 0:
            self._mx(deps, k, self.cnt[k])
        self._need(q, deps)
        ins = self.E[q].dma_start(out=out, in_=in_, **kw)
        self.cnt[k] += 16
        ins.then_inc(self.sem[k], 16)
        self._commit(k, self.cnt[k], reads, writes, add)
        self.n_inst += 1
        return ins

    def finish(self):
        deps = {"d%d" % i: self.cnt["d%d" % i] for i in range(self.NDMA)
                if self.cnt["d%d" % i] > 0}
        self._need("sp", deps)


def _newnc():
    return bass.Bass("TRN2", target_bir_lowering=False)


def _din(nc, name, shape, dt=F32):
    return nc.dram_tensor(name, list(shape), dt, kind="ExternalInput").ap()


def _dout(nc, name, shape, dt=F32):
    return nc.dram_tensor(name, list(shape), dt, kind="ExternalOutput").ap()


def _run(nc, in_maps):
    res = run_bass_kernel_spmd(nc, in_maps, core_ids=list(range(NCORES)))
    return res.results


def load_cast(p, dst_bf, dst_ap, src_ap, stage, stage_ap, eng="pool"):
    p.dma(stage_ap, src_ap, writes=[stage])
    p.op(eng, lambda e: e.tensor_copy(dst_ap, stage_ap), [stage], [dst_bf])


L0_IN = 1968


def build_ka0():
    nc = _newnc()
    xT = _din(nc, "xT", [D, TOK])
    posl = _din(nc, "posl", [128, 32], I32)
    w_in = _din(nc, "w_in", [D, L0_IN])
    wg2 = _din(nc, "wg2", [16, 256])
    bg = _din(nc, "bg", [1, 256])
    qn = _din(nc, "qn", [128, 2])
    w_uq = _din(nc, "w_uq", [256, 768])
    kvn = _din(nc, "kvn", [128, 1])
    w_ukv = _din(nc, "w_ukv", [128, 1024])
    invf = _din(nc, "invf", [1, 128])
    o_gq = _dout(nc, "o_gq", [TOK, 256], BF16)
    o_gk = _dout(nc, "o_gk", [TOK, 256], BF16)
    o_gv = _dout(nc, "o_gv", [TOK, 512], BF16)
    o_ga = _dout(nc, "o_ga", [TOK, 256], F32)
    o_sr = _dout(nc, "o_sr", [TOK, 512], F32)
    o_qf = _dout(nc, "o_qf", [TOK, 768], BF16)
    o_kf = _dout(nc, "o_kf", [TOK, 768], BF16)
    o_vm = _dout(nc, "o_vm", [TOK, 512], BF16)
    p = Prog(nc)
    import os
    p.limit = int(os.environ.get('KA0_LIM', '1000000000'))

    winb = p.sb("winb", [128, 8, L0_IN], BF16)
    wst = [p.sb("wst%d" % i, [128, L0_IN], F32) for i in range(2)]
    w_in_v = w_in.rearrange("(c p) n -> p c n", p=128)
    for c in range(8):
        st = wst[c % 2]
        p.dma(st[:], w_in_v[:, c, :], writes=[st])
        p.op(("pool", "dve")[c % 2], lambda e: e.tensor_copy(winb[:, c, :], st[:]), [st], [winb], add=True)
    qn_s = p.sb("qn_s", [128, 2], F32)
    kvn_s = p.sb("kvn_s", [128, 1], F32)
    p.dma(qn_s[:], qn[:, :], writes=[qn_s])
    p.dma(kvn_s[:], kvn[:, :], writes=[kvn_s])
    wuqb = p.sb("wuqb", [128, 2, 768], BF16)
    wukvb = p.sb("wukvb", [128, 1024], BF16)
    st = wst[0]
    p.dma(st[:, 0:1536].rearrange("p (c n) -> p c n", c=2), w_uq.rearrange("(c p) n -> p c n", p=128), writes=[st])
    for c in range(2):
        p.op("dve", lambda e: e.tensor_scalar(wuqb[:, c, :], st[:, c * 768:(c + 1) * 768], qn_s[:, c:c + 1],
                                               96 ** -0.5, ALU.mult, ALU.mult), [st, qn_s], [wuqb])
    st = wst[1]
    p.dma(st[:, 0:1024], w_ukv[:, :], writes=[st])
    p.op("dve", lambda e: e.tensor_scalar(wukvb[:], st[:, 0:1024], kvn_s[:, 0:1], None, ALU.mult), [st, kvn_s], [wukvb])
    wg2s = p.sb("wg2s", [16, 256], F32)
    wg2b = p.sb("wg2b", [16, 256], BF16)
    bgs = p.sb("bgs", [1, 256], F32)
    bgb = p.sb("bgb", [1, 256], BF16)
    onesb = p.sb("onesb", [1, 128], BF16)
    p.dma(wg2s[:], wg2[:, :], writes=[wg2s])
    p.dma(bgs[:], bg[:, :], writes=[bgs])
    p.op("dve", lambda e: e.tensor_copy(wg2b[:], wg2s[:]), [wg2s], [wg2b])
    p.op("dve", lambda e: e.tensor_copy(bgb[:], bgs[:]), [bgs], [bgb])
    p.op("dve", lambda e: e.memset(onesb[:], 1.0), [], [onesb])
    invf8 = p.sb("invf8", [128, 128], F32)
    p.dma(invf8[:], invf[0:1, :].to_broadcast([128, 128]), writes=[invf8])
    p.op("dve", lambda e: e.tensor_scalar(invf8[:], invf8[:], 1.0 / (2 * np.pi), None, ALU.mult), [invf8], [invf8])
    posi = p.sb("posi", [128, 32], I32)
    posf = p.sb("posf", [128, 32], F32)
    p.dma(posi[:], posl[:, :], writes=[posi])
    p.op("dve", lambda e: e.tensor_copy(posf[:], posi[:]), [posi], [posf])

    xs = [p.sb("xs%d" % i, [128, 8, 512], F32) for i in range(2)]
    xb = [p.sb("xb%d" % i, [128, 8, 512], BF16) for i in range(2)]
    fmb = [p.sb("fmb%d" % i, [128, 4, 512], BF16) for i in range(2)]
    hps = [p.ps("hps%d" % i, [128, 512]) for i in range(4)]
    fps = [p.ps("fps%d" % i, [128, 512]) for i in range(2)]
    sps = [p.ps("sps%d" % i, [128, 512]) for i in range(2)]
    NB = 2
    gq_t = [p.sb("gq_t%d" % i, [128, 256], BF16) for i in range(NB)]
    gk_t = [p.sb("gk_t%d" % i, [128, 256], BF16) for i in range(NB)]
    gv_t = [p.sb("gv_t%d" % i, [128, 512], BF16) for i in range(NB)]
    ga_t = [p.sb("ga_t%d" % i, [128, 256], F32) for i in range(NB)]
    sr_t = [p.sb("sr_t%d" % i, [128, 512], F32) for i in range(NB)]
    qf_t = [p.sb("qf_t%d" % i, [128, 8, 96], BF16) for i in range(NB)]
    kf_t = [p.sb("kf_t%d" % i, [128, 8, 96], BF16) for i in range(NB)]
    vm_t = [p.sb("vm_t%d" % i, [128, 8, 64], BF16) for i in range(NB)]
    junk = p.sb("junk", [128, 256], F32)
    ss = p.sb("ss", [128, 2], F32)
    rstd = p.sb("rstd", [128, 2], F32)
    t2 = p.sb("t2", [128, 2, 128], F32)
    ti = p.sb("ti", [128, 2, 128], I32)
    tf = p.sb("tf", [128, 2, 128], F32)
    sc = p.sb("sc", [128, 2, 128], F32)
    Qs = p.sb("Qs", [128, 8, 96], F32)
    ra = p.sb("ra", [128, 8, 16], F32)
    rb = p.sb("rb", [128, 8, 16], F32)
    kr = p.sb("kr", [128, 32], F32)
    kro = p.sb("kro", [128, 32], F32)
    ez = p.sb("ez", [128, 256], F32)

    xT_v = xT.rearrange("(c p) t -> p c t", p=128)
    fm_cols = [(1552, 128), (1680, 128), (1808, 128), (1024, 16)]
    import os
    for g in range(int(os.environ.get('KA0_G', '8'))):
        xsg, xbg, fm = xs[g % 2], xb[g % 2], fmb[g % 2]
        for c in range(8):
            p.dma(xsg[:, c, :], xT_v[:, c, g * 512:(g + 1) * 512], writes=[xsg], add=(c > 0))
        for c in range(8):
            p.op(("pool", "dve")[c % 2], lambda e: e.tensor_copy(xbg[:, c, :], xsg[:, c, :]), [xsg], [xbg], add=(c > 0))
        for mi, (c0, m) in enumerate(fm_cols):
            fp_ = fps[mi % 2]
            for c in range(8):
                p.op("pe", lambda e: e.matmul(fp_[0:m, :], winb[:, c, c0:c0 + m], xbg[:, c, :],
                                              start=(c == 0), stop=(c == 7)), [winb, xbg], [fp_])
            p.op("act", lambda e: e.copy(fm[0:m, mi, :], fp_[0:m, :]), [fp_], [fm])
        for s in range(int(os.environ.get('KA0_S', '4'))):
            sub = g * 4 + s
            b = sub % NB
            tsl = slice(s * 128, (s + 1) * 128)
            for bi in range(4):
                n0 = bi * 512
                n = min(512, L0_IN - n0)
                for c in range(8):
                    p.op("pe", lambda e: e.matmul(hps[bi][:, 0:n], xbg[:, c, tsl], winb[:, c, n0:n0 + n],
                                                  start=(c == 0), stop=(c == 7)), [xbg, winb], [hps[bi]])
            if os.environ.get('KA0_NOROPE', '0') == '1':
                p.op('act', lambda e: e.activation(sc[:], posf[:, 0:1].to_broadcast([128, 256]).rearrange('p (a b) -> p a b', a=2), AF.Copy), [posf], [sc])
            else:
                p.op("dve", lambda e: e.tensor_scalar(t2[:, 0, :], invf8[:], posf[:, sub:sub + 1], None, ALU.mult),
                     [invf8, posf], [t2])
                p.op("dve", lambda e: e.tensor_scalar(t2[:, 1, :], t2[:, 0, :], 0.25, None, ALU.add), [t2], [t2])
                p.op("dve", lambda e: e.tensor_copy(ti[:], t2[:]), [t2], [ti])
                p.op("dve", lambda e: e.tensor_copy(tf[:], ti[:]), [ti], [tf])
                p.op("dve", lambda e: e.tensor_sub(t2[:], t2[:], tf[:]), [t2, tf], [t2])
                p.op("act", lambda e: e.activation(sc[:], t2[:], AF.Sin, scale=6.28318), [t2], [sc])
            p.op("act", lambda e: e.mul(gq_t[b][:], hps[0][:, 0:256], 0.125), [hps[0]], [gq_t[b]])
            if os.environ.get('KA0_V', '0') == '1':
                p.op("act", lambda e: e.copy(gk_t[b][:], hps[0][:, 256:512]), [hps[0]], [gk_t[b]])
            elif os.environ.get('KA0_V', '0') == '4':
                p.op("dve", lambda e: e.tensor_copy(gk_t[b][:], hps[0][:, 256:512]), [hps[0], gq_t[b]], [gk_t[b]])
            elif os.environ.get('KA0_V', '0') == '2':
                p.op("pool", lambda e: e.tensor_copy(kr[:], kr[:]), [kr], [kr])
            else:
                p.op("dve", lambda e: e.tensor_copy(gk_t[b][:], hps[0][:, 256:512]), [hps[0]], [gk_t[b]])
            p.op("act", lambda e: e.copy(gv_t[b][:], hps[1][:, :]), [hps[1]], [gv_t[b]])
            p.op("act", lambda e: e.activation(sr_t[b][:, 0:496], hps[2][:, 16:512], AF.Silu), [hps[2]], [sr_t[b]])
            p.op("act", lambda e: e.activation(sr_t[b][:, 496:512], hps[3][:, 0:16], AF.Silu), [hps[3]], [sr_t[b]])
            p.op("act", lambda e: e.activation(junk[:, 0:256], hps[3][:, 16:272], AF.Square, accum_out=ss[:, 0:1]),
                 [hps[3]], [junk, ss])
            p.op("act", lambda e: e.activation(junk[:, 0:128], hps[3][:, 272:400], AF.Square, accum_out=ss[:, 1:2]),
                 [hps[3]], [junk, ss])
            p.op("dve", lambda e: e.tensor_scalar(rstd[:, 0:1], ss[:, 0:1], 1.0 / 256, LN_EPS, ALU.mult, ALU.add), [ss], [rstd])
            p.op("dve", lambda e: e.tensor_scalar(rstd[:, 1:2], ss[:, 1:2], 1.0 / 128, LN_EPS, ALU.mult, ALU.add), [ss], [rstd])
            p.op("act", lambda e: e.sqrt(rstd[:], rstd[:]), [rstd], [rstd])
            p.op("dve", lambda e: e.reciprocal(rstd[:], rstd[:]), [rstd], [rstd])
            p.op("dve", lambda e: e.tensor_copy(kr[:], hps[3][:, 400:432]), [hps[3]], [kr])
            for bi, (n0, n) in enumerate(((0, 512), (512, 256))):
                for c in range(2):
                    p.op("pe", lambda e: e.matmul(sps[bi][:, 0:n], fm[:, c, tsl], wuqb[:, c, n0:n0 + n],
                                                  start=(c == 0), stop=(c == 1)), [fm, wuqb], [sps[bi]])
            Qf = Qs[:].rearrange("p h d -> p (h d)")
            p.op("act", lambda e: e.activation(Qf[:, 0:512], sps[0][:, :], AF.Copy, scale=rstd[:, 0:1]), [sps[0], rstd], [Qs])
            p.op("act", lambda e: e.activation(Qf[:, 512:768], sps[1][:, 0:256], AF.Copy, scale=rstd[:, 0:1]), [sps[1], rstd], [Qs])
            qf = qf_t[b]
            p.op("pool", lambda e: e.tensor_copy(qf[:, :, 0:64], Qs[:, :, 0:64]), [Qs], [qf])
            sin8 = sc[:, 0, :].rearrange("p (h j) -> p h j", h=8)
            cos8 = sc[:, 1, :].rearrange("p (h j) -> p h j", h=8)
            p.op("dve", lambda e: e.tensor_mul(ra[:], Qs[:, :, 64:80], cos8), [Qs, sc], [ra])
            p.op("pool", lambda e: e.tensor_mul(rb[:], Qs[:, :, 80:96], sin8), [Qs, sc], [rb])
            p.op("dve", lambda e: e.tensor_sub(qf[:, :, 64:80], ra[:], rb[:]), [ra, rb], [qf])
            p.op("dve", lambda e: e.tensor_mul(ra[:], Qs[:, :, 80:96], cos8), [Qs, sc], [ra])
            p.op("pool", lambda e: e.tensor_mul(rb[:], Qs[:, :, 64:80], sin8), [Qs, sc], [rb])
            p.op("dve", lambda e: e.tensor_add(qf[:, :, 80:96], ra[:], rb[:]), [ra, rb], [qf])
            for bi in range(2):
                p.op("pe", lambda e: e.matmul(sps[bi][:, :], fm[:, 2, tsl], wukvb[:, bi * 512:(bi + 1) * 512],
                                              start=True, stop=True), [fm, wukvb], [sps[bi]])
            kf, vm = kf_t[b], vm_t[b]
            for bi in range(2):
                src = sps[bi][:, :].rearrange("p (h d) -> p h d", h=4)
                p.op("act", lambda e: e.activation(kf[:, bi * 4:(bi + 1) * 4, 0:64], src[:, :, 0:64], AF.Copy,
                                                   scale=rstd[:, 1:2]), [sps[bi], rstd], [kf])
                p.op("act", lambda e: e.activation(vm[:, bi * 4:(bi + 1) * 4, :], src[:, :, 64:128], AF.Copy,
                                                   scale=rstd[:, 1:2]), [sps[bi], rstd], [vm])
            s16, c16 = sc[:, 0, 0:16], sc[:, 1, 0:16]
            p.op("dve", lambda e: e.tensor_mul(ra[:, 0, :], kr[:, 0:16], c16), [kr, sc], [ra])
            p.op("dve", lambda e: e.tensor_mul(rb[:, 0, :], kr[:, 16:32], s16), [kr, sc], [rb])
            p.op("dve", lambda e: e.tensor_sub(kro[:, 0:16], ra[:, 0, :], rb[:, 0, :]), [ra, rb], [kro])
            p.op("dve", lambda e: e.tensor_mul(ra[:, 0, :], kr[:, 16:32], c16), [kr, sc], [ra])
            p.op("dve", lambda e: e.tensor_mul(rb[:, 0, :], kr[:, 0:16], s16), [kr, sc], [rb])
            p.op("dve", lambda e: e.tensor_add(kro[:, 16:32], ra[:, 0, :], rb[:, 0, :]), [ra, rb], [kro])
            p.op("pool", lambda e: e.tensor_copy(kf[:, :, 64:96], kro[:].unsqueeze(1).to_broadcast([128, 8, 32])), [kro], [kf])
            p.op("pe", lambda e: e.matmul(sps[0][:, 0:256], fm[0:16, 3, tsl], wg2b[:], start=True, stop=False), [fm, wg2b], [sps[0]])
            p.op("pe", lambda e: e.matmul(sps[0][:, 0:256], onesb[:], bgb[:], start=False, stop=True), [onesb, bgb], [sps[0]])
            p.op("act", lambda e: e.activation(ez[:], sps[0][:, 0:256], AF.Exp, scale=-1.0), [sps[0]], [ez])
            p.op("act", lambda e: e.activation(ez[:], ez[:], AF.Ln, bias=1.0), [ez], [ez])
            p.op("pool", lambda e: e.tensor_scalar(ga_t[b][:], ez[:], -1.0 / 16, None, ALU.mult), [ez], [ga_t[b]])
            r0 = sub * 128
            p.dma(o_gq[r0:r0 + 128, :], gq_t[b][:], reads=[gq_t[b]])
            p.dma(o_gk[r0:r0 + 128, :], gk_t[b][:], reads=[gk_t[b]])
            p.dma(o_gv[r0:r0 + 128, :], gv_t[b][:], reads=[gv_t[b]])
            p.dma(o_ga[r0:r0 + 128, :], ga_t[b][:], reads=[ga_t[b]])
            p.dma(o_sr[r0:r0 + 128, :], sr_t[b][:], reads=[sr_t[b]])
            p.dma(o_qf[r0:r0 + 128, :], qf[:].rearrange("p h d -> p (h d)"), reads=[qf])
            p.dma(o_kf[r0:r0 + 128, :], kf[:].rearrange("p h d -> p (h d)"), reads=[kf])
            p.dma(o_vm[r0:r0 + 128, :], vm[:].rearrange("p h d -> p (h d)"), reads=[vm])
    p.finish()
    print('ka0 nops', p.nops, 'n_inst', p.n_inst)
    return nc


def invfreq_const():
    half = 16
    inv = (10000.0 ** (-np.arange(half, dtype=np.float32) / half)).astype(np.float32)
    return np.tile(inv, 8)[None, :].astype(np.float32)


def run_ka0(x, positions, W):
    xf = x.reshape(32768, D)
    in_maps = []
    for c in range(NCORES):
        xs = xf[c * TOK:(c + 1) * TOK]
        pos = positions.reshape(-1)[c * TOK:(c + 1) * TOK]
        in_maps.append(dict(
            xT=np.ascontiguousarray(xs.T),
            posl=np.ascontiguousarray(pos.reshape(32, 128).T),
            w_in=W["l0_w_in"], wg2=W["l0_gla_wg2"], bg=W["l0_gla_bg"].reshape(1, 256),
            qn=np.ascontiguousarray(W["l0_mla_q_norm"].reshape(2, 128).T),
            w_uq=W["l0_mla_w_uq"], kvn=W["l0_mla_kv_norm"].reshape(128, 1),
            w_ukv=W["l0_mla_w_ukv"], invf=invfreq_const()))
    return _run(build_ka0(), in_maps)


def _layernorm_tile(p, v, junkb, st, G, B, out_t):
    p.op("act", lambda e: e.activation(junkb[:], v[:], AF.Copy, accum_out=st[:, 0:1]), [v], [junkb, st])
    p.op("act", lambda e: e.activation(junkb[:], v[:], AF.Square, accum_out=st[:, 1:2]), [v], [junkb, st])
    p.op("dve", lambda e: e.tensor_scalar(st[:, 2:3], st[:, 0:1], 1.0 / D, None, ALU.mult), [st], [st])
    p.op("dve", lambda e: e.tensor_mul(st[:, 3:4], st[:, 2:3], st[:, 2:3]), [st], [st])
    p.op("dve", lambda e: e.scalar_tensor_tensor(st[:, 4:5], st[:, 1:2], 1.0 / D, st[:, 3:4], ALU.mult, ALU.subtract), [st], [st])
    p.op("dve", lambda e: e.tensor_scalar(st[:, 4:5], st[:, 4:5], LN_EPS, None, ALU.add), [st], [st])
    p.op("act", lambda e: e.sqrt(st[:, 4:5], st[:, 4:5]), [st], [st])
    p.op("dve", lambda e: e.reciprocal(st[:, 4:5], st[:, 4:5]), [st], [st])
    p.op("dve", lambda e: e.tensor_scalar(v[:], v[:], st[:, 2:3], st[:, 4:5], ALU.subtract, ALU.mult), [v, st], [v])
    p.op("pool", lambda e: e.tensor_mul(v[:], v[:], G[:]), [v, G], [v])
    p.op("dve", lambda e: e.tensor_add(out_t[:], v[:], B[:]), [v, B], [out_t])


def build_kc(glu=False, ngroups=8):
    nc = _newnc()
    ntok = ngroups * 512
    x = _din(nc, "x", [ntok, D])
    mixT = _din(nc, "mixT", [512 if glu else D, ntok])
    w_out = _din(nc, "w_out", [D, D])
    lnp = _din(nc, "lnp", [4, D])
    w1 = _din(nc, "w1", [D, 4 * D])
    w2 = _din(nc, "w2", [4 * D, D])
    ident = _din(nc, "ident", [128, 128])
    if glu:
        yT = _din(nc, "yT", [512, ntok])
        glu_w = _din(nc, "glu_w", [512, 512])
        glu_b = _din(nc, "glu_b", [128, 4])
    o_x = _dout(nc, "o_x", [ntok, D])
    p = Prog(nc)

    idf = p.sb("idf", [128, 128], F32)
    idb = p.sb("idb", [128, 128], BF16)
    p.dma(idf[:], ident[:, :], writes=[idf])
    p.op("dve", lambda e: e.tensor_copy(idb[:], idf[:]), [idf], [idb])
    GB = [p.sb("GB%d" % i, [128, D], F32) for i in range(4)]
    for i in range(4):
        p.dma(GB[i][:], lnp[i:i + 1, :].to_broadcast([128, D]), writes=[GB[i]])
    stg = [p.sb("stg%d" % i, [128, 1024], F32) for i in range(2)]
    woutb = p.sb("woutb", [128, 8, D], BF16)
    w2b = p.sb("w2b", [128, 32, D], BF16)
    wo_v = w_out.rearrange("(c p) n -> p c n", p=128)
    w2_v = w2.rearrange("(c p) n -> p c n", p=128)
    k = 0
    for c in range(8):
        st_ = stg[k % 2]
        p.dma(st_[:], wo_v[:, c, :], writes=[st_])
        p.op(("pool", "dve")[k % 2], lambda e: e.tensor_copy(woutb[:, c, :], st_[:]), [st_], [woutb], add=True)
        k += 1
    for c in range(32):
        st_ = stg[k % 2]
        p.dma(st_[:], w2_v[:, c, :], writes=[st_])
        p.op(("pool", "dve")[k % 2], lambda e: e.tensor_copy(w2b[:, c, :], st_[:]), [st_], [w2b], add=True)
        k += 1
    if glu:
        glub = p.sb("glub", [128, 4, 512], BF16)
        gbias = p.sb("gbias", [128, 4], F32)
        p.dma(gbias[:], glu_b[:, :], writes=[gbias])
        gw_v = glu_w.rearrange("(c p) n -> p c n", p=128)
        for c in range(4):
            st_ = stg[k % 2]
            p.dma(st_[:, 0:512], gw_v[:, c, :], writes=[st_])
            p.op(("pool", "dve")[k % 2], lambda e: e.tensor_copy(glub[:, c, :], st_[:, 0:512]), [st_], [glub], add=True)
            k += 1

    mixb = p.sb("mixb", [128, 8, 512], BF16)
    x1T = p.sb("x1T", [128, 8, 512], BF16)
    x1a = p.sb("x1a", [128, 4, D], F32)
    hT = p.sb("hT", [128, 32, 512], BF16)
    xs = [p.sb("xs%d" % i, [128, D], F32) for i in range(2)]
    vb0 = p.sb("vb0", [128, D], F32)
    vb = [vb0, vb0]
    x1b = p.sb("x1b", [128, D], BF16)
    stt = p.sb("stt", [128, 8], F32)
    w1s = [p.sb("w1s%d" % i, [128, 8, 128], F32) for i in range(2)]
    w1b = [p.sb("w1b%d" % i, [128, 8, 128], BF16) for i in range(2)]
    rl = [p.sb("rl%d" % i, [128, 512], F32) for i in range(2)]
    yps = [p.ps("yps%d" % i, [128, 512]) for i in range(4)]
    tps = p.ps("tps", [128, 8, 128], BF16)
    hps = [p.ps("hps%d" % i, [128, 512]) for i in range(2)]
    if glu:
        ygb = p.sb("ygb", [128, 4, 512], BF16)
    print("kc sbuf remaining", nc.sbuf_bytes_remaining)

    mix_v = mixT.rearrange("(c p) t -> p c t", p=128)
    w1_v = w1.rearrange("(c p) n -> p c n", p=128)
    if glu:
        y_v = yT.rearrange("(c p) t -> p c t", p=128)
    for g in range(ngroups):
        gs = slice(g * 512, (g + 1) * 512)
        nmix = 4 if glu else 8
        for c in range(nmix):
            st_ = stg[k % 2]
            p.dma(st_[:, 0:512], mix_v[:, c, gs], writes=[st_])
            p.op(("pool", "dve")[k % 2], lambda e: e.tensor_copy(mixb[:, c, :], st_[:, 0:512]), [st_], [mixb], add=(c > 0))
            k += 1
        if glu:
            for c in range(4):
                ys_, yg_ = rl[0], rl[1]
                p.dma(ys_[:], y_v[:, c, gs], writes=[ys_])
                p.op("act", lambda e: e.activation(yg_[:], ys_[:], AF.Square), [ys_], [yg_])
                p.op("dve", lambda e: e.tensor_scalar(yg_[:], yg_[:], 0.044715, 1.0, ALU.mult, ALU.add), [yg_], [yg_])
                p.op("pool", lambda e: e.tensor_mul(yg_[:], yg_[:], ys_[:]), [yg_, ys_], [yg_])
                p.op("act", lambda e: e.activation(yg_[:], yg_[:], AF.Sigmoid, scale=1.5957691216), [yg_], [yg_])
                p.op("dve", lambda e: e.tensor_mul(ygb[:, c, :], yg_[:], ys_[:]), [yg_, ys_], [ygb], add=(c > 0))
            for m in range(4):
                hp = hps[m % 2]
                for c in range(4):
                    p.op("pe", lambda e: e.matmul(hp[:], glub[:, c, m * 128:(m + 1) * 128], ygb[:, c, :],
                                                  start=(c == 0), stop=(c == 3)), [glub, ygb], [hp])
                sg = rl[m % 2]
                p.op("act", lambda e: e.activation(sg[:], hp[:], AF.Sigmoid, bias=gbias[:, m:m + 1]), [hp, gbias], [sg])
                p.op("dve", lambda e: e.tensor_mul(mixb[:, 4 + m, :], sg[:], ygb[:, m, :]), [sg, ygb], [mixb], add=True)
        for s in range(4):
            tsl = slice(s * 128, (s + 1) * 128)
            r0 = g * 512 + s * 128
            xt, v = xs[s % 2], vb[s % 2]
            p.dma(xt[:], x[r0:r0 + 128, :], writes=[xt])
            for nh in range(2):
                yp = yps[nh]
                for c in range(8):
                    p.op("pe", lambda e: e.matmul(yp[:], mixb[:, c, tsl], woutb[:, c, nh * 512:(nh + 1) * 512],
                                                  start=(c == 0), stop=(c == 7)), [mixb, woutb], [yp])
                p.op("dve", lambda e: e.scalar_tensor_tensor(v[:, nh * 512:(nh + 1) * 512], xt[:, nh * 512:(nh + 1) * 512],
                                                              ALPHA, yp[:], ALU.mult, ALU.add), [xt, yp], [v], add=(nh > 0))
            _layernorm_tile(p, v, x1b, stt, GB[0], GB[1], _View(x1a, x1a[:, s, :]))
            p.op("pool", lambda e: e.tensor_copy(x1b[:], x1a[:, s, :]), [x1a], [x1b])
            for c in range(8):
                p.op("pe", lambda e: e.transpose(tps[:, c, :], x1b[:, c * 128:(c + 1) * 128], idb[:]), [x1b, idb], [tps])
            p.op("act", lambda e: e.copy(x1T[:, :, tsl], tps[:]), [tps], [x1T], add=(s > 0))
        for j in range(32):
            ws_, wb_ = w1s[j % 2], w1b[j % 2]
            p.dma(ws_[:], w1_v[:, :, j * 128:(j + 1) * 128], writes=[ws_])
            p.op(("pool", "dve")[j % 2], lambda e: e.tensor_copy(wb_[:], ws_[:]), [ws_], [wb_])
            hp = hps[j % 2]
            for c in range(8):
                p.op("pe", lambda e: e.matmul(hp[:], wb_[:, c, :], x1T[:, c, :], start=(c == 0), stop=(c == 7)), [wb_, x1T], [hp])
            r_ = rl[j % 2]
            p.op("act", lambda e: e.activation(r_[:], hp[:], AF.Relu), [hp], [r_])
            p.op(("dve", "pool")[j % 2], lambda e: e.tensor_mul(hT[:, j, :], r_[:], r_[:]), [r_], [hT], add=(j > 0))
        for s in range(4):
            tsl = slice(s * 128, (s + 1) * 128)
            r0 = g * 512 + s * 128
            v = vb[s % 2]
            for nh in range(2):
                yp = yps[2 + nh]
                for j in range(32):
                    p.op("pe", lambda e: e.matmul(yp[:], hT[:, j, tsl], w2b[:, j, nh * 512:(nh + 1) * 512],
                                                  start=(j == 0), stop=(j == 31)), [hT, w2b], [yp])
                p.op("dve", lambda e: e.scalar_tensor_tensor(v[:, nh * 512:(nh + 1) * 512], x1a[:, s, nh * 512:(nh + 1) * 512],
                                                              ALPHA, yp[:], ALU.mult, ALU.add), [x1a, yp], [v], add=(nh > 0))
            ot = xs[s % 2]
            _layernorm_tile(p, v, x1b, stt, GB[2], GB[3], ot)
            p.dma(o_x[r0:r0 + 128, :], ot[:], reads=[ot])
    p.finish()
    print("kc nops", p.nops, "n_inst", p.n_inst)
    return nc


class _View:
    def __init__(self, parent, ap):
        self.parent = parent
        self.ap = ap

    def __getitem__(self, idx):
        return self.ap

    @property
    def w(self):
        return self.parent.w

    @w.setter
    def w(self, v):
        self.parent.w = v

    @property
    def wb(self):
        return self.parent.wb

    @wb.setter
    def wb(self, v):
        self.parent.wb = v

    @property
    def r(self):
        return self.parent.r

    @r.setter
    def r(self, v):
        self.parent.r = v

    @property
    def pr(self):
        return self.parent.pr

    @pr.setter
    def pr(self, v):
        self.parent.pr = v

    @property
    def psum(self):
        return self.parent.psum


def kc_inputs(x_tok, mixT, W, pre, glu_in=None):
    d = dict(x=np.ascontiguousarray(x_tok), mixT=np.ascontiguousarray(mixT), w_out=W[pre + "w_out"],
             lnp=np.stack([W[pre + "ln1_g"], W[pre + "ln1_b"], W[pre + "ln2_g"], W[pre + "ln2_b"]]).astype(np.float32),
             w1=W[pre + "mlp_w1"], w2=W[pre + "mlp_w2"], ident=np.eye(128, dtype=np.float32))
    if glu_in is not None:
        d["yT"] = np.ascontiguousarray(glu_in)
        d["glu_w"] = W["l1_glu_w"]
        d["glu_b"] = np.ascontiguousarray(W["l1_glu_b"].reshape(4, 128).T)
    return d


L1_IN = 2632


def build_ka1(ngroups=8):
    nc = _newnc()
    ntok = ngroups * 512
    xT = _din(nc, "xT", [D, ntok])
    w_in = _din(nc, "w_in", [D, L1_IN])
    o_dq = _dout(nc, "o_dq", [ntok, 512], BF16)
    o_dk = _dout(nc, "o_dk", [ntok, 512], BF16)
    o_dv = _dout(nc, "o_dv", [ntok, 512], BF16)
    o_qi = _dout(nc, "o_qi", [ntok, 512], BF16)
    o_ki = _dout(nc, "o_ki", [ntok, 64], BF16)
    o_sg = _dout(nc, "o_sg", [ntok, 8], F32)
    o_u = _dout(nc, "o_u", [ntok, 512], F32)
    p = Prog(nc)
    winb = p.sb("winb", [128, 8, L1_IN], BF16)
    wst = [p.sb("wst%d" % i, [128, L1_IN], F32) for i in range(2)]
    w_in_v = w_in.rearrange("(c p) n -> p c n", p=128)
    for c in range(8):
        st = wst[c % 2]
        p.dma(st[:], w_in_v[:, c, :], writes=[st])
        p.op(("pool", "dve")[c % 2], lambda e: e.tensor_copy(winb[:, c, :], st[:]), [st], [winb], add=True)
    xs = [p.sb("xs%d" % i, [128, 8, 512], F32) for i in range(2)]
    xb = [p.sb("xb%d" % i, [128, 8, 512], BF16) for i in range(2)]
    hps = [p.ps("hps%d" % i, [128, 512]) for i in range(6)]
    NB = 2
    T = {}
    for nm, w, dt in (("dq", 512, BF16), ("dk", 512, BF16), ("dv", 512, BF16), ("qi", 512, BF16),
                      ("ki", 64, BF16), ("sg", 8, F32), ("u", 512, F32)):
        T[nm] = [p.sb("%s_t%d" % (nm, i), [128, w], dt) for i in range(NB)]
    aw = p.sb("aw", [128, 8], F32)
    xT_v = xT.rearrange("(c p) t -> p c t", p=128)
    for g in range(ngroups):
        xsg, xbg = xs[g % 2], xb[g % 2]
        for c in range(8):
            p.dma(xsg[:, c, :], xT_v[:, c, g * 512:(g + 1) * 512], writes=[xsg], add=(c > 0))
        for c in range(8):
            p.op(("pool", "dve")[c % 2], lambda e: e.tensor_copy(xbg[:, c, :], xsg[:, c, :]), [xsg], [xbg], add=(c > 0))
        for s in range(4):
            sub = g * 4 + s
            b = sub % NB
            tsl = slice(s * 128, (s + 1) * 128)
            for bi in range(6):
                n0 = bi * 512
                n = min(512, L1_IN - n0)
                for c in range(8):
                    p.op("pe", lambda e: e.matmul(hps[bi][:, 0:n], xbg[:, c, tsl], winb[:, c, n0:n0 + n],
                                                  start=(c == 0), stop=(c == 7)), [xbg, winb], [hps[bi]])
            p.op("act", lambda e: e.activation(T["dq"][b][:], hps[0][:], AF.Copy, scale=0.125), [hps[0]], [T["dq"][b]])
            p.op("dve", lambda e: e.tensor_copy(T["dk"][b][:], hps[1][:]), [hps[1]], [T["dk"][b]])
            p.op("act", lambda e: e.copy(T["dv"][b][:], hps[2][:]), [hps[2]], [T["dv"][b]])
            p.op("act", lambda e: e.activation(aw[:], hps[4][:, 64:72], AF.Abs, scale=(8 ** -0.5) * 0.125), [hps[4]], [aw])
            p.op("act", lambda e: e.activation(T["sg"][b][:], hps[4][:, 64:72], AF.Sign), [hps[4]], [T["sg"][b]])
            p.op("act", lambda e: e.copy(T["ki"][b][:], hps[4][:, 0:64]), [hps[4]], [T["ki"][b]])
            p.op("act", lambda e: e.copy(T["u"][b][:, 0:440], hps[4][:, 72:512]), [hps[4]], [T["u"][b]])
            p.op("dve", lambda e: e.tensor_copy(T["u"][b][:, 440:512], hps[5][:, 0:72]), [hps[5]], [T["u"][b]], add=True)
            p.op("dve", lambda e: e.tensor_tensor(T["qi"][b][:].rearrange("p (h d) -> p h d", h=8),
                                                  hps[3][:].rearrange("p (h d) -> p h d", h=8),
                                                  aw[:].unsqueeze(2).to_broadcast([128, 8, 64]), ALU.mult),
                 [hps[3], aw], [T["qi"][b]])
            r0 = sub * 128
            for nm, o in (("dq", o_dq), ("dk", o_dk), ("dv", o_dv), ("qi", o_qi), ("ki", o_ki), ("sg", o_sg), ("u", o_u)):
                p.dma(o[r0:r0 + 128, :], T[nm][b][:], reads=[T[nm][b]])
    p.finish()
    print("ka1 nops", p.nops, "n_inst", p.n_inst)
    return nc


def build_gla(nchunks=128):
    nc = _newnc()
    S = nchunks * 64
    g2 = _din(nc, "g2", [S, 128])
    qT2 = _din(nc, "qT2", [64, 2, S], BF16)
    kT2 = _din(nc, "kT2", [64, 2, S], BF16)
    k2 = _din(nc, "k2", [S, 128], BF16)
    v2 = _din(nc, "v2", [S, 256], BF16)
    sr2 = _din(nc, "sr2", [S, 256])
    gain = _din(nc, "gain", [1, 256])
    tri = _din(nc, "tri", [64, 64])
    upp = _din(nc, "upp", [64, 64])
    o_gla = _dout(nc, "o_gla", [S, 256])
    p = Prog(nc)
    Tri = p.sb("Tri", [64, 64], F32)
    Upp = p.sb("Upp", [64, 64], F32)
    TriM = p.sb("TriM", [64, 2, 64], F32)
    gbc = p.sb("gbc", [64, 256], F32)
    p.dma(Tri[:], tri[:, :], writes=[Tri])
    p.dma(Upp[:], upp[:, :], writes=[Upp])
    p.dma(gbc[:], gain[0:1, :].to_broadcast([64, 256]), writes=[gbc])
    for h in range(2):
        p.op("dve", lambda e: e.tensor_copy(TriM[:, h, :], Tri[:]), [Tri], [TriM], add=(h > 0))
    St = p.sb("St", [64, 2, 128], F32)
    Sb = [p.sb("Sb%d" % i, [64, 2, 128], BF16) for i in range(2)]
    p.op("dve", lambda e: e.memset(St[:], 0.0), [], [St])
    p.op("dve", lambda e: e.memset(Sb[0][:], 0.0), [], [Sb[0]])
    NBLK = 2
    gB = [p.sb("gB%d" % i, [64, 8, 128], F32) for i in range(NBLK)]
    kB = [p.sb("kB%d" % i, [64, 8, 128], BF16) for i in range(NBLK)]
    vB = [p.sb("vB%d" % i, [64, 8, 256], BF16) for i in range(NBLK)]
    sB = [p.sb("sB%d" % i, [64, 8, 256], F32) for i in range(NBLK)]
    qTB = [p.sb("qTB%d" % i, [64, 2, 512], BF16) for i in range(NBLK)]
    kTB = [p.sb("kTB%d" % i, [64, 2, 512], BF16) for i in range(NBLK)]
    oB = [p.sb("oB%d" % i, [64, 8, 256], F32) for i in range(NBLK)]
    E1 = [p.sb("E1_%d" % i, [64, 2, 64], F32) for i in range(2)]
    E2 = [p.sb("E2_%d" % i, [64, 2, 64], F32) for i in range(2)]
    E3 = [p.sb("E3_%d" % i, [64, 128], F32) for i in range(2)]
    qd = [p.sb("qd%d" % i, [64, 2, 64], BF16) for i in range(2)]
    ki = [p.sb("ki%d" % i, [64, 2, 64], BF16) for i in range(2)]
    KE = [p.sb("KE%d" % i, [64, 128], BF16) for i in range(2)]
    attm = [p.sb("attm%d" % i, [64, 2, 64], BF16) for i in range(2)]
    GS = [p.sb("GS%d" % i, [64, 256], F32) for i in range(2)]
    junk = p.sb("junk", [64, 128], F32)
    ss = [p.sb("ss%d" % i, [64, 2], F32) for i in range(2)]
    cum_ps = p.ps("cum_ps", [128, 512])
    rc_ps = p.ps("rc_ps", [128, 512])
    att_ps = p.ps("att_ps", [128, 512])
    kv_ps = p.ps("kv_ps", [128, 512])
    o_ps = [p.ps("o_ps%d" % i, [128, 512]) for i in range(2)]
    g_v = g2.rearrange("(n t) c -> t n c", t=64)
    k_v = k2.rearrange("(n t) c -> t n c", t=64)
    v_v = v2.rearrange("(n t) c -> t n c", t=64)
    s_v = sr2.rearrange("(n t) c -> t n c", t=64)
    o_v = o_gla.rearrange("(n t) c -> t n c", t=64)
    nblk = nchunks // 8
    f2 = lambda ap: ap.rearrange("p h i -> p (h i)")
    for blk in range(nblk):
        bb = blk % NBLK
        ns = slice(blk * 8, (blk + 1) * 8)
        ts_ = slice(blk * 512, (blk + 1) * 512)
        p.dma(gB[bb][:], g_v[:, ns, :], writes=[gB[bb]])
        p.dma(kB[bb][:], k_v[:, ns, :], writes=[kB[bb]])
        p.dma(vB[bb][:], v_v[:, ns, :], writes=[vB[bb]])
        p.dma(sB[bb][:], s_v[:, ns, :], writes=[sB[bb]])
        p.dma(qTB[bb][:], qT2[:, :, ts_], writes=[qTB[bb]])
        p.dma(kTB[bb][:], kT2[:, :, ts_], writes=[kTB[bb]])
        for ci in range(8):
            n = blk * 8 + ci
            a = n % 2
            cs = slice(ci * 64, (ci + 1) * 64)
            for h in range(2):
                p.op("pe", lambda e: e.matmul(cum_ps[0:64, h * 64:(h + 1) * 64], gB[bb][:, ci, h * 64:(h + 1) * 64], Tri[:],
                                              start=True, stop=True), [gB[bb], Tri], [cum_ps])
            p.op("pe", lambda e: e.matmul(rc_ps[0:64, 0:128], Upp[:], gB[bb][:, ci, :], start=True, stop=True), [gB[bb], Upp], [rc_ps])
            p.op("act", lambda e: e.activation(f2(E1[a][:]), cum_ps[0:64, 0:128], AF.Exp), [cum_ps], [E1[a]])
            p.op("act", lambda e: e.activation(f2(E2[a][:]), cum_ps[0:64, 0:128], AF.Exp, scale=-1.0), [cum_ps], [E2[a]])
            p.op("act", lambda e: e.activation(E3[a][:], rc_ps[0:64, 0:128], AF.Exp), [rc_ps], [E3[a]])
            p.op("dve", lambda e: e.tensor_mul(qd[a][:], qTB[bb][:, :, cs], E1[a][:]), [qTB[bb], E1[a]], [qd[a]])
            p.op("pool", lambda e: e.tensor_mul(ki[a][:], kTB[bb][:, :, cs], E2[a][:]), [kTB[bb], E2[a]], [ki[a]])
            p.op("dve", lambda e: e.tensor_mul(KE[a][:], kB[bb][:, ci, :], E3[a][:]), [kB[bb], E3[a]], [KE[a]])
            for h in range(2):
                p.op("pe", lambda e: e.matmul(att_ps[0:64, h * 64:(h + 1) * 64], ki[a][:, h, :], qd[a][:, h, :], start=True, stop=True),
                     [ki[a], qd[a]], [att_ps])
            p.op("dve", lambda e: e.tensor_mul(f2(attm[a][:]), att_ps[0:64, 0:128], f2(TriM[:])), [att_ps, TriM], [attm[a]])
            for h in range(2):
                p.op("pe", lambda e: e.matmul(kv_ps[0:64, h * 128:(h + 1) * 128], KE[a][:, h * 64:(h + 1) * 64],
                                              vB[bb][:, ci, h * 128:(h + 1) * 128], start=True, stop=True), [KE[a], vB[bb]], [kv_ps])
            op_ = o_ps[a]
            for h in range(2):
                p.op("pe", lambda e: e.matmul(op_[0:64, h * 128:(h + 1) * 128], attm[a][:, h, :], vB[bb][:, ci, h * 128:(h + 1) * 128],
                                              start=True, stop=False), [attm[a], vB[bb]], [op_])
                p.op("pe", lambda e: e.matmul(op_[0:64, h * 128:(h + 1) * 128], qd[a][:, h, :], Sb[a][:, h, :],
                                              start=False, stop=True), [qd[a], Sb[a]], [op_])
            for h in range(2):
                p.op("dve", lambda e: e.scalar_tensor_tensor(St[:, h, :], St[:, h, :], E1[a][:, h, 63:64],
                                                              kv_ps[0:64, h * 128:(h + 1) * 128], ALU.mult, ALU.add),
                     [St, E1[a], kv_ps], [St])
            p.op("act", lambda e: e.copy(Sb[1 - a][:], St[:]), [St], [Sb[1 - a]])
            for h in range(2):
                p.op("act", lambda e: e.activation(junk[:], op_[0:64, h * 128:(h + 1) * 128], AF.Square, accum_out=ss[a][:, h:h + 1]),
                     [op_], [junk, ss[a]], add=(h > 0))
            p.op("dve", lambda e: e.tensor_scalar(ss[a][:], ss[a][:], 1.0 / 128, LN_EPS, ALU.mult, ALU.add), [ss[a]], [ss[a]])
            p.op("act", lambda e: e.sqrt(ss[a][:], ss[a][:]), [ss[a]], [ss[a]])
            p.op("dve", lambda e: e.reciprocal(ss[a][:], ss[a][:]), [ss[a]], [ss[a]])
            p.op("pool", lambda e: e.tensor_mul(GS[a][:], gbc[:], sB[bb][:, ci, :]), [gbc, sB[bb]], [GS[a]])
            for h in range(2):
                p.op("dve", lambda e: e.scalar_tensor_tensor(oB[bb][:, ci, h * 128:(h + 1) * 128], op_[0:64, h * 128:(h + 1) * 128],
                                                              ss[a][:, h:h + 1], GS[a][:, h * 128:(h + 1) * 128], ALU.mult, ALU.mult),
                     [op_, ss[a], GS[a]], [oB[bb]], add=True)
        p.dma(o_v[:, ns, :], oB[bb][:], reads=[oB[bb]])
    p.finish()
    print("gla nops", p.nops, "n_inst", p.n_inst)
    return nc


def gla_consts():
    t = np.arange(64)
    tri = (t[:, None] <= t[None, :]).astype(np.float32)
    upp = (t[:, None] > t[None, :]).astype(np.float32)
    return tri, upp


def qgroups(ng, j):
    sel = (0, 3) if j == 0 else (1, 2)
    return [g for g in range(ng) if (g % 4) in sel]


def build_mla(ng=16, nheads=8):
    nc = _newnc()
    S = ng * 512
    nlg = ng // 2
    nq = nlg * 512
    qT = _din(nc, "qT", [nheads, 96, nq], BF16)
    kT = _din(nc, "kT", [nheads, 96, S], BF16)
    vP = _din(nc, "vP", [nheads, 128, S // 128, 64], BF16)
    dmask = _din(nc, "dmask", [2, 8, 128, 512], BF16)
    o_T = _dout(nc, "o_T", [nheads, 64, nq])
    p = Prog(nc)
    masks = p.sb("masks", [128, 2, 8, 512], BF16)
    for a_ in range(2):
        p.dma(masks[:, a_, :, :], dmask[a_].rearrange("r p f -> p r f"), writes=[masks], add=(a_ > 0))
    Sel = p.sb("Sel", [65, 64], F32)
    R = p.sb("R", [65, 512], F32)
    p.op("dve", lambda e: e.memset(Sel[:], 0.0), [], [Sel])
    p.op("dve", lambda e: e.memset(Sel[64:65, :], 1.0), [], [Sel])
    p.op("dve", lambda e: e.memset(R[:], 0.0), [], [R])
    nkb_tot = S // 128
    kTs = [p.sb("kTs%d" % i, [96, S], BF16) for i in range(2)]
    qTs = [p.sb("qTs%d" % i, [96, nq], BF16) for i in range(2)]
    Va = [p.sb("Va%d" % i, [128, nkb_tot, 65], BF16) for i in range(2)]
    for i in range(2):
        p.op("pool", lambda e: e.memset(Va[i][:, :, 64:65], 1.0), [], [Va[i]])
    NST = 5
    st_ps = [p.ps("st_ps%d" % i, [128, 512]) for i in range(NST)]
    pt = [p.sb("pt%d" % i, [128, 512], BF16) for i in range(NST)]
    ptf = [p.sb("ptf%d" % i, [128, 512], BF16) for i in range(2)]
    ot_ps = [p.ps("ot_ps%d" % i, [128, 512]) for i in range(2)]
    bc_ps = p.ps("bc_ps", [128, 512])
    OTs = [p.sb("OTs%d" % i, [65, 512], F32) for i in range(2)]
    oo = [p.sb("oo%d" % i, [64, 512], F32) for i in range(2)]
    it = 0
    ep = 0
    for h in range(nheads):
        hb = h % 2
        p.dma(kTs[hb][:], kT[h, :, :], writes=[kTs[hb]])
        p.dma(qTs[hb][:], qT[h, :, :], writes=[qTs[hb]])
        p.dma(Va[hb][:, :, 0:64], vP[h, :, :, :], writes=[Va[hb]], add=True)
        for gi in range(nlg):
            nkb = 8 * gi + 8
            otp = ot_ps[ep % 2]
            for kb in range(nkb):
                sp_, pt_ = st_ps[it % NST], pt[it % NST]
                p.op("pe", lambda e: e.matmul(sp_[:], kTs[hb][:, kb * 128:(kb + 1) * 128], qTs[hb][:, gi * 512:(gi + 1) * 512],
                                              start=True, stop=True), [kTs[hb], qTs[hb]], [sp_])
                if kb < 8 * gi:
                    p.op("act", lambda e: e.activation(pt_[:], sp_[:], AF.Exp), [sp_], [pt_])
                else:
                    r = kb - 8 * gi
                    pf = ptf[it % 2]
                    p.op("act", lambda e: e.activation(pf[:], sp_[:], AF.Exp), [sp_], [pf])
                    p.op(("dve", "pool")[r % 2], lambda e: e.tensor_mul(pt_[:], pf[:], masks[:, gi % 2, r, :]), [pf, masks], [pt_])
                p.op("pe", lambda e: e.matmul(otp[0:65, :], Va[hb][:, kb, :], pt_[:], start=(kb == 0), stop=(kb == nkb - 1)),
                     [Va[hb], pt_], [otp])
                it += 1
            ots, o_ = OTs[ep % 2], oo[ep % 2]
            p.op("act", lambda e: e.copy(ots[:], otp[0:65, :]), [otp], [ots])
            p.op("dve", lambda e: e.reciprocal(R[64:65, :], ots[64:65, :]), [ots], [R])
            p.op("pe", lambda e: e.matmul(bc_ps[0:64, :], Sel[:], R[:], start=True, stop=True), [Sel, R], [bc_ps])
            p.op("dve", lambda e: e.tensor_mul(o_[:], ots[0:64, :], bc_ps[0:64, :]), [ots, bc_ps], [o_])
            p.dma(o_T[h, :, gi * 512:(gi + 1) * 512], o_[:], reads=[o_])
            ep += 1
    p.finish()
    print("mla nops", p.nops, "n_inst", p.n_inst)
    return nc


def diag_masks(j):
    pp = np.arange(128)[:, None]
    f = np.arange(512)[None, :]
    dg = np.stack([(128 * r + pp <= f) for r in range(4)]).astype(np.float32)
    low = np.concatenate([dg, np.zeros_like(dg)])
    high = np.concatenate([np.ones_like(dg), dg])
    m = np.stack([low, high]) if j == 0 else np.stack([high, low])
    return m.astype(NPBF)


def mla_inputs(qf, kf, vm, j, ng):
    S = ng * 512
    groups = qgroups(ng, j)
    qsel = np.concatenate([qf[g * 512:(g + 1) * 512] for g in groups], 0)
    return dict(qT=np.ascontiguousarray(qsel.transpose(1, 2, 0)), kT=np.ascontiguousarray(kf.transpose(1, 2, 0)),
                vP=np.ascontiguousarray(vm.reshape(S // 128, 128, 8, 64).transpose(2, 1, 0, 3)), dmask=diag_masks(j))


S5T = 512


def build_s5(nchunks=16):
    nc = _newnc()
    S = nchunks * S5T
    uT = _din(nc, "uT", [256, S])
    a_re = _din(nc, "a_re", [128, 8])
    a_im = _din(nc, "a_im", [128, 8])
    lstep = _din(nc, "lstep", [128, 8])
    b_re = _din(nc, "b_re", [8, 128, 16])
    b_im = _din(nc, "b_im", [8, 128, 16])
    c_re = _din(nc, "c_re", [8, 128, 16])
    c_im = _din(nc, "c_im", [8, 128, 16])
    dsk = _din(nc, "dsk", [128, 2])
    tvals = _din(nc, "tvals", [1, S5T])
    ident = _din(nc, "ident", [128, 128])
    o_y = _dout(nc, "o_y", [256, S])
    p = Prog(nc)
    T = S5T
    def sm(name, w=8):
        return p.sb(name, [128, w], F32)
    are, aim, lst, dt_, lre, th, thc = sm("are"), sm("aim"), sm("lst"), sm("dt_"), sm("lre"), sm("th"), sm("thc")
    p.dma(are[:], a_re[:, :], writes=[are])
    p.dma(aim[:], a_im[:, :], writes=[aim])
    p.dma(lst[:], lstep[:, :], writes=[lst])
    dk = sm("dk", 2)
    p.dma(dk[:], dsk[:, :], writes=[dk])
    idf = p.sb("idf", [128, 128], F32)
    idb = p.sb("idb", [128, 128], BF16)
    p.dma(idf[:], ident[:, :], writes=[idf])
    p.op("dve", lambda e: e.tensor_copy(idb[:], idf[:]), [idf], [idb])
    p.op("act", lambda e: e.activation(dt_[:], lst[:], AF.Exp), [lst], [dt_])
    p.op("dve", lambda e: e.tensor_scalar(lre[:], are[:], -1e-4, None, ALU.min), [are], [lre])
    rmag = sm("rmag")
    tmp = sm("tmp")
    p.op("dve", lambda e: e.tensor_mul(tmp[:], lre[:], dt_[:]), [lre, dt_], [tmp])
    p.op("act", lambda e: e.activation(rmag[:], tmp[:], AF.Exp), [tmp], [rmag])
    p.op("dve", lambda e: e.tensor_mul(th[:], aim[:], dt_[:]), [aim, dt_], [th])
    p.op("dve", lambda e: e.tensor_scalar(thc[:], th[:], 1.0 / (2 * np.pi), None, ALU.mult), [th], [thc])

    def sincos(src_cyc, w, nm):
        t2 = p.sb(nm + "t2", [128, 2, w], F32)
        ti = p.sb(nm + "ti", [128, 2, w], I32)
        tf = p.sb(nm + "tf", [128, 2, w], F32)
        sc = p.sb(nm + "sc", [128, 2, w], F32)
        p.op("dve", lambda e: e.tensor_copy(t2[:, 0, :], src_cyc[:]), [src_cyc], [t2])
        p.op("dve", lambda e: e.tensor_scalar(t2[:, 1, :], src_cyc[:], 0.25, None, ALU.add), [src_cyc], [t2], add=True)
        p.op("dve", lambda e: e.tensor_copy(ti[:], t2[:]), [t2], [ti])
        p.op("dve", lambda e: e.tensor_copy(tf[:], ti[:]), [ti], [tf])
        p.op("dve", lambda e: e.tensor_sub(t2[:], t2[:], tf[:]), [t2, tf], [t2])
        p.op("act", lambda e: e.activation(sc[:], t2[:], AF.Sin, scale=6.28318), [t2], [sc])
        return sc

    sc1 = sincos(thc, 8, "p1")
    abr, abi = sm("abr"), sm("abi")
    p.op("dve", lambda e: e.tensor_mul(abr[:], rmag[:], sc1[:, 1, :]), [rmag, sc1], [abr])
    p.op("dve", lambda e: e.tensor_mul(abi[:], rmag[:], sc1[:, 0, :]), [rmag, sc1], [abi])
    nr, den, t1, t2_, cre, cim, ncim = sm("nr"), sm("den"), sm("t1"), sm("t2_"), sm("cre"), sm("cim"), sm("ncim")
    p.op("dve", lambda e: e.tensor_scalar(nr[:], abr[:], -1.0, None, ALU.add), [abr], [nr])
    p.op("dve", lambda e: e.tensor_mul(den[:], lre[:], lre[:]), [lre], [den])
    p.op("dve", lambda e: e.tensor_mul(t1[:], aim[:], aim[:]), [aim], [t1])
    p.op("dve", lambda e: e.tensor_add(den[:], den[:], t1[:]), [den, t1], [den])
    p.op("dve", lambda e: e.reciprocal(den[:], den[:]), [den], [den])
    p.op("dve", lambda e: e.tensor_mul(t1[:], nr[:], lre[:]), [nr, lre], [t1])
    p.op("dve", lambda e: e.tensor_mul(t2_[:], abi[:], aim[:]), [abi, aim], [t2_])
    p.op("dve", lambda e: e.tensor_add(t1[:], t1[:], t2_[:]), [t1, t2_], [t1])
    p.op("dve", lambda e: e.tensor_mul(cre[:], t1[:], den[:]), [t1, den], [cre])
    p.op("dve", lambda e: e.tensor_mul(t1[:], abi[:], lre[:]), [abi, lre], [t1])
    p.op("dve", lambda e: e.tensor_mul(t2_[:], nr[:], aim[:]), [nr, aim], [t2_])
    p.op("dve", lambda e: e.tensor_sub(t1[:], t1[:], t2_[:]), [t1, t2_], [t1])
    p.op("dve", lambda e: e.tensor_mul(cim[:], t1[:], den[:]), [t1, den], [cim])
    p.op("dve", lambda e: e.tensor_scalar(ncim[:], cim[:], -1.0, None, ALU.mult), [cim], [ncim])

    BTr = p.sb("BTr", [128, 8, 128], BF16)
    BTi = p.sb("BTi", [128, 8, 128], BF16)
    CPr = p.sb("CPr", [128, 8, 128], BF16)
    CPi = p.sb("CPi", [128, 8, 128], BF16)
    BP = [p.sb("BP%d" % i, [128, 128], BF16) for i in range(2)]
    bre_t = p.sb("bre_t", [128, 8, 16], F32)
    bim_t = p.sb("bim_t", [128, 8, 16], F32)
    cre_t = p.sb("cre_t", [128, 8, 16], F32)
    cim_t = p.sb("cim_t", [128, 8, 16], F32)
    for src, dst in ((b_re, bre_t), (b_im, bim_t), (c_re, cre_t), (c_im, cim_t)):
        p.dma(dst[:], src.rearrange("i p c -> p i c"), writes=[dst])
    p.op("pool", lambda e: e.memset(CPr[:], 0.0), [], [CPr])
    p.op("pool", lambda e: e.memset(CPi[:], 0.0), [], [CPi])
    tb = p.sb("tb", [128, 16], F32)
    tr_ps = p.ps("tr_ps", [128, 8, 128], BF16)
    for i in range(8):
        c0 = (i % 4) * 32
        for (half, ps_) in ((0, slice(0, 64)), (1, slice(64, 128))):
            cs = slice(c0 + half * 16, c0 + half * 16 + 16)
            p.op("dve", lambda e: e.tensor_copy(CPr[ps_, i, cs], cre_t[ps_, i, :]), [cre_t], [CPr], add=True)
            p.op("dve", lambda e: e.tensor_scalar(CPi[ps_, i, cs], cim_t[ps_, i, :], -1.0, None, ALU.mult), [cim_t], [CPi], add=True)
        for which, BT in ((0, BTr), (1, BTi)):
            bp = BP[which]
            p.op("pool", lambda e: e.memset(bp[:], 0.0), [], [bp])
            if which == 0:
                p.op("dve", lambda e: e.tensor_scalar(tb[:], bre_t[:, i, :], cre[:, i:i + 1], None, ALU.mult), [bre_t, cre], [tb])
                src2, sc2 = bim_t, ncim
            else:
                p.op("dve", lambda e: e.tensor_scalar(tb[:], bim_t[:, i, :], cre[:, i:i + 1], None, ALU.mult), [bim_t, cre], [tb])
                src2, sc2 = bre_t, cim
            for (half, ps_) in ((0, slice(0, 64)), (1, slice(64, 128))):
                cs = slice(c0 + half * 16, c0 + half * 16 + 16)
                p.op("dve", lambda e: e.scalar_tensor_tensor(bp[ps_, cs], src2[ps_, i, :], sc2[ps_, i:i + 1], tb[ps_, :], ALU.mult, ALU.add),
                     [src2, sc2, tb], [bp], add=True)
            p.op("pe", lambda e: e.transpose(tr_ps[:, i, :], bp[:], idb[:]), [bp, idb], [tr_ps])
            p.op("act", lambda e: e.copy(BT[:, i, :], tr_ps[:, i, :]), [tr_ps], [BT], add=True)

    tv = p.sb("tv", [128, T], F32)
    p.dma(tv[:], tvals[0:1, :].to_broadcast([128, T]), writes=[tv])
    CS = p.sb("CS", [128, 8, 2, T], F32)
    Rt = p.sb("Rt", [128, 8, T], F32)
    ang = p.sb("ang", [128, T], F32)
    t2 = p.sb("rt2", [128, 2, T], F32)
    ti = p.sb("rti", [128, 2, T], I32)
    tf = p.sb("rtf", [128, 2, T], F32)
    for i in range(8):
        p.op("dve", lambda e: e.tensor_scalar(t2[:, 0, :], tv[:], thc[:, i:i + 1], None, ALU.mult), [tv, thc], [t2])
        p.op("dve", lambda e: e.tensor_scalar(t2[:, 1, :], t2[:, 0, :], 0.25, None, ALU.add), [t2], [t2])
        p.op("dve", lambda e: e.tensor_copy(ti[:], t2[:]), [t2], [ti])
        p.op("dve", lambda e: e.tensor_copy(tf[:], ti[:]), [ti], [tf])
        p.op("dve", lambda e: e.tensor_sub(t2[:], t2[:], tf[:]), [t2, tf], [t2])
        p.op("act", lambda e: e.activation(CS[:, i, :, :], t2[:], AF.Sin, scale=6.28318), [t2], [CS], add=True)
        p.op("pool", lambda e: e.memset(Rt[:, i, :], 1.0), [], [Rt], add=True)
        p.op("pool", lambda e: e.tensor_scalar(Rt[:, i, :], Rt[:, i, :], rmag[:, i:i + 1], None, ALU.mult), [Rt, rmag], [Rt], add=True)

    carry = p.sb("carry", [128, 8, 2], F32)
    p.op("dve", lambda e: e.memset(carry[:], 0.0), [], [carry])
    uf = [p.sb("uf%d" % i, [128, 2, T], F32) for i in range(2)]
    ub = [p.sb("ub%d" % i, [128, 2, T], BF16) for i in range(2)]
    NW = 2
    W = {nm: [p.sb("%s%d" % (nm, i), [128, T], F32) for i in range(NW)]
         for nm in ("br", "bi", "m1", "m2", "m3", "m4", "wr", "wi", "sr", "si")}
    Xr = [p.sb("Xr%d" % i, [128, T], BF16) for i in range(4)]
    Xi = [p.sb("Xi%d" % i, [128, T], BF16) for i in range(4)]
    cz = p.sb("cz", [128, 4], F32)
    yo = [p.sb("yo%d" % i, [128, T], F32) for i in range(2)]
    bu_ps = [p.ps("bu_ps%d" % i, [128, 512]) for i in range(4)]
    y_ps = [p.ps("y_ps%d" % i, [128, 512]) for i in range(2)]
    u_v = uT.rearrange("(c p) t -> p c t", p=128)
    o_v = o_y.rearrange("(c p) t -> p c t", p=128)
    unit = 0
    for ch in range(nchunks):
        tsl = slice(ch * T, (ch + 1) * T)
        ufc, ubc = uf[ch % 2], ub[ch % 2]
        p.dma(ufc[:], u_v[:, :, tsl], writes=[ufc])
        p.op("pool", lambda e: e.tensor_copy(ubc[:], ufc[:]), [ufc], [ubc])
        for ct in range(2):
            yp = y_ps[ct]
            for li in range(4):
                i = ct * 4 + li
                w = unit % NW
                unit += 1
                bpr, bpi = bu_ps[(2 * unit) % 4], bu_ps[(2 * unit + 1) % 4]
                p.op("pe", lambda e: e.matmul(bpr[:], BTr[:, i, :], ubc[:, ct, :], start=True, stop=True), [BTr, ubc], [bpr])
                p.op("pe", lambda e: e.matmul(bpi[:], BTi[:, i, :], ubc[:, ct, :], start=True, stop=True), [BTi, ubc], [bpi])
                br, bi_, m1, m2, m3, m4 = W["br"][w], W["bi"][w], W["m1"][w], W["m2"][w], W["m3"][w], W["m4"][w]
                wr, wi_, sr, si = W["wr"][w], W["wi"][w], W["sr"][w], W["si"][w]
                Sn, Cs = CS[:, i, 0, :], CS[:, i, 1, :]
                p.op("act", lambda e: e.copy(br[:], bpr[:]), [bpr], [br])
                p.op("act", lambda e: e.copy(bi_[:], bpi[:]), [bpi], [bi_])
                p.op("dve", lambda e: e.tensor_mul(m1[:], br[:], Cs), [br, CS], [m1])
                p.op("pool", lambda e: e.tensor_mul(m2[:], bi_[:], Sn), [bi_, CS], [m2])
                p.op("dve", lambda e: e.tensor_mul(m3[:], bi_[:], Cs), [bi_, CS], [m3])
                p.op("pool", lambda e: e.tensor_mul(m4[:], br[:], Sn), [br, CS], [m4])
                p.op("pool", lambda e: e.tensor_add(wr[:], m1[:], m2[:]), [m1, m2], [wr])
                p.op("pool", lambda e: e.tensor_sub(wi_[:], m3[:], m4[:]), [m3, m4], [wi_])
                p.op("dve", lambda e: e.tensor_tensor_scan(sr[:], Rt[:, i, :], wr[:], carry[:, i, 0:1], ALU.mult, ALU.add),
                     [Rt, wr, carry], [sr])
                p.op("dve", lambda e: e.tensor_tensor_scan(si[:], Rt[:, i, :], wi_[:], carry[:, i, 1:2], ALU.mult, ALU.add),
                     [Rt, wi_, carry], [si])
                p.op("dve", lambda e: e.tensor_mul(m1[:], sr[:], Cs), [sr, CS], [m1])
                p.op("pool", lambda e: e.tensor_mul(m2[:], si[:], Sn), [si, CS], [m2])
                p.op("dve", lambda e: e.tensor_mul(m3[:], sr[:], Sn), [sr, CS], [m3])
                p.op("pool", lambda e: e.tensor_mul(m4[:], si[:], Cs), [si, CS], [m4])
                xr, xi = Xr[li], Xi[li]
                p.op("dve", lambda e: e.tensor_sub(xr[:], m1[:], m2[:]), [m1, m2], [xr])
                p.op("pool", lambda e: e.tensor_add(xi[:], m3[:], m4[:]), [m3, m4], [xi])
                p.op("dve", lambda e: e.tensor_sub(carry[:, i, 0:1], m1[:, T - 1:T], m2[:, T - 1:T]), [m1, m2], [carry])
                p.op("dve", lambda e: e.tensor_add(carry[:, i, 1:2], m3[:, T - 1:T], m4[:, T - 1:T]), [m3, m4], [carry])
                p.op("pe", lambda e: e.matmul(yp[:], CPr[:, i, :], xr[:], start=(li == 0), stop=False), [CPr, xr], [yp])
                p.op("pe", lambda e: e.matmul(yp[:], CPi[:, i, :], xi[:], start=False, stop=(li == 3)), [CPi, xi], [yp])
            yo_ = yo[ct]
            p.op("dve", lambda e: e.scalar_tensor_tensor(yo_[:], ufc[:, ct, :], dk[:, ct:ct + 1], yp[:], ALU.mult, ALU.add),
                 [ufc, dk, yp], [yo_])
            p.dma(o_v[:, ct, tsl], yo_[:], reads=[yo_])
    p.finish()
    print("s5 nops", p.nops, "n_inst", p.n_inst, "sbuf left", nc.sbuf_bytes_remaining)
    return nc


def s5_inputs(uT, W, j):
    gs = slice(16 * j, 16 * j + 16)
    r8 = lambda a: np.ascontiguousarray(a.reshape(8, 128).T)
    return dict(uT=np.ascontiguousarray(uT), a_re=r8(W["l1_s5_a_re"][gs]), a_im=r8(W["l1_s5_a_im"][gs]),
                lstep=r8(np.repeat(W["l1_s5_log_step"][gs], 64)),
                b_re=np.ascontiguousarray(W["l1_s5_b_re"][gs].reshape(8, 128, 16)),
                b_im=np.ascontiguousarray(W["l1_s5_b_im"][gs].reshape(8, 128, 16)),
                c_re=np.ascontiguousarray(W["l1_s5_c_re"][gs].transpose(0, 2, 1).reshape(8, 128, 16)),
                c_im=np.ascontiguousarray(W["l1_s5_c_im"][gs].transpose(0, 2, 1).reshape(8, 128, 16)),
                dsk=np.ascontiguousarray(W["l1_s5_d"][256 * j:256 * j + 256].reshape(2, 128).T),
                tvals=np.arange(1, S5T + 1, dtype=np.float32)[None, :], ident=np.eye(128, dtype=np.float32))


DSA_K = 256
DSA_NIT = 22
BIG = 1.0e30


def build_dsa(ng=16, nit=DSA_NIT, nhg=4, do_p2=True):
    nc = _newnc()
    S = ng * 512
    nlg = ng // 2
    QBs = [(i, r) for i in range(nlg) for r in range(4)]
    nqb = len(QBs)
    nq = nqb * 128
    qiT = _din(nc, "qiT", [8, 64, nq], BF16)
    kiT = _din(nc, "kiT", [64, S], BF16)
    sgn = _din(nc, "sgn", [128, nqb, 8])
    qT = _din(nc, "qT", [8, 64, nq], BF16)
    kT = _din(nc, "kT", [8, 64, S], BF16)
    vP = _din(nc, "vP", [8, 128, S // 128, 64], BF16)
    ident = _din(nc, "ident", [128, 128], BF16)
    cbig = _din(nc, "cbig", [2, 4, 128, 1024])
    p2row = _din(nc, "p2row", [2, nit])
    o_dsaT = _dout(nc, "o_dsaT", [8, 64, nq])
    mscrT = nc.dram_tensor("mscrT", [nlg, S // 128, 128, 512], BF16).ap()
    p = Prog(nc)
    idb = p.sb("idb", [128, 128], BF16)
    CBs = [p.sb("CB%d" % i, [128, 1024], F32) for i in range(2)]
    P2 = p.sb("P2", [128, 2, nit], F32)
    p.dma(idb[:], ident[:, :], writes=[idb])

    for i in range(2):
        p.dma(P2[:, i, :], p2row[i:i + 1, :].to_broadcast([128, nit]), writes=[P2], add=(i > 0))
    kis = p.sb("kis", [64, S], BF16)
    p.dma(kis[:], kiT[:, :], writes=[kis])
    sg = p.sb("sg", [128, nqb, 8], F32)
    p.dma(sg[:], sgn[:, :, :], writes=[sg])
    Score = p.sb("Score", [128, S], F32)
    Mj = p.sb("Mj", [128, S], BF16)
    qis = [p.sb("qis%d" % i, [64, 8, 128], BF16) for i in range(2)]
    Rh = [p.sb("Rh%d" % i, [128, 512], BF16) for i in range(8)]
    Rl = [p.sb("Rl%d" % i, [128, 512], BF16) for i in range(8)]
    Dsg = [p.sb("Dsg%d" % i, [128, 8, 128], BF16) for i in range(2)]
    l_ps = [p.ps("l_ps%d" % i, [128, 512]) for i in range(2)]
    sc_ps = p.ps("sc_ps", [128, 512])
    st = p.sb("st", [128, 8], F32)
    Wt = p.sb("Wt", [128, 2, nit], F32)
    mregT = [Buf(None, "mregT%d" % i) for i in range(nlg)]
    tpm = p.ps("tpm", [128, 4, 128], BF16)
    MTs = [p.sb("MTs%d" % i, [128, 4, 128], BF16) for i in range(3)]
    tcount = 0
    qi_v = qiT.rearrange("h d q -> d h q")
    for lb, (gi, rr) in enumerate(QBs):
        L = (2 * gi + 1) * 512 + (rr + 1) * 128
        nch = 2 * gi + 2
        qs = qis[lb % 2]
        dsg = Dsg[lb % 2]
        p.dma(qs[:], qi_v[:, :, lb * 128:(lb + 1) * 128], writes=[qs])
        for h in range(8):
            p.op(("dve", "pool")[h % 2], lambda e: e.tensor_scalar(dsg[:, h, :], idb[:], sg[:, lb, h:h + 1], None, ALU.mult),
                 [idb, sg], [dsg], add=(h > 0))
        for c in range(nch):
            w = 512 if c < nch - 1 else (rr + 1) * 128
            ks = slice(c * 512, c * 512 + w)
            for h in range(8):
                lp = l_ps[h % 2]
                p.op("pe", lambda e: e.matmul(lp[:, 0:w], qs[:, h, :], kis[:, ks], start=True, stop=True), [qs, kis], [lp])
                p.op("act", lambda e: e.activation(Rh[h][:, 0:w], lp[:, 0:w], AF.Relu), [lp], [Rh[h]])
                p.op("dve", lambda e: e.scalar_tensor_tensor(Rl[h][:, 0:w], lp[:, 0:w], 0.0, Rh[h][:, 0:w], ALU.max, ALU.subtract),
                     [lp, Rh[h]], [Rl[h]])
            for h in range(8):
                p.op("pe", lambda e: e.matmul(sc_ps[:, 0:w], dsg[:, h, :], Rh[h][:, 0:w], start=(h == 0), stop=False),
                     [dsg, Rh[h]], [sc_ps])
                p.op("pe", lambda e: e.matmul(sc_ps[:, 0:w], dsg[:, h, :], Rl[h][:, 0:w], start=False, stop=(h == 7)),
                     [dsg, Rl[h]], [sc_ps])
            p.op("act", lambda e: e.copy(Score[:, ks], sc_ps[:, 0:w]), [sc_ps], [Score], add=(c > 0))
        p.op("dve", lambda e: e.tensor_reduce(st[:, 0:1], Score[:, 0:L], AX.X, ALU.max), [Score], [st])
        p.op("dve", lambda e: e.tensor_reduce(st[:, 1:2], Score[:, 0:L], AX.X, ALU.min), [Score], [st])
        t0_ = 2 * gi * 512
        CB = CBs[lb % 2]
        p.dma(CB[:], cbig[gi % 2, rr, :, :], writes=[CB])
        p.op("dve", lambda e: e.tensor_tensor(Score[:, t0_:L], Score[:, t0_:L], CB[:, 0:L - t0_], ALU.min), [Score, CB], [Score])
        p.op("dve", lambda e: e.tensor_sub(st[:, 4:5], st[:, 0:1], st[:, 1:2]), [st], [st])
        p.op("dve", lambda e: e.tensor_scalar(st[:, 4:5], st[:, 4:5], 2.0, 0.5, ALU.add, ALU.mult), [st], [st])
        p.op("dve", lambda e: e.tensor_scalar(Wt[:, 0, :], P2[:, 0, :], st[:, 4:5], None, ALU.mult), [P2, st], [Wt])
        p.op("dve", lambda e: e.tensor_scalar(Wt[:, 1, :], P2[:, 1, :], st[:, 4:5], None, ALU.mult), [P2, st], [Wt])
        p.op("dve", lambda e: e.scalar_tensor_tensor(st[:, 2:3], st[:, 1:2], -1.0, st[:, 4:5], ALU.add, ALU.add), [st], [st])
        for k in range(nit):
            p.op("dve", lambda e: e.tensor_scalar(Mj[:, 0:L], Score[:, 0:L], st[:, 2:3], None, ALU.is_ge, ALU.add,
                                                   accum_out=st[:, 3:4]), [Score, st], [Mj, st])
            p.op("dve", lambda e: e.tensor_scalar(st[:, 4:5], st[:, 3:4], DSA_K - 0.5, Wt[:, 0, k:k + 1], ALU.is_ge, ALU.mult),
                 [st, Wt], [st])
            p.op("dve", lambda e: e.scalar_tensor_tensor(st[:, 2:3], st[:, 4:5], Wt[:, 1, k:k + 1], st[:, 2:3], ALU.subtract, ALU.add),
                 [st, Wt], [st])
        p.op("dve", lambda e: e.tensor_scalar(Mj[:, 0:L], Score[:, 0:L], st[:, 2:3], None, ALU.is_ge), [Score, st], [Mj])
        Lp = (8 * gi + 8) * 128
        if L < Lp:
            p.op("dve", lambda e: e.memset(Mj[:, L:Lp], 0.0), [], [Mj], add=True)
        mT_v = mscrT[gi].rearrange("k p q -> p k q")
        for kb4 in range((8 * gi + 8) // 4):
            for kk in range(4):
                c0 = (kb4 * 4 + kk) * 128
                p.op("pe", lambda e: e.transpose(tpm[:, kk, :], Mj[:, c0:c0 + 128], idb[:]), [Mj, idb], [tpm])
            mts = MTs[tcount % 3]
            if tcount % 2 == 0:
                p.op("act", lambda e: e.copy(mts[:], tpm[:]), [tpm], [mts])
            else:
                p.op("dve", lambda e: e.tensor_copy(mts[:], tpm[:]), [tpm], [mts])
            tcount += 1
            p.dma(mT_v[:, kb4 * 4:(kb4 + 1) * 4, rr * 128:(rr + 1) * 128], mts[:], reads=[mts], writes=[mregT[gi]], add=True)
    if not do_p2:
        p.finish()
        return nc
    Sel = p.sb("Sel", [65, 64], F32)
    R = p.sb("R", [65, 512], F32)
    p.op("dve", lambda e: e.memset(Sel[:], 0.0), [], [Sel])
    p.op("dve", lambda e: e.memset(Sel[64:65, :], 1.0), [], [Sel])
    p.op("dve", lambda e: e.memset(R[:], 0.0), [], [R])
    kTs = p.sb("kTs", [64, 2, S], BF16)
    qTs = p.sb("qTs", [64, 2, nq], BF16)
    Va = p.sb("Va", [128, 2, S // 128, 65], BF16)
    p.op("pool", lambda e: e.memset(Va[:, :, :, 64:65], 1.0), [], [Va])
    NP = 4
    Pe = [p.sb("Pe%d" % i, [128, 512], BF16) for i in range(NP)]
    Pm = [p.sb("Pm%d" % i, [128, 512], BF16) for i in range(NP)]
    MTt = [p.sb("MTt%d" % i, [128, 4, 512], BF16) for i in range(3)]
    OTs = [p.sb("OTs%d" % i, [65, 512], F32) for i in range(2)]
    oo = [p.sb("oo%d" % i, [64, 512], F32) for i in range(2)]
    st_ps = [l_ps[0], l_ps[1], sc_ps]
    ot_ps = [p.ps("ot_ps%d" % i, [128, 512]) for i in range(3)]
    bc_ps = p.ps("bc_ps", [128, 512])
    kT_v = kT.rearrange("h d s -> d h s")
    qT_v = qT.rearrange("h d q -> d h q")
    it = 0
    ep = 0
    mcount = 0
    obase = 0
    for hp in range(4):
        p.dma(kTs[:], kT_v[:, 2 * hp:2 * hp + 2, :], writes=[kTs])
        p.dma(qTs[:], qT_v[:, 2 * hp:2 * hp + 2, :], writes=[qTs])
        for hh in range(2):
            p.dma(Va[:, hh, :, 0:64], vP[2 * hp + hh, :, :, :], writes=[Va], add=True)
        for gi in range(nlg):
            nkb = 8 * gi + 8
            mT_v = mscrT[gi].rearrange("k p q -> p k q")
            otp = [ot_ps[(obase + hh) % 3] for hh in range(2)]
            obase += 2
            for kb in range(nkb):
                if kb % 4 == 0:
                    mt = MTt[mcount % 3]
                    mcount += 1
                    p.dma(mt[:], mT_v[:, kb:kb + 4, :], reads=[mregT[gi]], writes=[mt])
                for hh in range(2):
                    sp_ = st_ps[it % 3]
                    pe_, pm_ = Pe[it % NP], Pm[it % NP]
                    p.op("pe", lambda e: e.matmul(sp_[:], kTs[:, hh, kb * 128:(kb + 1) * 128], qTs[:, hh, gi * 512:(gi + 1) * 512],
                                                  start=True, stop=True), [kTs, qTs], [sp_])
                    p.op("act", lambda e: e.activation(pe_[:], sp_[:], AF.Exp), [sp_], [pe_])
                    p.op("dve", lambda e: e.tensor_mul(pm_[:], pe_[:], mt[:, kb % 4, :]), [pe_, mt], [pm_])
                    p.op("pe", lambda e: e.matmul(otp[hh][0:65, :], Va[:, hh, kb, :], pm_[:], start=(kb == 0), stop=(kb == nkb - 1)),
                         [Va, pm_], [otp[hh]])
                    it += 1
            for hh in range(2):
                h = 2 * hp + hh
                ots, o_ = OTs[ep % 2], oo[ep % 2]
                p.op("act", lambda e: e.copy(ots[:], otp[hh][0:65, :]), [otp[hh]], [ots])
                p.op("dve", lambda e: e.reciprocal(R[64:65, :], ots[64:65, :]), [ots], [R])
                p.op("pe", lambda e: e.matmul(bc_ps[0:64, :], Sel[:], R[:], start=True, stop=True), [Sel, R], [bc_ps])
                p.op("dve", lambda e: e.tensor_mul(o_[:], ots[0:64, :], bc_ps[0:64, :]), [ots, bc_ps], [o_])
                p.dma(o_dsaT[h, :, gi * 512:(gi + 1) * 512], o_[:], reads=[o_])
                ep += 1
    p.finish()
    print("dsa nops", p.nops, "n_inst", p.n_inst, "sbuf left", nc.sbuf_bytes_remaining)
    return nc


def dsa_consts(nit, j):
    q = np.arange(128)[:, None]
    k = np.arange(128)[None, :]
    dg = np.where(k <= q, BIG, -BIG).astype(np.float32)
    pos = np.full((128, 128), BIG, np.float32)
    neg = -pos
    cb = np.zeros((2, 4, 128, 1024), np.float32)
    for r in range(4):
        low = [pos] * r + [dg] + [neg] * (3 - r) + [neg] * 4
        high = [pos] * 4 + [pos] * r + [dg] + [neg] * (3 - r)
        lo_, hi_ = np.concatenate(low, 1), np.concatenate(high, 1)
        cb[0, r], cb[1, r] = (lo_, hi_) if j == 0 else (hi_, lo_)
    wk = 0.5 ** np.arange(nit)
    bk = wk * 0.5
    bk[-1] = wk[-1]
    return cb, np.stack([wk, bk]).astype(np.float32)


def dsa_inputs(dq, dk, dv, qi, ki, sg, j, ng, nit=DSA_NIT):
    S = ng * 512
    groups = qgroups(ng, j)
    sel = np.concatenate([np.arange(g * 512, (g + 1) * 512) for g in groups])
    nqb = len(sel) // 128
    cb, p2 = dsa_consts(nit, j)
    T8 = lambda a: np.ascontiguousarray(a.reshape(a.shape[0], 8, 64).transpose(1, 2, 0))
    return dict(qiT=T8(qi[sel]), kiT=np.ascontiguousarray(ki.T), sgn=np.ascontiguousarray(sg[sel].reshape(nqb, 128, 8).transpose(1, 0, 2)),
                qT=T8(dq[sel]), kT=T8(dk), vP=np.ascontiguousarray(dv.reshape(S // 128, 128, 8, 64).transpose(2, 1, 0, 3)),
                ident=np.eye(128, dtype=np.float32).astype(NPBF), cbig=cb, p2row=p2)


def _cat_batch(res, key, b):
    return np.concatenate([np.asarray(res[2 * b][key]), np.asarray(res[2 * b + 1][key])], 0)


def _tokens_of(j, ng=16):
    return np.concatenate([np.arange(g * 512, (g + 1) * 512) for g in qgroups(ng, j)])


def kernel(**inputs):
    W = {k: np.asarray(v) for k, v in inputs.items()}
    x = W["x"].astype(np.float32)
    positions = W["positions"]
    B = 4
    A0 = run_ka0(x, positions, W)
    tri, upp = gla_consts()
    a0 = {k: [_cat_batch(A0, "o_" + k, b) for b in range(B)] for k in ("gq", "gk", "gv", "ga", "sr", "qf", "kf", "vm")}
    del A0
    in_maps = []
    for c in range(NCORES):
        b, j = c // 2, c % 2
        hs = slice(128 * j, 128 * j + 128)
        vs = slice(256 * j, 256 * j + 256)
        in_maps.append(dict(
            g2=np.ascontiguousarray(a0["ga"][b][:, hs]),
            qT2=np.ascontiguousarray(a0["gq"][b][:, hs].reshape(SEQ, 2, 64).transpose(2, 1, 0)),
            kT2=np.ascontiguousarray(a0["gk"][b][:, hs].reshape(SEQ, 2, 64).transpose(2, 1, 0)),
            k2=np.ascontiguousarray(a0["gk"][b][:, hs]), v2=np.ascontiguousarray(a0["gv"][b][:, vs]),
            sr2=np.ascontiguousarray(a0["sr"][b][:, vs]), gain=np.ascontiguousarray(W["l0_gla_norm"][None, vs]),
            tri=tri, upp=upp))
    G = _run(build_gla(128), in_maps)
    o_gla = [np.concatenate([np.asarray(G[2 * b]["o_gla"]), np.asarray(G[2 * b + 1]["o_gla"])], 1) for b in range(B)]
    del G
    in_maps = []
    for c in range(NCORES):
        b, j = c // 2, c % 2
        in_maps.append(mla_inputs(a0["qf"][b].reshape(SEQ, 8, 96), a0["kf"][b].reshape(SEQ, 8, 96),
                                  a0["vm"][b].reshape(SEQ, 8, 64), j, 16))
    M = _run(build_mla(16), in_maps)
    o_mlaT = [np.zeros((512, SEQ), np.float32) for _ in range(B)]
    for c in range(NCORES):
        b, j = c // 2, c % 2
        o_mlaT[b][:, _tokens_of(j)] = np.asarray(M[c]["o_T"]).reshape(512, TOK)
    del M, a0
    xf = x.reshape(B * SEQ, D)
    in_maps = []
    for c in range(NCORES):
        b, hf = c // 2, c % 2
        ts = slice(hf * TOK, (hf + 1) * TOK)
        mixT = np.concatenate([o_gla[b][ts].T, o_mlaT[b][:, ts]], 0)
        in_maps.append(kc_inputs(xf[c * TOK:(c + 1) * TOK], mixT, W, "l0_"))
    C0 = _run(build_kc(False, 8), in_maps)
    x2 = [np.asarray(C0[c]["o_x"]) for c in range(NCORES)]
    del C0, o_gla, o_mlaT
    in_maps = [dict(xT=np.ascontiguousarray(x2[c].T), w_in=W["l1_w_in"]) for c in range(NCORES)]
    A1 = _run(build_ka1(8), in_maps)
    a1 = {k: [_cat_batch(A1, "o_" + k, b) for b in range(B)] for k in ("dq", "dk", "dv", "qi", "ki", "sg", "u")}
    del A1
    in_maps = []
    for c in range(NCORES):
        b, j = c // 2, c % 2
        in_maps.append(dsa_inputs(a1["dq"][b], a1["dk"][b], a1["dv"][b], a1["qi"][b], a1["ki"][b], a1["sg"][b], j, 16))
    Dr = _run(build_dsa(16), in_maps)
    o_dsaT = [np.zeros((512, SEQ), np.float32) for _ in range(B)]
    for c in range(NCORES):
        b, j = c // 2, c % 2
        o_dsaT[b][:, _tokens_of(j)] = np.asarray(Dr[c]["o_dsaT"]).reshape(512, TOK)
    del Dr
    in_maps = []
    for c in range(NCORES):
        b, j = c // 2, c % 2
        in_maps.append(s5_inputs(a1["u"][b][:, 256 * j:256 * j + 256].T, W, j))
    Sr = _run(build_s5(16), in_maps)
    yT = [np.concatenate([np.asarray(Sr[2 * b]["o_y"]), np.asarray(Sr[2 * b + 1]["o_y"])], 0) for b in range(B)]
    del Sr, a1
    in_maps = []
    for c in range(NCORES):
        b, hf = c // 2, c % 2
        ts = slice(hf * TOK, (hf + 1) * TOK)
        in_maps.append(kc_inputs(x2[c], o_dsaT[b][:, ts], W, "l1_", glu_in=yT[b][:, ts]))
    C1 = _run(build_kc(True, 8), in_maps)
    out = np.concatenate([np.asarray(C1[c]["o_x"]) for c in range(NCORES)], 0)
    return out.reshape(B, SEQ, D).astype(np.float32)
```
